# Optimizing a Trainium2 kernel written in Bass

```python
import math
import jax, jax.numpy as jnp
from jax import lax
import numpy as np

D_MODEL = 2048
BATCH = 4
SEQ = 8192
DEPTH = 4

GRID_W = 64
CTX_LEN = 256
EPS = 1e-6
ROPE_THETA = 10000.0
Q_BLOCK = 128

SSM_HEADS = 32
SSM_HEAD_DIM = 64
SSM_INNER = SSM_HEADS * SSM_HEAD_DIM
SSM_GROUPS = 4
SSM_STATE = 128
SSM_CONV = 5
SSM_CHUNK = 128
SSM_CONV_DIM = SSM_INNER + 2 * SSM_GROUPS * SSM_STATE

GQA_HEADS = 8
GQA_KV_HEADS = 4
GQA_HEAD_DIM = 128
GQA_WIDTH = GQA_HEADS * GQA_HEAD_DIM
GQA_KV_WIDTH = GQA_KV_HEADS * GQA_HEAD_DIM

NA_HEADS = 8
NA_HEAD_DIM = 128
NA_WIDTH = NA_HEADS * NA_HEAD_DIM
NA_WIN_H = 8
NA_WIN_W = 16

MLA_HEADS = 8
MLA_Q_LORA = 768
MLA_KV_LORA = 512
MLA_NOPE = 128
MLA_ROPE = 64
MLA_V = 128
MLA_WIDTH = MLA_HEADS * MLA_V

N_BRANCH = 4

IN_SIZES = (SSM_INNER, SSM_CONV_DIM, 2 * SSM_HEADS,
            GQA_WIDTH, GQA_KV_WIDTH, GQA_KV_WIDTH, GQA_WIDTH,
            NA_WIDTH, NA_WIDTH, NA_WIDTH, NA_WIDTH,
            MLA_Q_LORA, MLA_KV_LORA, MLA_ROPE, MLA_WIDTH,
            N_BRANCH * D_MODEL)
IN_WIDTH = sum(IN_SIZES)

kernel_name = 'hybrid_ssd_gqa_natten_mla_dit'


def rmsnorm(x, w):
    xf = x.astype(jnp.float32)
    y = xf * lax.rsqrt(jnp.mean(xf * xf, axis=-1, keepdims=True) + EPS)
    return (y * w.astype(jnp.float32)).astype(x.dtype)


def heads(t, n):
    return t.reshape(t.shape[:-1] + (n, t.shape[-1] // n))


def group_q(q, g):
    return q.reshape(q.shape[:2] + (g, q.shape[2] // g, q.shape[3]))


def split_cols(p):
    return jnp.split(p, np.cumsum(IN_SIZES)[:-1].tolist(), axis=-1)


def rope_tables(n_tok, dim):
    t = jnp.arange(n_tok, dtype=jnp.int32)
    row = (t // GRID_W).astype(jnp.float32)
    col = (t % GRID_W).astype(jnp.float32)
    quarter = dim // 4
    freqs = ROPE_THETA ** (-jnp.arange(quarter, dtype=jnp.float32) / quarter)
    ang = jnp.concatenate([row[:, None] * freqs, col[:, None] * freqs], axis=-1)
    return jnp.cos(ang), jnp.sin(ang)


def apply_rope(x, cos, sin):
    xp = x.reshape(x.shape[:-1] + (x.shape[-1] // 2, 2))
    x0, x1 = xp[..., 0], xp[..., 1]
    c = cos[:, None, :].astype(x.dtype)
    s = sin[:, None, :].astype(x.dtype)
    return jnp.stack([x0 * c - x1 * s, x0 * s + x1 * c], axis=-1).reshape(x.shape)


def dwconv_silu(u, w, b):
    y = lax.conv_general_dilated(u, w[:, None, :].astype(u.dtype), window_strides=(1,),
                                 padding=[(SSM_CONV // 2, SSM_CONV // 2)],
                                 dimension_numbers=('NWC', 'WIO', 'NWC'),
                                 feature_group_count=u.shape[-1])
    return jax.nn.silu(y + b.astype(u.dtype))


def ssd_scan(xh, dt, a, bm, cm, s0):
    f32 = jnp.float32
    b_, n_tok, n_h, p_dim = xh.shape
    g_n = bm.shape[2]
    r_n = n_h // g_n
    s_dim = bm.shape[-1]
    nc, cl = n_tok // SSM_CHUNK, SSM_CHUNK
    dtf = dt.astype(f32)
    xs = (xh.astype(f32) * dtf[..., None]).reshape(b_, nc, cl, g_n, r_n, p_dim)
    da = (dtf * a.astype(f32)).reshape(b_, nc, cl, g_n, r_n)
    bc = bm.astype(f32).reshape(b_, nc, cl, g_n, s_dim)
    cc = cm.astype(f32).reshape(b_, nc, cl, g_n, s_dim)
    a_cs = jnp.cumsum(da, axis=2)
    lower = jnp.tril(jnp.ones((cl, cl), dtype=bool))[None, None, :, :, None, None]
    seg = a_cs[:, :, :, None] - a_cs[:, :, None, :]
    decay = jnp.exp(jnp.where(lower, seg, -jnp.inf))
    cb = jnp.einsum('bclgn,bcsgn->bclsg', cc, bc)
    y_diag = jnp.einsum('bclsgr,bcsgrp->bclgrp', cb[..., None] * decay, xs)
    to_end = jnp.exp(a_cs[:, :, -1:] - a_cs)
    chunk_states = jnp.einsum('bclgn,bclgrp->bcgrpn', bc, xs * to_end[..., None])
    chunk_decay = jnp.exp(a_cs[:, :, -1])

    def step(state, inp):
        st, dec = inp
        return dec[..., None, None] * state + st, state

    final, s_in = lax.scan(step, s0.astype(f32),
                           (jnp.moveaxis(chunk_states, 1, 0), jnp.moveaxis(chunk_decay, 1, 0)))
    s_in = jnp.moveaxis(s_in, 0, 1)
    y_off = jnp.einsum('bclgn,bcgrpn->bclgrp', cc, s_in) * jnp.exp(a_cs)[..., None]
    return (y_diag + y_off).reshape(b_, n_tok, n_h, p_dim), final


def gated_norm(y, z, w):
    g = y.reshape(z.shape).astype(jnp.float32) * jax.nn.silu(z.astype(jnp.float32))
    gg = g.reshape(z.shape[:-1] + (SSM_GROUPS, -1))
    gg = gg * lax.rsqrt(jnp.mean(gg * gg, axis=-1, keepdims=True) + EPS)
    return (gg.reshape(z.shape) * w.astype(jnp.float32)).astype(z.dtype)


def ssm_branch(z, xbc, dtr, z_c, xbc_c, dtr_c, conv_w, conv_b, a_log, dt_bias, d_skip,
               norm_w, need_ctx):
    def prep(u, dt_raw):
        u = dwconv_silu(u, conv_w, conv_b)
        xs, bm, cm = jnp.split(u, [SSM_INNER, SSM_INNER + SSM_GROUPS * SSM_STATE], axis=-1)
        dt = jax.nn.softplus(dt_raw.astype(jnp.float32).reshape(dt_raw.shape[:-1] + (2, SSM_HEADS))
                             + dt_bias.astype(jnp.float32))
        return heads(xs, SSM_HEADS), heads(bm, SSM_GROUPS), heads(cm, SSM_GROUPS), dt

    xl, bl, cl_, dtl = prep(xbc, dtr)
    xc, bc, cc, dtc = prep(xbc_c, dtr_c)
    a = -jnp.exp(a_log.astype(jnp.float32))
    s0 = jnp.zeros((xl.shape[0], SSM_GROUPS, SSM_HEADS // SSM_GROUPS, SSM_HEAD_DIM, SSM_STATE),
                   jnp.float32)
    flip = lambda t: jnp.flip(t, axis=1)
    yc_f, sc_f = ssd_scan(xc, dtc[:, :, 0], a[0], bc, cc, s0)
    yl_f, _ = ssd_scan(xl, dtl[:, :, 0], a[0], bl, cl_, sc_f)
    yc_b, sc_b = ssd_scan(flip(xc), flip(dtc[:, :, 1]), a[1], flip(bc), flip(cc), s0)
    yl_b, _ = ssd_scan(flip(xl), flip(dtl[:, :, 1]), a[1], flip(bl), flip(cl_), sc_b)
    skip = d_skip.astype(jnp.float32)[:, None]
    out = gated_norm(yl_f + flip(yl_b) + skip * xl.astype(jnp.float32), z, norm_w)
    out_c = None
    if need_ctx:
        out_c = gated_norm(yc_f + flip(yc_b) + skip * xc.astype(jnp.float32), z_c, norm_w)
    return out, out_c


def attend_latent(q, k, v, k_ctx, v_ctx, scale):
    b_, n_tok, g_n, r_n, dq = q.shape
    nb = n_tok // Q_BLOCK
    n_ctx = k_ctx.shape[1]
    qb = jnp.moveaxis(q.reshape(b_, nb, Q_BLOCK, g_n, r_n, dq), 1, 0)

    def block(qi):
        s = jnp.concatenate([jnp.einsum('bqgrd,bkgd->bgrqk', qi, k_ctx),
                             jnp.einsum('bqgrd,bkgd->bgrqk', qi, k)], axis=-1)
        p = jax.nn.softmax(s.astype(jnp.float32) * scale, axis=-1).astype(v.dtype)
        return (jnp.einsum('bgrqk,bkgd->bqgrd', p[..., :n_ctx], v_ctx)
                + jnp.einsum('bgrqk,bkgd->bqgrd', p[..., n_ctx:], v))

    o = lax.map(block, qb)
    return jnp.moveaxis(o, 0, 1).reshape(b_, n_tok, g_n * r_n * v.shape[-1])


def attend_context(q, k, v, scale):
    s = jnp.einsum('bqgrd,bkgd->bgrqk', q, k).astype(jnp.float32) * scale
    p = jax.nn.softmax(s, axis=-1).astype(v.dtype)
    o = jnp.einsum('bgrqk,bkgd->bqgrd', p, v)
    return o.reshape(o.shape[:2] + (-1,))


def na_latent(q, k, v, k_ctx, v_ctx, rpb):
    b_, n_tok, n_h, d = q.shape
    rows = n_tok // GRID_W
    wh = min(NA_WIN_H, rows)
    ww = NA_WIN_W
    n_ctx = k_ctx.shape[1]
    scale = d ** -0.5
    qg = q.reshape(b_, rows, GRID_W, n_h, d)
    kg = k.reshape(b_, rows, GRID_W, n_h, d)
    vg = v.reshape(b_, rows, GRID_W, n_h, d)
    cols = jnp.arange(GRID_W)
    col_idx = jnp.clip(cols - ww // 2, 0, GRID_W - ww)[:, None] + jnp.arange(ww)[None, :]
    rpb_x = rpb[:, :, col_idx - cols[:, None] + (NA_WIN_W - 1)]

    def row(r):
        r0 = jnp.clip(r - wh // 2, 0, rows - wh)
        q_r = lax.dynamic_index_in_dim(qg, r, axis=1, keepdims=False)
        k_win = lax.dynamic_slice_in_dim(kg, r0, wh, axis=1)[:, :, col_idx]
        v_win = lax.dynamic_slice_in_dim(vg, r0, wh, axis=1)[:, :, col_idx]
        bias = rpb_x[:, r0 + jnp.arange(wh) - r + (NA_WIN_H - 1)]
        s_loc = (jnp.einsum('bqhd,byqxhd->bhqyx', q_r, k_win).astype(jnp.float32) * scale
                 + jnp.transpose(bias, (0, 2, 1, 3))[None].astype(jnp.float32))
        s_ctx = jnp.einsum('bqhd,bkhd->bhqk', q_r, k_ctx).astype(jnp.float32) * scale
        s = jnp.concatenate([s_ctx, s_loc.reshape(b_, n_h, GRID_W, wh * ww)], axis=-1)
        p = jax.nn.softmax(s, axis=-1).astype(v.dtype)
        p_loc = p[..., n_ctx:].reshape(b_, n_h, GRID_W, wh, ww)
        return (jnp.einsum('bhqk,bkhd->bqhd', p[..., :n_ctx], v_ctx)
                + jnp.einsum('bhqyx,byqxhd->bqhd', p_loc, v_win))

    o = lax.map(row, jnp.arange(rows))
    return jnp.moveaxis(o, 0, 1).reshape(b_, n_tok, n_h * d)


def mla_q(qa, q_norm, w_uq, rope):
    q = heads(rmsnorm(qa, q_norm) @ w_uq, MLA_HEADS)
    q_nope, q_rope = jnp.split(q, [MLA_NOPE], axis=-1)
    if rope is not None:
        q_rope = apply_rope(q_rope, *rope)
    return jnp.concatenate([q_nope, q_rope], axis=-1)


def mla_kv(kva, kr, kv_norm, w_ukv, rope):
    kv = heads(rmsnorm(kva, kv_norm) @ w_ukv, MLA_HEADS)
    k_nope, v = jnp.split(kv, [MLA_NOPE], axis=-1)
    k_rope = kr[:, :, None, :]
    if rope is not None:
        k_rope = apply_rope(k_rope, *rope)
    k = jnp.concatenate([k_nope, jnp.broadcast_to(k_rope, k_nope.shape[:-1] + (MLA_ROPE,))],
                        axis=-1)
    return k, v


def merge(o_ssm, o_gqa, o_na, o_mla, gate_logits, w_o_ssm, w_o_gqa, w_o_na, w_o_mla, w_out):
    g = jax.nn.sigmoid(gate_logits.astype(jnp.float32)).astype(gate_logits.dtype)
    g_ssm, g_gqa, g_na, g_mla = jnp.split(g, N_BRANCH, axis=-1)
    y = (g_ssm * (o_ssm @ w_o_ssm) + g_gqa * (o_gqa @ w_o_gqa)
         + g_na * (o_na @ w_o_na) + g_mla * (o_mla @ w_o_mla))
    return y @ w_out


def layer(x, ctx, c, c_ctx, rope_g, rope_m, ada_w, ada_b, norm_pre, norm_post, w_in,
          conv_w, conv_b, a_log, dt_bias, d_skip, ssm_norm, w_o_ssm,
          gqa_q_norm, gqa_k_norm, w_o_gqa, na_rpb, w_o_na,
          mla_q_norm, w_uq, mla_kv_norm, w_ukv, w_o_mla, w_out, need_ctx):
    shift, scale, gate = jnp.split((jax.nn.silu(c) @ ada_w + ada_b)[:, None, :], 3, axis=-1)
    shift_c, scale_c, gate_c = jnp.split(jax.nn.silu(c_ctx) @ ada_w + ada_b, 3, axis=-1)
    h = rmsnorm(x, norm_pre) * (1 + scale) + shift
    h_c = rmsnorm(ctx, norm_pre) * (1 + scale_c) + shift_c
    (z, xbc, dtr, gq, gk, gv, gg, nq, nk, nv, ng,
     mqa, mkva, mkr, mg, mix) = split_cols(h @ w_in)
    (z_c, xbc_c, dtr_c, gq_c, gk_c, gv_c, gg_c, nq_c, nk_c, nv_c, ng_c,
     mqa_c, mkva_c, mkr_c, mg_c, mix_c) = split_cols(h_c @ w_in)

    o_ssm, o_ssm_c = ssm_branch(z, xbc, dtr, z_c, xbc_c, dtr_c, conv_w, conv_b, a_log, dt_bias,
                                d_skip, ssm_norm, need_ctx)

    gqa_scale = GQA_HEAD_DIM ** -0.5
    q = apply_rope(rmsnorm(heads(gq, GQA_HEADS), gqa_q_norm), *rope_g)
    k = apply_rope(rmsnorm(heads(gk, GQA_KV_HEADS), gqa_k_norm), *rope_g)
    v = heads(gv, GQA_KV_HEADS)
    k_c = rmsnorm(heads(gk_c, GQA_KV_HEADS), gqa_k_norm)
    v_c = heads(gv_c, GQA_KV_HEADS)
    o_gqa = attend_latent(group_q(q, GQA_KV_HEADS), k, v, k_c, v_c, gqa_scale) * jax.nn.silu(gg)

    nk_ch, nv_ch = heads(nk_c, NA_HEADS), heads(nv_c, NA_HEADS)
    o_na = na_latent(heads(nq, NA_HEADS), heads(nk, NA_HEADS), heads(nv, NA_HEADS),
                     nk_ch, nv_ch, na_rpb) * jax.nn.silu(ng)

    mla_scale = (MLA_NOPE + MLA_ROPE) ** -0.5
    q_m = mla_q(mqa, mla_q_norm, w_uq, rope_m)
    k_m, v_m = mla_kv(mkva, mkr, mla_kv_norm, w_ukv, rope_m)
    k_mc, v_mc = mla_kv(mkva_c, mkr_c, mla_kv_norm, w_ukv, None)
    o_mla = attend_latent(group_q(q_m, MLA_HEADS), k_m, v_m, k_mc, v_mc, mla_scale) * jax.nn.silu(mg)

    y = merge(o_ssm, o_gqa, o_na, o_mla, mix, w_o_ssm, w_o_gqa, w_o_na, w_o_mla, w_out)
    x = x + gate * rmsnorm(y, norm_post)

    if need_ctx:
        q_c = rmsnorm(heads(gq_c, GQA_HEADS), gqa_q_norm)
        o_gqa_c = attend_context(group_q(q_c, GQA_KV_HEADS), k_c, v_c, gqa_scale) * jax.nn.silu(gg_c)
        o_na_c = attend_context(group_q(heads(nq_c, NA_HEADS), NA_HEADS), nk_ch, nv_ch,
                                NA_HEAD_DIM ** -0.5) * jax.nn.silu(ng_c)
        q_mc = mla_q(mqa_c, mla_q_norm, w_uq, None)
        o_mla_c = attend_context(group_q(q_mc, MLA_HEADS), k_mc, v_mc, mla_scale) * jax.nn.silu(mg_c)
        y_c = merge(o_ssm_c, o_gqa_c, o_na_c, o_mla_c, mix_c, w_o_ssm, w_o_gqa, w_o_na, w_o_mla, w_out)
        ctx = ctx + gate_c * rmsnorm(y_c, norm_post)
    return x, ctx


def setup_inputs(seed: int = 0) -> dict:
    key = jax.random.key(seed)
    ks = jax.random.split(key, 32)
    f32 = jnp.float32
    n_l = DEPTH

    def dense(k, shape, fan_in):
        return jax.random.normal(k, shape, f32) * fan_in ** -0.5

    def gain(k, shape):
        return 1.0 + 0.05 * jax.random.normal(k, shape, f32)

    dt0 = jnp.exp(jax.random.uniform(ks[10], (n_l, 2, SSM_HEADS), f32,
                                     math.log(1e-3), math.log(1e-1)))
    return {
        'x': jax.random.normal(ks[0], (BATCH, SEQ, D_MODEL), f32),
        'c': jax.random.normal(ks[1], (BATCH, D_MODEL), f32),
        'ctx': jax.random.normal(ks[2], (BATCH, CTX_LEN, D_MODEL), f32),
        'c_ctx': jax.random.normal(ks[3], (D_MODEL,), f32),
        'ada_w': dense(ks[4], (n_l, D_MODEL, 3 * D_MODEL), D_MODEL),
        'ada_b': 0.02 * jax.random.normal(ks[5], (n_l, 3 * D_MODEL), f32),
        'norm_pre': gain(ks[6], (n_l, D_MODEL)),
        'norm_post': gain(ks[7], (n_l, D_MODEL)),
        'w_in': dense(ks[8], (n_l, D_MODEL, IN_WIDTH), D_MODEL),
        'conv_w': dense(ks[9], (n_l, SSM_CONV, SSM_CONV_DIM), SSM_CONV),
        'conv_b': 0.02 * jax.random.normal(ks[11], (n_l, SSM_CONV_DIM), f32),
        'a_log': jnp.log(jax.random.uniform(ks[12], (n_l, 2, SSM_HEADS), f32, 1.0, 16.0)),
        'dt_bias': dt0 + jnp.log(-jnp.expm1(-dt0)),
        'd_skip': 1.0 + 0.1 * jax.random.normal(ks[13], (n_l, SSM_HEADS), f32),
        'ssm_norm': gain(ks[14], (n_l, SSM_INNER)),
        'w_o_ssm': dense(ks[15], (n_l, SSM_INNER, D_MODEL), SSM_INNER),
        'gqa_q_norm': gain(ks[16], (n_l, GQA_HEAD_DIM)),
        'gqa_k_norm': gain(ks[17], (n_l, GQA_HEAD_DIM)),
        'w_o_gqa': dense(ks[18], (n_l, GQA_WIDTH, D_MODEL), GQA_WIDTH),
        'na_rpb': 0.1 * jax.random.normal(ks[19], (n_l, NA_HEADS, 2 * NA_WIN_H - 1, 2 * NA_WIN_W - 1), f32),
        'w_o_na': dense(ks[20], (n_l, NA_WIDTH, D_MODEL), NA_WIDTH),
        'mla_q_norm': gain(ks[21], (n_l, MLA_Q_LORA)),
        'w_uq': dense(ks[22], (n_l, MLA_Q_LORA, MLA_HEADS * (MLA_NOPE + MLA_ROPE)), MLA_Q_LORA),
        'mla_kv_norm': gain(ks[23], (n_l, MLA_KV_LORA)),
        'w_ukv': dense(ks[24], (n_l, MLA_KV_LORA, MLA_HEADS * (MLA_NOPE + MLA_V)), MLA_KV_LORA),
        'w_o_mla': dense(ks[25], (n_l, MLA_WIDTH, D_MODEL), MLA_WIDTH),
        'w_out': dense(ks[26], (n_l, D_MODEL, D_MODEL), D_MODEL),
    }


def reference(x, c, ctx, c_ctx, ada_w, ada_b, norm_pre, norm_post, w_in, conv_w, conv_b,
              a_log, dt_bias, d_skip, ssm_norm, w_o_ssm, gqa_q_norm, gqa_k_norm, w_o_gqa,
              na_rpb, w_o_na, mla_q_norm, w_uq, mla_kv_norm, w_ukv, w_o_mla, w_out):
    n_tok = x.shape[1]
    rope_g = rope_tables(n_tok, GQA_HEAD_DIM)
    rope_m = rope_tables(n_tok, MLA_ROPE)
    for l in range(DEPTH):
        x, ctx = layer(x, ctx, c, c_ctx, rope_g, rope_m, ada_w[l], ada_b[l], norm_pre[l],
                       norm_post[l], w_in[l], conv_w[l], conv_b[l], a_log[l], dt_bias[l],
                       d_skip[l], ssm_norm[l], w_o_ssm[l], gqa_q_norm[l], gqa_k_norm[l],
                       w_o_gqa[l], na_rpb[l], w_o_na[l], mla_q_norm[l], w_uq[l],
                       mla_kv_norm[l], w_ukv[l], w_o_mla[l], w_out[l],
                       need_ctx=l < DEPTH - 1)
    return x
```

```python
import contextlib
import math
import numpy as np
import concourse.bass as bass
import concourse.mybir as mybir
from concourse.bass_utils import run_bass_kernel_spmd

F32 = mybir.dt.float32
BF16 = mybir.dt.bfloat16
AF = mybir.ActivationFunctionType
ALU = mybir.AluOpType
AX = mybir.AxisListType

ENGS = ("sp", "act", "dve", "pool", "pe")

D = 2048
CTX = 256
GRID_W = 64
EPS = 1e-6
ROPE_THETA = 10000.0
SSM_HEADS = 32
SSM_P = 64
SSM_G = 4
SSM_N = 128
SSM_CONV = 5
NA_WH = 8
NA_WW = 16
IN_SIZES = (2048, 3072, 64, 1024, 512, 512, 1024, 1024, 1024, 1024, 1024, 768, 512, 64, 1024, 8192)
IN_OFF = np.concatenate([[0], np.cumsum(IN_SIZES)]).astype(np.int64)
(O_Z, O_XBC, O_DT, O_GQ, O_GK, O_GV, O_GG, O_NQ, O_NK, O_NV, O_NG, O_MQA, O_MKVA, O_MKR, O_MG, O_MIX) = [int(v) for v in IN_OFF[:-1]]
NBLK = 45


class Buf:
    __slots__ = ("w", "r", "dsem", "name")

    def __init__(self, name=""):
        self.w = {}
        self.r = {}
        self.dsem = None
        self.name = name


class Tile:
    __slots__ = ("t", "b")

    def __init__(self, t, name=""):
        self.t = t
        self.b = Buf(name)


class Sched:
    def __init__(self, nc, n_dma_sems=84):
        self.nc = nc
        self.es = contextlib.ExitStack()
        self.esem = {}
        self.ecount = {}
        self.seen = {}
        self.q = {}
        for e in ENGS:
            self.esem[e] = self.es.enter_context(nc.semaphore("es_" + e))
            self.ecount[e] = 0
            self.seen[e] = {}
            self.q[e] = []
        self.dma_sems = [self.es.enter_context(nc.semaphore("ds%d" % i)) for i in range(n_dma_sems)]
        self.free_dsems = list(self.dma_sems)
        self.scount = {id(s): 0 for s in self.dma_sems}
        self.semobj = {id(s): s for s in self.dma_sems}
        for e in ENGS:
            self.semobj[id(self.esem[e])] = self.esem[e]
        self.is_dma = set(id(s) for s in self.dma_sems)
        self.phase_bufs = []
        self.phase_es = None
        self.n_instr = 0
        self.uid = 0

    def begin_phase(self):
        self.phase_es = contextlib.ExitStack()
        self.phase_bufs = []

    def sbuf(self, name, shape, dtype):
        self.uid += 1
        nm = "%s_%d" % (name, self.uid)
        t = self.phase_es.enter_context(self.nc.sbuf_tensor(nm, list(shape), dtype))
        return Tile(t, nm)

    def psum(self, name, shape, dtype=F32):
        self.uid += 1
        nm = "%s_%d" % (name, self.uid)
        t = self.phase_es.enter_context(self.nc.psum_tensor(nm, list(shape), dtype))
        return Tile(t, nm)

    def _dsem(self, buf):
        if buf.dsem is None:
            if not self.free_dsems:
                raise RuntimeError("out of DMA semaphores")
            buf.dsem = self.free_dsems.pop()
            self.phase_bufs.append(buf)
        return buf.dsem

    def _events(self, eng, reads, writes, self_sync):
        ev = {}
        own = id(self.esem[eng])
        for b in reads:
            for k, v in b.w.items():
                if ev.get(k, 0) < v:
                    ev[k] = v
        for b in writes:
            for k, v in b.w.items():
                if ev.get(k, 0) < v:
                    ev[k] = v
            for k, v in b.r.items():
                if k == own:
                    continue
                if ev.get(k, 0) < v:
                    ev[k] = v
        if not self_sync and own in ev:
            del ev[own]
        waits = []
        seen = self.seen[eng]
        for k, v in ev.items():
            if k in self.is_dma:
                v = self.scount[k]
            if seen.get(k, 0) < v:
                seen[k] = v
                waits.append((self.semobj[k], v))
        return waits

    def op(self, eng, fn, reads=(), writes=(), self_sync=None):
        if self_sync is None:
            self_sync = eng != "pe"
        waits = self._events(eng, reads, writes, self_sync)
        self.ecount[eng] += 1
        c = self.ecount[eng]
        own = self.esem[eng]
        self.q[eng].append((waits, fn, own, 1))
        k = id(own)
        for b in reads:
            b.r[k] = c
        for b in writes:
            b.w = {k: c}
            b.r = {}
        self.n_instr += 1

    def dma(self, q, out, in_, reads=(), writes=(), owner=None):
        waits = self._events(q, reads, writes, True)
        sem = self._dsem(owner)
        k = id(sem)
        self.scount[k] += 16
        c = self.scount[k]
        self.q[q].append((waits, (lambda e, o=out, i=in_: e.dma_start(out=o, in_=i)), sem, 16))
        for b in reads:
            b.r[k] = c
        for b in writes:
            b.w = {k: c}
            b.r = {}
        self.n_instr += 1

    def barrier(self):
        tot = {}
        for e in ENGS:
            tot[id(self.esem[e])] = self.ecount[e]
        for k, v in self.scount.items():
            tot[k] = v
        for e in ENGS:
            seen = self.seen[e]
            waits = []
            for k, v in tot.items():
                if v > 0 and seen.get(k, 0) < v and k != id(self.esem[e]):
                    seen[k] = v
                    waits.append((self.semobj[k], v))
            if waits:
                self.q[e].append((waits, None, None, 0))

    def end_phase(self):
        self.barrier()
        nc = self.nc
        with nc.Block() as block:
            decos = {"sp": block.sync, "act": block.scalar, "dve": block.vector,
                     "pool": block.gpsimd, "pe": block.tensor}
            for name in ENGS:
                items = self.q[name]

                def body(e, items=items):
                    for waits, fn, sem, inc in items:
                        for ws, wv in waits:
                            e.wait_ge(ws, wv)
                        if fn is not None:
                            fn(e).then_inc(sem, inc)

                if items:
                    decos[name](body)
                self.q[name] = []
        for b in self.phase_bufs:
            self.free_dsems.append(b.dsem)
            b.dsem = None
        self.phase_bufs = []
        self.phase_es.close()
        self.phase_es = None

    def close(self):
        self.es.close()


class Cfg:
    def __init__(self, S=8192, DEPTH=4, debug=(), stop_after=None):
        self.S = S
        self.C = CTX
        self.T = S + CTX
        self.NT = self.T // 128
        self.DEPTH = DEPTH
        self.rows = S // GRID_W
        self.TP = self.T + 8
        self.debug = set(debug)
        self.stop_after = stop_after
        st = [(0, 2)]
        t = 2
        while t < self.NT:
            n = min(4, self.NT - t)
            st.append((t, n))
            t += n
        self.stiles = st

    def padcol(self, tok):
        return tok + 2 if tok < self.C else tok + 6


def _deint(n):
    return np.concatenate([np.arange(0, n, 2), np.arange(1, n, 2)])


def win_device_cols():
    cols = []

    def add(a):
        cols.append(np.asarray(a, dtype=np.int64))

    add(O_Z + np.arange(2048))
    add(O_GG + np.arange(1024))
    add(O_NG + np.arange(1024))
    add(O_MG + np.arange(1024))
    add(O_MIX + np.arange(8192))
    add(O_XBC + np.arange(3072))
    add(O_NQ + np.arange(1024))
    add(O_NK + np.arange(1024))
    add(O_GV + np.arange(512))
    add(O_NV + np.arange(1024))
    di = _deint(128)
    add(np.concatenate([O_GQ + h * 128 + di for h in range(8)]))
    add(np.concatenate([O_GK + h * 128 + di for h in range(4)]))
    add(O_MQA + np.arange(512))
    add(np.concatenate([O_MQA + 512 + np.arange(256), O_MKR + _deint(64), O_DT + np.arange(64),
                        -np.ones(128, np.int64)]))
    add(O_MKVA + np.arange(512))
    c = np.concatenate(cols)
    assert c.shape[0] == NBLK * 512
    return c


BLK_KIND = (["z"] * 4 + ["gg"] * 2 + ["ng"] * 2 + ["mg"] * 2 + ["mix"] * 16 + ["xbc"] * 6 + ["nq"] * 2 + ["nk"] * 2
            + ["gv"] + ["nv"] * 2 + ["gq"] * 2 + ["gk"] + ["mqa0", "mqa1", "mkva"])
BLK_FIRST = {}
for _i, _k in enumerate(BLK_KIND):
    BLK_FIRST.setdefault(_k, _i)
FM_KINDS = ("gg", "ng", "mg", "mix", "xbc", "nq", "nk")


class MK:
    def __init__(self, cfg):
        self.cfg = cfg
        self.nc = bass.Bass("TRN2", target_bir_lowering=False)
        self.S = Sched(self.nc)
        self.dbufs = {}
        self.outputs = []

    def din(self, name, shape, dtype=F32):
        t = self.nc.dram_tensor(name, list(shape), dtype, kind="ExternalInput")
        self.dbufs[name] = Buf(name)
        return t.ap()

    def dscr(self, name, shape, dtype):
        kind = "ExternalOutput" if name in self.cfg.debug else "Internal"
        t = self.nc.dram_tensor(name, list(shape), dtype, kind=kind)
        if kind == "ExternalOutput":
            self.outputs.append(name)
        self.dbufs[name] = Buf(name)
        return t.ap()

    def B(self, name):
        return self.dbufs[name]

    def dump(self, name, ap, shape, dtype, buf):
        if ("dump:" + name) not in self.cfg.debug or name in self.dbufs:
            return
        t = self.nc.dram_tensor(name, list(shape), dtype, kind="ExternalOutput")
        self.outputs.append(name)
        self.dbufs[name] = Buf(name)
        self.S.dma("sp", t.ap(), ap, reads=[buf], writes=[self.dbufs[name]], owner=buf)

    def act(self, fn, reads, writes):
        self.S.op("act", fn, reads, writes)

    def dve(self, fn, reads, writes):
        self.S.op("dve", fn, reads, writes)

    def pool(self, fn, reads, writes):
        self.S.op("pool", fn, reads, writes)

    def pe(self, fn, reads, writes):
        self.S.op("pe", fn, reads, writes)

    def mm(self, out, lhsT, rhs, start, stop, reads, writes):
        self.S.op("pe", lambda e: e.matmul(out, lhsT, rhs, start=start, stop=stop), reads, writes)

    def tr(self, out, in_, ident, reads, writes):
        self.S.op("pe", lambda e: e.transpose(out, in_, ident), reads, writes)

    def declare(self):
        cfg = self.cfg
        L = cfg.DEPTH
        T = cfg.T
        self.xin = self.din("xin", [T, D])
        self.cmod = self.din("cmod", [128, 2, 16])
        self.ada_w = self.din("ada_w", [L, 12, 128, 16, 512])
        self.ada_b = self.din("ada_b", [L, 1, 3 * D])
        self.norm_pre = self.din("norm_pre", [L, 1, D])
        self.norm_post = self.din("norm_post", [L, 1, D])
        self.w_in = self.din("w_in", [L, NBLK, 128, 16, 512])
        self.w_uq_n = self.din("w_uq_n", [L, 128, 6, 1024])
        self.w_uq_r = self.din("w_uq_r", [L, 128, 6, 512])
        self.w_ukv_k = self.din("w_ukv_k", [L, 128, 4, 1024])
        self.w_ukv_v = self.din("w_ukv_v", [L, 128, 4, 1024])
        self.w_o = self.din("w_o", [L, 128, 40, D])
        self.w_out = self.din("w_out", [L, 128, 16, D])
        self.conv_w = self.din("conv_w", [L, 128, 24, 5])
        self.conv_b = self.din("conv_b", [L, 128, 24])
        self.conv_b_row = self.din("conv_b_row", [L, 1, 3072])
        self.a_log = self.din("a_log", [L, 1, 64])
        self.dt_bias = self.din("dt_bias", [L, 1, 64])
        self.d_skip = self.din("d_skip", [L, 1, 32])
        self.ssm_norm = self.din("ssm_norm", [L, 1, D])
        self.gq_norm = self.din("gq_norm", [L, 1, 128])
        self.gk_norm = self.din("gk_norm", [L, 1, 128])
        self.mq_norm = self.din("mq_norm", [L, 1, 768])
        self.mkv_norm = self.din("mkv_norm", [L, 1, 512])
        _mats, self.na_table, self.na_keyset = na_bias_tables(cfg, np.zeros((8, 15, 31), np.float32))
        self.nbm = len(_mats)
        self.na_bias = self.din("na_bias", [L, 8, self.nbm, 128, 128])
        self.ropeG = self.din("ropeG", [T, 128])
        self.ropeM = self.din("ropeM", [T, 64])
        self.out = self.nc.dram_tensor("out", [cfg.S, D], F32, kind="ExternalOutput").ap()
        self.dbufs["out"] = Buf("out")
        self.w_in_bf = [self.dscr("w_in_bf%d" % i, [NBLK, 128, 16, 512], BF16) for i in range(L)]
        self.w_o_bf = self.dscr("w_o_bf", [L, 128, 40, D], BF16)
        self.modv = self.dscr("modv", [L, 2, 3, D], F32)
        self.xres = [self.dscr("xres%d" % i, [T, D], F32) for i in range(2)]
        self.z_s = self.dscr("z_s", [T, D], BF16)
        self.gT = {k: self.dscr("gT_" + k, [1024, T], BF16) for k in ("gg", "ng", "mg")}
        self.mixT = self.dscr("mixT", [8192, T], BF16)
        self.xbcT = self.dscr("xbcT", [3072, cfg.TP], BF16)
        self.QT_n = self.dscr("QT_n", [1024, T], BF16)
        self.KT_n = self.dscr("KT_n", [1024, T], BF16)
        self.V_g = self.dscr("V_g", [T, 512], BF16)
        self.V_n = self.dscr("V_n", [T, 1024], BF16)
        self.QT_g = self.dscr("QT_g", [1024, T], BF16)
        self.KT_g = self.dscr("KT_g", [512, T], BF16)
        self.dtr = self.dscr("dtr", [T, 64], F32)
        self.QT_mn = self.dscr("QT_mn", [1024, T], BF16)
        self.QT_mr = self.dscr("QT_mr", [512, T], BF16)
        self.KT_mn = self.dscr("KT_mn", [1024, T], BF16)
        self.KT_mr = self.dscr("KT_mr", [64, T], BF16)
        self.V_m = self.dscr("V_m", [T, 1024], BF16)
        self.xc = self.dscr("xc", [T, D], BF16)
        self.Bc = self.dscr("Bc", [T, 512], BF16)
        self.BT = self.dscr("BT", [512, T], BF16)
        self.CT = self.dscr("CT", [512, T], BF16)
        self.y_f = self.dscr("y_f", [T, D], F32)
        self.yT = self.dscr("yT", [2048, T], BF16)
        self.oT = {"ssm": self.dscr("oT_ssm", [2048, T], BF16), "gqa": self.dscr("oT_gqa", [1024, T], BF16),
                   "na": self.dscr("oT_na", [1024, T], BF16), "mla": self.dscr("oT_mla", [1024, T], BF16)}

    def consts(self):
        nc = self.nc
        es = self.S.es
        S = self.S

        def g(name, shape, dt):
            return Tile(es.enter_context(nc.sbuf_tensor(name, list(shape), dt)), name)

        self.ident_f = g("ident_f", [128, 128], F32)
        self.ident_b = g("ident_b", [128, 128], BF16)
        self.ones_b = g("ones_b", [128, 128], BF16)
        self.ones_f = g("ones_f", [128, 128], F32)
        S.begin_phase()
        i_f, i_b, o_b, o_f = self.ident_f, self.ident_b, self.ones_b, self.ones_f
        self.pool(lambda e: e.memset(i_f.t[:], 0.0), [], [i_f.b])
        self.pool(lambda e: e.affine_select(out=i_f.t[:], in_=i_f.t[:], pattern=[[-1, 128]], compare_op=ALU.not_equal,
                                            fill=1.0, base=0, channel_multiplier=1), [i_f.b], [i_f.b])
        self.dve(lambda e: e.tensor_copy(i_b.t[:], i_f.t[:]), [i_f.b], [i_b.b])
        self.pool(lambda e: e.memset(o_b.t[:], 1.0), [], [o_b.b])
        self.pool(lambda e: e.memset(o_f.t[:], 1.0), [], [o_f.b])
        S.end_phase()

    def phase0(self):
        cfg = self.cfg
        S = self.S
        L = cfg.DEPTH
        S.begin_phase()
        self.B_win = [Buf("win%d" % l) for l in range(L)]
        self.B_wo = [Buf("wo%d" % l) for l in range(L)]
        p0 = getattr(cfg, "p0", ("cast", "pad", "mod"))
        for l in range(L if "cast" in p0 else 0):
            for j in range(NBLK):
                S.dma("pool", self.w_in_bf[l][j], self.w_in[l, j], reads=[], writes=[self.B_win[l]], owner=self.B_win[l])
            for k in range(0, 40, 8):
                S.dma("pool", self.w_o_bf[l, :, k:k + 8, :], self.w_o[l, :, k:k + 8, :], reads=[], writes=[self.B_wo[l]],
                      owner=self.B_wo[l])
        zt = S.sbuf("zt", [128, 24, 4], BF16)
        self.dve(lambda e: e.memset(zt.t[:], 0.0), [], [zt.b])
        xv = self.xbcT.rearrange("(k p) t -> p k t", p=128)
        C, TP = cfg.C, cfg.TP
        if "pad" in p0:
            S.dma("sp", xv[:, :, 0:2], zt.t[:, :, 0:2], reads=[zt.b], writes=[self.B("xbcT")], owner=zt.b)
            S.dma("sp", xv[:, :, C + 2:C + 6], zt.t[:, :, 0:4], reads=[zt.b], writes=[self.B("xbcT")], owner=zt.b)
            S.dma("sp", xv[:, :, TP - 2:TP], zt.t[:, :, 0:2], reads=[zt.b], writes=[self.B("xbcT")], owner=zt.b)
        if "mod" not in p0:
            S.end_phase()
            return
        cm = S.sbuf("cm", [128, 2, 16], F32)
        S.dma("sp", cm.t[:], self.cmod, reads=[], writes=[cm.b], owner=cm.b)
        cs = S.sbuf("cs", [128, 2, 16], F32)
        self.act(lambda e: e.activation(out=cs.t[:], in_=cm.t[:], func=AF.Silu), [cm.b], [cs.b])
        csr = S.sbuf("csr", [128, 2, 16, 128], BF16)
        for w in range(2):
            self.dve(lambda e, w=w: e.tensor_copy(csr.t[:, w], cs.t[:, w].unsqueeze(2).to_broadcast([128, 16, 128])),
                     [cs.b], [csr.b])
        wts = [S.sbuf("adaw%d" % i, [128, 16, 512], BF16) for i in range(2)]
        ps = [S.psum("ps0_%d" % i, [128, 512]) for i in range(4)]
        modsb = [S.sbuf("modsb%d" % w, [128, 3 * D], F32) for w in range(2)]
        adab = S.sbuf("adab", [128, 3 * D], F32)
        npre = S.sbuf("npre", [128, D], F32)
        npost = S.sbuf("npost", [128, D], F32)
        res = [S.sbuf("modres%d" % i, [1, D], F32) for i in range(3)]
        pi = 0
        for l in range(L):
            S.dma("sp", adab.t[:], self.ada_b[l].partition_broadcast(128), reads=[], writes=[adab.b], owner=adab.b)
            S.dma("sp", npre.t[:], self.norm_pre[l].partition_broadcast(128), reads=[], writes=[npre.b], owner=npre.b)
            S.dma("sp", npost.t[:], self.norm_post[l].partition_broadcast(128), reads=[], writes=[npost.b], owner=npost.b)
            for j in range(12):
                wt = wts[j % 2]
                S.dma("pool", wt.t[:], self.ada_w[l, j], reads=[], writes=[wt.b], owner=wt.b)
                for w in range(2):
                    p = ps[pi % 4]
                    pi += 1
                    for k in range(16):
                        self.mm(p.t[:], csr.t[:, w, k, :], wt.t[:, k, :], k == 0, k == 15, [csr.b, wt.b], [p.b])
                    self.dve(lambda e, p=p, w=w, j=j: e.tensor_tensor(modsb[w].t[:, j * 512:(j + 1) * 512], p.t[:],
                                                                      adab.t[:, j * 512:(j + 1) * 512], ALU.add),
                             [p.b, adab.b], [modsb[w].b])
            for w in range(2):
                m = modsb[w]
                self.dve(lambda e, m=m: e.scalar_tensor_tensor(out=res[0].t[:], in0=m.t[0:1, D:2 * D], scalar=1.0,
                                                               in1=npre.t[0:1, :], op0=ALU.add, op1=ALU.mult),
                         [m.b, npre.b], [res[0].b])
                self.act(lambda e, m=m: e.activation(out=res[1].t[:], in_=m.t[0:1, 0:D], func=AF.Copy), [m.b], [res[1].b])
                self.dve(lambda e, m=m: e.tensor_tensor(res[2].t[:], m.t[0:1, 2 * D:3 * D], npost.t[0:1, :], ALU.mult),
                         [m.b, npost.b], [res[2].b])
                for i in range(3):
                    S.dma("sp", self.modv[l, w, i:i + 1, :], res[i].t[:], reads=[res[i].b], writes=[self.B("modv")],
                          owner=res[i].b)
        S.end_phase()

    def phaseA(self, l):
        cfg = self.cfg
        S = self.S
        T = cfg.T
        S.begin_phase()
        xsrc = self.xin if l == 0 else self.xres[(l - 1) % 2]
        xsrcB = self.B("xin") if l == 0 else self.B("xres%d" % ((l - 1) % 2))
        ident = self.ident_b
        Apre = S.sbuf("Apre", [128, D], F32)
        shf = S.sbuf("shf", [128, D], F32)
        gqn = S.sbuf("gqn", [128, 128], F32)
        gkn = S.sbuf("gkn", [128, 128], F32)
        mqn = S.sbuf("mqn", [128, 768], F32)
        mkvn = S.sbuf("mkvn", [128, 512], F32)
        for tl, src in ((gqn, self.gq_norm), (gkn, self.gk_norm), (mqn, self.mq_norm), (mkvn, self.mkv_norm)):
            S.dma("sp", tl.t[:], src[l].partition_broadcast(128), reads=[], writes=[tl.b], owner=tl.b)
        wuq_n = S.sbuf("wuq_n", [128, 6, 1024], BF16)
        wuq_r = S.sbuf("wuq_r", [128, 6, 512], BF16)
        wukv_k = S.sbuf("wukv_k", [128, 4, 1024], BF16)
        wukv_v = S.sbuf("wukv_v", [128, 4, 1024], BF16)
        for tl, src in ((wuq_n, self.w_uq_n), (wuq_r, self.w_uq_r), (wukv_k, self.w_ukv_k), (wukv_v, self.w_ukv_v)):
            S.dma("pool", tl.t[:], src[l], reads=[], writes=[tl.b], owner=tl.b)
        hT = S.sbuf("hT", [128, 16, 512], BF16)
        Wt = [S.sbuf("Wt%d" % i, [128, 16, 512], BF16) for i in range(2)]
        xt = [S.sbuf("xt%d" % i, [128, D], F32) for i in range(2)]
        tmpf = S.sbuf("tmpf", [128, D], F32)
        hb = S.sbuf("hb", [128, D], BF16)
        stg = [S.sbuf("stg%d" % i, [128, 4, 512], BF16) for i in range(4)]
        rg = S.sbuf("rg", [128, 4, 128], F32)
        rm = S.sbuf("rm", [128, 4, 64], F32)
        qaT = S.sbuf("qaT", [128, 6, 512], BF16)
        kvaT = S.sbuf("kvaT", [128, 4, 512], BF16)
        dts = S.sbuf("dts", [128, 4, 64], F32)
        krT = S.sbuf("krT", [64, 512], BF16)
        sqt = S.sbuf("sqt", [128, 512], F32)
        qn = S.sbuf("qn", [128, 4, 128], F32)
        rt = [S.sbuf("rt%d" % i, [128, 4, 64], F32) for i in range(4)]
        qr = S.sbuf("qr", [128, 4, 128], BF16)
        qan = S.sbuf("qan", [128, 768], BF16)
        kr = S.sbuf("kr", [128, 64], BF16)
        sm = [S.sbuf("sm%d" % i, [128, 8], F32) for i in range(6)]
        pm = [S.psum("pm%d" % i, [128, 512]) for i in range(6)]
        pt = [S.psum("pt%d" % i, [128, 1024], BF16) for i in range(2)]
        st = {"pm": 0, "pt": 0, "stg": 0, "g": 0}

        def nb():
            p = pm[st["pm"] % 6]
            st["pm"] += 1
            return p

        def npt():
            p = pt[st["pt"] % 2]
            st["pt"] += 1
            return p

        def nstg():
            s_ = stg[st["stg"] % 4]
            st["stg"] += 1
            return s_

        slot = {}

        def ensure_loaded(si, j):
            if (si, j) in slot or si >= len(cfg.stiles):
                return
            w = Wt[st["g"] % 2]
            st["g"] += 1
            slot[(si, j)] = w
            S.dma("sp", w.t[:], self.w_in_bf[l][j], reads=[self.B_win[l]], writes=[w.b], owner=w.b)

        def rstd_from(ss_ap, n, rs, r1, r2, reads):
            self.dve(lambda e: e.tensor_scalar(r1[0], ss_ap, 1.0 / n, EPS, ALU.mult, ALU.add), reads, [r1[1]])
            self.act(lambda e: e.activation(out=r2[0], in_=r1[0], func=AF.Sqrt), [r1[1]], [r2[1]])
            self.dve(lambda e: e.reciprocal(rs[0], r2[0]), [r2[1]], [rs[1]])

        def rope(dst, src, reads, cos, sin, tabB, nh, hd, dstB):
            x0 = src[:, :, 0:hd]
            x1 = src[:, :, hd:2 * hd]
            cb = cos.unsqueeze(1).to_broadcast([128, nh, hd])
            sb = sin.unsqueeze(1).to_broadcast([128, nh, hd])
            r = [t_.t[:, 0:nh, 0:hd] if nh <= 4 else None for t_ in rt]
            for i, (a, b_) in enumerate(((x0, cb), (x1, sb), (x0, sb), (x1, cb))):
                self.dve(lambda e, i=i, a=a, b_=b_: e.tensor_tensor(r[i], a, b_, ALU.mult), reads + [tabB], [rt[i].b])
            self.dve(lambda e: e.tensor_tensor(dst[:, :, 0:hd], r[0], r[1], ALU.subtract), [rt[0].b, rt[1].b], [dstB])
            self.dve(lambda e: e.tensor_tensor(dst[:, :, hd:2 * hd], r[2], r[3], ALU.add), [rt[2].b, rt[3].b], [dstB])

        cur_which = [None]
        for si, (t0, n) in enumerate(cfg.stiles):
            TS = n * 128
            tok0 = t0 * 128
            which = 1 if si == 0 else 0
            if cur_which[0] != which:
                cur_which[0] = which
                S.dma("sp", Apre.t[:], self.modv[l, which, 0:1, :].partition_broadcast(128), reads=[self.B("modv")],
                      writes=[Apre.b], owner=Apre.b)
                S.dma("sp", shf.t[:], self.modv[l, which, 1:2, :].partition_broadcast(128), reads=[self.B("modv")],
                      writes=[shf.b], owner=shf.b)
            ensure_loaded(si, 0)
            S.dma("sp", rg.t[:, 0:n, :], self.ropeG[tok0:tok0 + TS, :].rearrange("(a p) c -> p a c", p=128), reads=[],
                  writes=[rg.b], owner=rg.b)
            S.dma("sp", rm.t[:, 0:n, :], self.ropeM[tok0:tok0 + TS, :].rearrange("(a p) c -> p a c", p=128), reads=[],
                  writes=[rm.b], owner=rm.b)
            for tt in range(n):
                gt = t0 + tt
                x_ = xt[gt % 2]
                S.dma("sp", x_.t[:], xsrc[gt * 128:(gt + 1) * 128, :], reads=[xsrcB], writes=[x_.b], owner=x_.b)
                ss, r1, r2, rs = sm[0], sm[1], sm[2], sm[3]
                self.act(lambda e, x_=x_: e.activation(out=tmpf.t[:], in_=x_.t[:], func=AF.Square, accum_out=ss.t[:, 0:1]),
                         [x_.b], [tmpf.b, ss.b])
                rstd_from(ss.t[:, 0:1], D, (rs.t[:, 0:1], rs.b), (r1.t[:, 0:1], r1.b), (r2.t[:, 0:1], r2.b), [ss.b])
                self.dve(lambda e, x_=x_: e.scalar_tensor_tensor(out=tmpf.t[:], in0=x_.t[:], scalar=rs.t[:, 0:1], in1=Apre.t[:],
                                                                 op0=ALU.mult, op1=ALU.mult), [x_.b, rs.b, Apre.b], [tmpf.b])
                self.dve(lambda e: e.tensor_tensor(hb.t[:], tmpf.t[:], shf.t[:], ALU.add), [tmpf.b, shf.b], [hb.b])
                for half in range(2):
                    p = npt()
                    for c in range(8):
                        cc = half * 8 + c
                        self.tr(p.t[:, c * 128:(c + 1) * 128], hb.t[:, cc * 128:(cc + 1) * 128], ident.t[:], [hb.b, ident.b], [p.b])
                    self.act(lambda e, p=p, half=half, tt=tt: e.activation(
                        out=hT.t[:, half * 8:(half + 1) * 8, tt * 128:(tt + 1) * 128],
                        in_=p.t[:].rearrange("p (c t) -> p c t", c=8), func=AF.Copy), [p.b], [hT.b])

            def tm_dst(ap2d, c0, ncols):
                return ap2d[tok0:tok0 + TS, c0:c0 + ncols].rearrange("(a p) c -> p a c", p=128)

            def fm_dst(ap2d, r0, nrows, col0):
                return ap2d[r0:r0 + nrows, col0:col0 + TS].rearrange("(a p) t -> p a t", p=128)

            def tm_matmul(w, tt):
                p = nb()
                for k in range(16):
                    self.mm(p.t[:], hT.t[:, k, tt * 128:(tt + 1) * 128], w.t[:, k, :], k == 0, k == 15, [hT.b, w.b], [p.b])
                return p

            def fm_matmul(w, cb):
                p = nb()
                for k in range(16):
                    self.mm(p.t[:, 0:TS], w.t[:, k, cb * 128:(cb + 1) * 128], hT.t[:, k, 0:TS], k == 0, k == 15,
                            [hT.b, w.b], [p.b])
                return p

            def simple_tm(j, w, func, dst2d, dstB, c0):
                sg = nstg()
                for tt in range(n):
                    p = tm_matmul(w, tt)
                    self.act(lambda e, p=p, tt=tt: e.activation(out=sg.t[:, tt, :], in_=p.t[:], func=func), [p.b], [sg.b])
                S.dma("pool", tm_dst(dst2d, c0, 512), sg.t[:, 0:n, :], reads=[sg.b], writes=[dstB], owner=sg.b)

            def simple_fm(j, w, func, dst2d, dstB, r0, col0, scale=1.0):
                sg = nstg()
                for cb in range(4):
                    p = fm_matmul(w, cb)
                    self.act(lambda e, p=p, cb=cb, TS=TS: e.activation(out=sg.t[:, cb, 0:TS], in_=p.t[:, 0:TS], func=func, scale=scale),
                             [p.b], [sg.b])
                S.dma("pool", fm_dst(dst2d, r0, 512, col0), sg.t[:, :, 0:TS], reads=[sg.b], writes=[dstB], owner=sg.b)

            def qk_block(j, w, gain, dst2d, dstB, h0):
                sg = nstg()
                for tt in range(n):
                    p = tm_matmul(w, tt)
                    ss4, r1, r2, rs4 = sm[0], sm[1], sm[2], sm[3]
                    self.act(lambda e, p=p: e.activation(out=sqt.t[:], in_=p.t[:], func=AF.Square), [p.b], [sqt.b])
                    self.dve(lambda e: e.reduce_sum(ss4.t[:, 0:4], sqt.t[:].rearrange("p (h d) -> p h d", h=4), AX.X),
                             [sqt.b], [ss4.b])
                    rstd_from(ss4.t[:, 0:4], 128, (rs4.t[:, 0:4], rs4.b), (r1.t[:, 0:4], r1.b), (r2.t[:, 0:4], r2.b), [ss4.b])
                    self.dve(lambda e, p=p: e.tensor_tensor(qn.t[:], p.t[:].rearrange("p (h d) -> p h d", h=4),
                                                            rs4.t[:, 0:4].unsqueeze(2).to_broadcast([128, 4, 128]), ALU.mult),
                             [p.b, rs4.b], [qn.b])
                    self.dve(lambda e: e.tensor_tensor(qn.t[:], qn.t[:], gain.t[:].unsqueeze(1).to_broadcast([128, 4, 128]),
                                                       ALU.mult), [qn.b, gain.b], [qn.b])
                    rope(qr.t, qn.t, [qn.b], rg.t[:, tt, 0:64], rg.t[:, tt, 64:128], rg.b, 4, 64, qr.b)
                    pp = npt()
                    for h in range(4):
                        self.tr(pp.t[:, h * 128:(h + 1) * 128], qr.t[:, h, :], ident.t[:], [qr.b, ident.b], [pp.b])
                    self.act(lambda e, pp=pp, tt=tt: e.activation(out=sg.t[:, :, tt * 128:(tt + 1) * 128],
                                                                 in_=pp.t[:, 0:512].rearrange("p (h t) -> p h t", h=4),
                                                                 func=AF.Copy), [pp.b], [sg.b])
                S.dma("pool", fm_dst(dst2d, h0 * 128, 512, tok0), sg.t[:, :, 0:TS], reads=[sg.b], writes=[dstB], owner=sg.b)

            def mqa_pair(w0, w1):
                for tt in range(n):
                    p0 = tm_matmul(w0, tt)
                    p1 = tm_matmul(w1, tt)
                    ssa, ssb, r1, r2, rs = sm[0], sm[4], sm[1], sm[2], sm[3]
                    self.act(lambda e, p0=p0: e.activation(out=sqt.t[:], in_=p0.t[:], func=AF.Square, accum_out=ssa.t[:, 0:1]),
                             [p0.b], [sqt.b, ssa.b])
                    self.act(lambda e, p1=p1: e.activation(out=sqt.t[:, 0:256], in_=p1.t[:, 0:256], func=AF.Square,
                                                           accum_out=ssb.t[:, 0:1]), [p1.b], [sqt.b, ssb.b])
                    self.dve(lambda e: e.tensor_tensor(ssa.t[:, 0:1], ssa.t[:, 0:1], ssb.t[:, 0:1], ALU.add), [ssa.b, ssb.b], [ssa.b])
                    rstd_from(ssa.t[:, 0:1], 768, (rs.t[:, 0:1], rs.b), (r1.t[:, 0:1], r1.b), (r2.t[:, 0:1], r2.b), [ssa.b])
                    self.dve(lambda e, p0=p0: e.scalar_tensor_tensor(out=qan.t[:, 0:512], in0=p0.t[:], scalar=rs.t[:, 0:1],
                                                                     in1=mqn.t[:, 0:512], op0=ALU.mult, op1=ALU.mult),
                             [p0.b, rs.b, mqn.b], [qan.b])
                    self.dve(lambda e, p1=p1: e.scalar_tensor_tensor(out=qan.t[:, 512:768], in0=p1.t[:, 0:256], scalar=rs.t[:, 0:1],
                                                                     in1=mqn.t[:, 512:768], op0=ALU.mult, op1=ALU.mult),
                             [p1.b, rs.b, mqn.b], [qan.b])
                    pp = npt()
                    for c in range(6):
                        self.tr(pp.t[:, c * 128:(c + 1) * 128], qan.t[:, c * 128:(c + 1) * 128], ident.t[:], [qan.b, ident.b], [pp.b])
                    self.act(lambda e, pp=pp, tt=tt: e.activation(out=qaT.t[:, :, tt * 128:(tt + 1) * 128],
                                                                 in_=pp.t[:, 0:768].rearrange("p (c t) -> p c t", c=6),
                                                                 func=AF.Copy), [pp.b], [qaT.b])
                    rope(kr.t[:].rearrange("p (h d) -> p h d", h=1), p1.t[:, 256:320].rearrange("p (h d) -> p h d", h=1), [p1.b],
                         rm.t[:, tt, 0:32], rm.t[:, tt, 32:64], rm.b, 1, 32, kr.b)
                    pp2 = npt()
                    self.tr(pp2.t[0:64, 0:128], kr.t[:, :], ident.t[:], [kr.b, ident.b], [pp2.b])
                    self.act(lambda e, pp2=pp2, tt=tt: e.activation(out=krT.t[:, tt * 128:(tt + 1) * 128], in_=pp2.t[0:64, 0:128],
                                                                   func=AF.Copy), [pp2.b], [krT.b])
                    self.act(lambda e, p1=p1, tt=tt: e.activation(out=dts.t[:, tt, :], in_=p1.t[:, 320:384], func=AF.Copy),
                             [p1.b], [dts.b])
                S.dma("pool", self.KT_mr[:, tok0:tok0 + TS], krT.t[:, 0:TS], reads=[krT.b], writes=[self.B("KT_mr")], owner=krT.b)
                S.dma("pool", self.dtr[tok0:tok0 + TS, :].rearrange("(a p) c -> p a c", p=128), dts.t[:, 0:n, :], reads=[dts.b],
                      writes=[self.B("dtr")], owner=dts.b)

            def mkva_block(w):
                for tt in range(n):
                    p = tm_matmul(w, tt)
                    ss, r1, r2, rs = sm[0], sm[1], sm[2], sm[3]
                    self.act(lambda e, p=p: e.activation(out=sqt.t[:], in_=p.t[:], func=AF.Square, accum_out=ss.t[:, 0:1]),
                             [p.b], [sqt.b, ss.b])
                    rstd_from(ss.t[:, 0:1], 512, (rs.t[:, 0:1], rs.b), (r1.t[:, 0:1], r1.b), (r2.t[:, 0:1], r2.b), [ss.b])
                    self.dve(lambda e, p=p: e.scalar_tensor_tensor(out=qan.t[:, 0:512], in0=p.t[:], scalar=rs.t[:, 0:1],
                                                                   in1=mkvn.t[:], op0=ALU.mult, op1=ALU.mult),
                             [p.b, rs.b, mkvn.b], [qan.b])
                    pp = npt()
                    for c in range(4):
                        self.tr(pp.t[:, c * 128:(c + 1) * 128], qan.t[:, c * 128:(c + 1) * 128], ident.t[:], [qan.b, ident.b], [pp.b])
                    self.act(lambda e, pp=pp, tt=tt: e.activation(out=kvaT.t[:, :, tt * 128:(tt + 1) * 128],
                                                                 in_=pp.t[:, 0:512].rearrange("p (c t) -> p c t", c=4),
                                                                 func=AF.Copy), [pp.b], [kvaT.b])

            def mla_stage2():
                for (wt_, nk, srcT, dst2d, dname) in ((wuq_n, 6, qaT, self.QT_mn, "QT_mn"), (wukv_k, 4, kvaT, self.KT_mn, "KT_mn")):
                    for hg in range(2):
                        sg = nstg()
                        for hh in range(4):
                            h = hg * 4 + hh
                            p = nb()
                            for k in range(nk):
                                self.mm(p.t[:, 0:TS], wt_.t[:, k, h * 128:(h + 1) * 128], srcT.t[:, k, 0:TS], k == 0, k == nk - 1,
                                        [wt_.b, srcT.b], [p.b])
                            self.act(lambda e, p=p, hh=hh, sg=sg, TS=TS: e.activation(out=sg.t[:, hh, 0:TS], in_=p.t[:, 0:TS], func=AF.Copy),
                                     [p.b], [sg.b])
                        S.dma("pool", fm_dst(dst2d, hg * 512, 512, tok0), sg.t[:, :, 0:TS], reads=[sg.b], writes=[self.B(dname)],
                              owner=sg.b)
                sg = nstg()
                for tt in range(n):
                    p = nb()
                    for k in range(6):
                        self.mm(p.t[:], qaT.t[:, k, tt * 128:(tt + 1) * 128], wuq_r.t[:, k, :], k == 0, k == 5, [qaT.b, wuq_r.b], [p.b])
                    for hg in range(2):
                        rope(qr.t[:].rearrange("p h d -> p (h d)")[:, hg * 256:(hg + 1) * 256].rearrange("p (h d) -> p h d", h=4),
                             p.t[:, hg * 256:(hg + 1) * 256].rearrange("p (h d) -> p h d", h=4), [p.b],
                             rm.t[:, tt, 0:32], rm.t[:, tt, 32:64], rm.b, 4, 32, qr.b)
                    pp = npt()
                    qrf = qr.t[:].rearrange("p h d -> p (h d)")
                    for c in range(4):
                        self.tr(pp.t[:, c * 128:(c + 1) * 128], qrf[:, c * 128:(c + 1) * 128], ident.t[:], [qr.b, ident.b], [pp.b])
                    self.act(lambda e, pp=pp, tt=tt, sg=sg: e.activation(out=sg.t[:, :, tt * 128:(tt + 1) * 128],
                                                                        in_=pp.t[:, 0:512].rearrange("p (c t) -> p c t", c=4),
                                                                        func=AF.Copy), [pp.b], [sg.b])
                S.dma("pool", fm_dst(self.QT_mr, 0, 512, tok0), sg.t[:, :, 0:TS], reads=[sg.b], writes=[self.B("QT_mr")], owner=sg.b)
                for half in range(2):
                    sg = nstg()
                    for tt in range(n):
                        p = nb()
                        for k in range(4):
                            self.mm(p.t[:], kvaT.t[:, k, tt * 128:(tt + 1) * 128], wukv_v.t[:, k, half * 512:(half + 1) * 512],
                                    k == 0, k == 3, [kvaT.b, wukv_v.b], [p.b])
                        self.act(lambda e, p=p, tt=tt, sg=sg: e.activation(out=sg.t[:, tt, :], in_=p.t[:], func=AF.Copy), [p.b], [sg.b])
                    S.dma("pool", tm_dst(self.V_m, half * 512, 512), sg.t[:, 0:n, :], reads=[sg.b], writes=[self.B("V_m")], owner=sg.b)

            for j in range(NBLK):
                kind = BLK_KIND[j]
                ensure_loaded(si, j)
                nxt = (si, j + 1) if j + 1 < NBLK else (si + 1, 0)
                if kind == "mqa0":
                    ensure_loaded(si, j + 1)
                    continue
                if kind != "mqa1":
                    ensure_loaded(*nxt)
                w = slot[(si, j)]
                jj = j - BLK_FIRST[kind]
                if kind == "z":
                    simple_tm(j, w, AF.Silu, self.z_s, self.B("z_s"), jj * 512)
                elif kind in ("gg", "ng", "mg"):
                    simple_fm(j, w, AF.Silu, self.gT[kind], self.B("gT_" + kind), jj * 512, tok0)
                elif kind == "mix":
                    simple_fm(j, w, AF.Sigmoid, self.mixT, self.B("mixT"), jj * 512, tok0)
                elif kind == "xbc":
                    simple_fm(j, w, AF.Copy, self.xbcT, self.B("xbcT"), jj * 512, cfg.padcol(tok0))
                elif kind == "nq":
                    simple_fm(j, w, AF.Copy, self.QT_n, self.B("QT_n"), jj * 512, tok0, scale=128.0 ** -0.5)
                elif kind == "nk":
                    simple_fm(j, w, AF.Copy, self.KT_n, self.B("KT_n"), jj * 512, tok0)
                elif kind == "gv":
                    simple_tm(j, w, AF.Copy, self.V_g, self.B("V_g"), 0)
                elif kind == "nv":
                    simple_tm(j, w, AF.Copy, self.V_n, self.B("V_n"), jj * 512)
                elif kind == "gq":
                    qk_block(j, w, gqn, self.QT_g, self.B("QT_g"), jj * 4)
                elif kind == "gk":
                    qk_block(j, w, gkn, self.KT_g, self.B("KT_g"), 0)
                elif kind == "mqa1":
                    mqa_pair(slot[(si, j - 1)], w)
                    ensure_loaded(*nxt)
                elif kind == "mkva":
                    mkva_block(w)
                    mla_stage2()
        S.end_phase()


    def phaseS1(self, l):
        cfg = self.cfg
        S = self.S
        S.begin_phase()
        ident = self.ident_b
        cw = S.sbuf("cw", [128, 24, 5], F32)
        cb = S.sbuf("cb", [128, 24], F32)
        cbrow = S.sbuf("cbrow", [128, 2560], F32)
        S.dma("sp", cw.t[:], self.conv_w[l], reads=[], writes=[cw.b], owner=cw.b)
        S.dma("sp", cb.t[:], self.conv_b[l], reads=[], writes=[cb.b], owner=cb.b)
        S.dma("sp", cbrow.t[:], self.conv_b_row[l][:, 0:2560].partition_broadcast(128), reads=[], writes=[cbrow.b], owner=cbrow.b)
        dg = S.sbuf("dg", [128, 120, 128], BF16)
        self.dve(lambda e: e.tensor_tensor(dg.t[:], ident.t[:].unsqueeze(1).to_broadcast([128, 120, 128]),
                                           cw.t[:].rearrange("p a b -> p (a b)").unsqueeze(2).to_broadcast([128, 120, 128]), ALU.mult),
                 [ident.b, cw.b], [dg.b])
        uw = [S.sbuf("uw%d" % i, [128, 24, 516], BF16) for i in range(2)]
        tf = [S.sbuf("tf%d" % i, [128, 512], F32) for i in range(2)]
        sx = [S.sbuf("sx%d" % i, [128, 2560], BF16) for i in range(2)]
        sf = [S.sbuf("sf%d" % i, [128, 4, 512], BF16) for i in range(2)]
        pm = [S.psum("pS1_%d" % i, [128, 512]) for i in range(8)]
        cnt = {"pm": 0, "tf": 0, "sx": 0, "sf": 0}

        def nb():
            p = pm[cnt["pm"] % 8]
            cnt["pm"] += 1
            return p

        xv = self.xbcT.rearrange("(k p) t -> p k t", p=128)
        for si, (t0, n) in enumerate(cfg.stiles):
            TS = n * 128
            tok0 = t0 * 128
            col0 = cfg.padcol(tok0)
            u = uw[si % 2]
            S.dma("sp", u.t[:, :, 0:TS + 4], xv[:, :, col0 - 2:col0 + TS + 2], reads=[self.B("xbcT")], writes=[u.b], owner=u.b)
            for tt in range(n):
                sxt = sx[cnt["sx"] % 2]
                cnt["sx"] += 1
                for bk in range(5):
                    p = nb()
                    for q4 in range(4):
                        blk = bk * 4 + q4
                        for tap in range(5):
                            self.mm(p.t[:, q4 * 128:(q4 + 1) * 128], u.t[:, blk, tt * 128 + tap:tt * 128 + tap + 128],
                                    dg.t[:, blk * 5 + tap, :], tap == 0, tap == 4, [u.b, dg.b], [p.b])
                    t_ = tf[cnt["tf"] % 2]
                    cnt["tf"] += 1
                    self.dve(lambda e, p=p, t_=t_, bk=bk: e.tensor_tensor(t_.t[:], p.t[:], cbrow.t[:, bk * 512:(bk + 1) * 512], ALU.add),
                             [p.b, cbrow.b], [t_.b])
                    self.act(lambda e, t_=t_, sxt=sxt, bk=bk: e.activation(out=sxt.t[:, bk * 512:(bk + 1) * 512], in_=t_.t[:], func=AF.Silu),
                             [t_.b], [sxt.b])
                r0 = tok0 + tt * 128
                S.dma("pool", self.xc[r0:r0 + 128, :], sxt.t[:, 0:2048], reads=[sxt.b], writes=[self.B("xc")], owner=sxt.b)
                S.dma("pool", self.Bc[r0:r0 + 128, :], sxt.t[:, 2048:2560], reads=[sxt.b], writes=[self.B("Bc")], owner=sxt.b)
            for which, dst, dname in ((0, self.BT, "BT"), (1, self.CT, "CT")):
                sft = sf[cnt["sf"] % 2]
                cnt["sf"] += 1
                for q4 in range(4):
                    blk = 16 + which * 4 + q4
                    p = nb()
                    for tap in range(5):
                        self.mm(p.t[:, 0:TS], dg.t[:, blk * 5 + tap, :], u.t[:, blk, tap:tap + TS], tap == 0, tap == 4, [u.b, dg.b], [p.b])
                    self.act(lambda e, p=p, sft=sft, q4=q4, blk=blk, TS=TS: e.activation(out=sft.t[:, q4, 0:TS], in_=p.t[:, 0:TS], func=AF.Silu,
                                                                                bias=cb.t[:, blk:blk + 1]), [p.b, cb.b], [sft.b])
                S.dma("pool", dst[:, tok0:tok0 + TS].rearrange("(a p) t -> p a t", p=128), sft.t[:, :, 0:TS], reads=[sft.b],
                      writes=[self.B(dname)], owner=sft.b)
        S.end_phase()

    def phaseS2(self, l):
        cfg = self.cfg
        S = self.S
        NT = cfg.NT
        S.begin_phase()
        ident_f, ident_b, ones_f = self.ident_f, self.ident_b, self.ones_f
        tri = [S.sbuf("tri%d" % d, [128, 128], F32) for d in range(2)]
        for d in range(2):
            self.pool(lambda e, d=d: e.memset(tri[d].t[:], 1.0), [], [tri[d].b])
            self.pool(lambda e, d=d: e.affine_select(out=tri[d].t[:], in_=tri[d].t[:], pattern=[[1 if d == 0 else -1, 128]],
                                                     compare_op=ALU.is_ge, fill=0.0, base=0,
                                                     channel_multiplier=-1 if d == 0 else 1), [tri[d].b], [tri[d].b])
        E = S.sbuf("Esel", [32, 32, 128], F32)
        self.pool(lambda e: e.memset(E.t[:], 0.0), [], [E.b])
        self.pool(lambda e: e.affine_select(out=E.t[:], in_=E.t[:], pattern=[[1, 32], [0, 128]], compare_op=ALU.not_equal, fill=1.0,
                                            base=0, channel_multiplier=-1), [E.b], [E.b])
        dt_all = S.sbuf("dt_all", [128, NT, 64], F32)
        da_all = S.sbuf("da_all", [128, NT, 64], F32)
        avec = S.sbuf("avec", [128, 64], F32)
        dtb = S.sbuf("dtb", [128, 64], F32)
        dsk = S.sbuf("dsk", [128, 32], F32)
        nrm = S.sbuf("nrm", [128, D], F32)
        S.dma("sp", dt_all.t[:], self.dtr.rearrange("(c p) h -> p c h", p=128), reads=[self.B("dtr")], writes=[dt_all.b], owner=dt_all.b)
        S.dma("sp", avec.t[:], self.a_log[l].partition_broadcast(128), reads=[], writes=[avec.b], owner=avec.b)
        S.dma("sp", dtb.t[:], self.dt_bias[l].partition_broadcast(128), reads=[], writes=[dtb.b], owner=dtb.b)
        S.dma("sp", dsk.t[:], self.d_skip[l].partition_broadcast(128), reads=[], writes=[dsk.b], owner=dsk.b)
        S.dma("sp", nrm.t[:], self.ssm_norm[l].partition_broadcast(128), reads=[], writes=[nrm.b], owner=nrm.b)
        self.dve(lambda e: e.tensor_tensor(dt_all.t[:], dt_all.t[:], dtb.t[:].unsqueeze(1).to_broadcast([128, NT, 64]), ALU.add),
                 [dt_all.b, dtb.b], [dt_all.b])
        self.act(lambda e: e.activation(out=dt_all.t[:], in_=dt_all.t[:], func=AF.Exp), [dt_all.b], [dt_all.b])
        self.act(lambda e: e.activation(out=dt_all.t[:], in_=dt_all.t[:], func=AF.Ln, bias=1.0), [dt_all.b], [dt_all.b])
        self.act(lambda e: e.activation(out=avec.t[:], in_=avec.t[:], func=AF.Exp), [avec.b], [avec.b])
        self.dve(lambda e: e.scalar_tensor_tensor(out=da_all.t[:], in0=dt_all.t[:], scalar=-1.0,
                                                  in1=avec.t[:].unsqueeze(1).to_broadcast([128, NT, 64]), op0=ALU.mult, op1=ALU.mult),
                 [dt_all.b, avec.b], [da_all.b])
        St = S.sbuf("St", [128, D], F32)
        Sbf = S.sbuf("Sbf", [128, D], BF16)
        xct = [S.sbuf("xct%d" % i, [128, D], BF16) for i in range(2)]
        bct = [S.sbuf("bct%d" % i, [128, 512], BF16) for i in range(2)]
        btt = [S.sbuf("btt%d" % i, [128, 4, 128], BF16) for i in range(2)]
        ctt = [S.sbuf("ctt%d" % i, [128, 4, 128], BF16) for i in range(2)]
        yft = [S.sbuf("yft%d" % i, [128, D], F32) for i in range(2)]
        zst = [S.sbuf("zst%d" % i, [128, D], BF16) for i in range(2)]
        xs = S.sbuf("xs", [128, D], BF16)
        xte = S.sbuf("xte", [128, D], BF16)
        acs = S.sbuf("acs", [128, 64], F32)
        acsT = S.sbuf("acsT", [32, 128], F32)
        ead = S.sbuf("ead", [128, 64], F32)
        te = S.sbuf("te", [128, 32], F32)
        dtte = S.sbuf("dtte", [128, 32], F32)
        CBm = S.sbuf("CBm", [128, 4, 128], F32)
        seg = [S.sbuf("seg%d" % i, [128, 4, 128], F32) for i in range(2)]
        e4 = [S.sbuf("e4_%d" % i, [128, 4, 128], F32) for i in range(2)]
        G4 = [S.sbuf("G4_%d" % i, [128, 4, 128], BF16) for i in range(3)]
        tg = S.sbuf("tg", [128, 512], F32)
        ysb = [S.sbuf("ysb%d" % i, [128, D], F32) for i in range(2)]
        tsk = S.sbuf("tsk", [128, D], F32)
        gnb = S.sbuf("gnb", [128, D], BF16)
        gts = [S.sbuf("gts%d" % i, [128, 16, 128], BF16) for i in range(2)]
        smx = [S.sbuf("smx%d" % i, [128, 4], F32) for i in range(4)]
        pR = [S.psum("pR%d" % i, [128, 512]) for i in range(2)]
        pYd = S.psum("pYd", [128, 512])
        pYo = S.psum("pYo", [128, 512])
        pC = S.psum("pC", [128, 512])
        pM = S.psum("pM", [128, 512])
        pT = [S.psum("pT%d" % i, [128, 1024], BF16) for i in range(2)]
        cnt = {"R": 0, "seg": 0, "G": 0, "T": 0}

        for d in range(2):
            order = [0, 1] + list(range(2, NT)) if d == 0 else [1, 0] + list(range(NT - 1, 1, -1))
            self.dve(lambda e: e.memset(St.t[:], 0.0), [], [St.b])
            self.dve(lambda e: e.memset(Sbf.t[:], 0.0), [], [Sbf.b])

            def load(i):
                c = order[i]
                r0 = c * 128
                S.dma("sp", xct[i % 2].t[:], self.xc[r0:r0 + 128, :], reads=[self.B("xc")], writes=[xct[i % 2].b], owner=xct[i % 2].b)
                S.dma("sp", bct[i % 2].t[:], self.Bc[r0:r0 + 128, :], reads=[self.B("Bc")], writes=[bct[i % 2].b], owner=bct[i % 2].b)
                S.dma("sp", btt[i % 2].t[:], self.BT[:, r0:r0 + 128].rearrange("(g n) t -> n g t", n=128), reads=[self.B("BT")],
                      writes=[btt[i % 2].b], owner=btt[i % 2].b)
                S.dma("sp", ctt[i % 2].t[:], self.CT[:, r0:r0 + 128].rearrange("(g n) t -> n g t", n=128), reads=[self.B("CT")],
                      writes=[ctt[i % 2].b], owner=ctt[i % 2].b)
                if d == 1:
                    S.dma("sp", yft[i % 2].t[:], self.y_f[r0:r0 + 128, :], reads=[self.B("y_f")], writes=[yft[i % 2].b], owner=yft[i % 2].b)
                    S.dma("sp", zst[i % 2].t[:], self.z_s[r0:r0 + 128, :], reads=[self.B("z_s")], writes=[zst[i % 2].b], owner=zst[i % 2].b)

            load(0)
            for i in range(NT):
                if i + 1 < NT:
                    load(i + 1)
                c = order[i]
                xc_, bc_, bt_, ct_ = xct[i % 2], bct[i % 2], btt[i % 2], ctt[i % 2]
                da_c = da_all.t[:, c, d * 32:(d + 1) * 32]
                dt_c = dt_all.t[:, c, d * 32:(d + 1) * 32]
                self.mm(pM.t[:, 0:32], tri[d].t[:], da_c, True, True, [tri[d].b, da_all.b], [pM.b])
                self.mm(pM.t[:, 32:64], ones_f.t[:], da_c, True, True, [ones_f.b, da_all.b], [pM.b])
                self.dve(lambda e: e.tensor_copy(acs.t[:], pM.t[:, 0:64]), [pM.b], [acs.b])
                self.S.op("pe", lambda e: e.transpose(pM.t[0:32, 128:256], acs.t[:, 0:32], ident_f.t[:]), [acs.b, ident_f.b], [pM.b])
                self.act(lambda e: e.activation(out=acsT.t[:], in_=pM.t[0:32, 128:256], func=AF.Copy), [pM.b], [acsT.b])
                self.act(lambda e: e.activation(out=ead.t[:], in_=acs.t[:], func=AF.Exp), [acs.b], [ead.b])
                self.dve(lambda e: e.tensor_tensor(te.t[:], acs.t[:, 32:64], acs.t[:, 0:32], ALU.subtract), [acs.b], [te.b])
                self.act(lambda e: e.activation(out=te.t[:], in_=te.t[:], func=AF.Exp), [te.b], [te.b])
                self.dve(lambda e, dt_c=dt_c: e.tensor_tensor(dtte.t[:], te.t[:], dt_c, ALU.mult), [te.b, dt_all.b], [dtte.b])
                self.dve(lambda e, xc_=xc_, dt_c=dt_c: e.tensor_tensor(xs.t[:].rearrange("p (h q) -> p h q", h=32),
                                                                       xc_.t[:].rearrange("p (h q) -> p h q", h=32),
                                                                       dt_c.unsqueeze(2).to_broadcast([128, 32, 64]), ALU.mult),
                         [xc_.b, dt_all.b], [xs.b])
                self.dve(lambda e, xc_=xc_: e.tensor_tensor(xte.t[:].rearrange("p (h q) -> p h q", h=32),
                                                            xc_.t[:].rearrange("p (h q) -> p h q", h=32),
                                                            dtte.t[:].unsqueeze(2).to_broadcast([128, 32, 64]), ALU.mult),
                         [xc_.b, dtte.b], [xte.b])
                if i == 0 and d == 0:
                    self.dump("d_dt", dt_all.t[:], [128, NT, 64], F32, dt_all.b)
                    self.dump("d_da", da_all.t[:], [128, NT, 64], F32, da_all.b)
                    self.dump("d_acs", acs.t[:], [128, 64], F32, acs.b)
                    self.dump("d_acsT", acsT.t[:], [32, 128], F32, acsT.b)
                    self.dump("d_xs", xs.t[:], [128, D], BF16, xs.b)
                    self.dump("d_ead", ead.t[:], [128, 64], F32, ead.b)
                    self.dump("d_tri", tri[0].t[:], [128, 128], F32, tri[0].b)
                    self.dump("d_E", E.t[:], [32, 32, 128], F32, E.b)
                for g in range(4):
                    self.mm(pC.t[:, g * 128:(g + 1) * 128], bt_.t[:, g, :], ct_.t[:, g, :], True, True, [bt_.b, ct_.b], [pC.b])
                self.dve(lambda e, d=d: e.tensor_tensor(CBm.t[:], pC.t[:].rearrange("p (g t) -> p g t", g=4),
                                                        tri[d].t[:].unsqueeze(1).to_broadcast([128, 4, 128]), ALU.mult), [pC.b, tri[d].b], [CBm.b])
                y_ = ysb[i % 2]
                for g in range(4):
                    for hb_ in range(2):
                        h0 = g * 8 + hb_ * 4
                        R = pR[cnt["R"] % 2]
                        cnt["R"] += 1
                        for hh in range(4):
                            self.mm(R.t[:, hh * 128:(hh + 1) * 128], E.t[:, h0 + hh, :], acsT.t[:], True, True, [E.b, acsT.b], [R.b])
                        sg_ = seg[cnt["seg"] % 2]
                        e_ = e4[cnt["seg"] % 2]
                        cnt["seg"] += 1
                        G_ = G4[cnt["G"] % 3]
                        cnt["G"] += 1
                        self.dve(lambda e, R=R, sg_=sg_, h0=h0: e.tensor_tensor(sg_.t[:], R.t[:].rearrange("p (h t) -> p h t", h=4),
                                                                               acs.t[:, h0:h0 + 4].unsqueeze(2).to_broadcast([128, 4, 128]),
                                                                               ALU.subtract), [R.b, acs.b], [sg_.b])
                        self.act(lambda e, sg_=sg_, e_=e_: e.activation(out=e_.t[:], in_=sg_.t[:], func=AF.Exp), [sg_.b], [e_.b])
                        self.dve(lambda e, e_=e_, G_=G_, g=g: e.scalar_tensor_tensor(out=G_.t[:], in0=e_.t[:], scalar=1.0,
                                                                                    in1=CBm.t[:, g:g + 1, :].to_broadcast([128, 4, 128]),
                                                                                    op0=ALU.min, op1=ALU.mult), [e_.b, CBm.b], [G_.b])
                        if i == 0 and d == 0 and g == 0 and hb_ == 0:
                            self.dump("d_CBm", CBm.t[:], [128, 4, 128], F32, CBm.b)
                            self.dump("d_seg", sg_.t[:], [128, 4, 128], F32, sg_.b)
                            self.dump("d_e4", e_.t[:], [128, 4, 128], F32, e_.b)
                            self.dump("d_G4", G_.t[:], [128, 4, 128], BF16, G_.b)
                        for hh in range(4):
                            h = h0 + hh
                            self.mm(pYd.t[:, (h % 8) * 64:(h % 8 + 1) * 64], G_.t[:, hh, :], xs.t[:, h * 64:(h + 1) * 64], True, True,
                                    [G_.b, xs.b], [pYd.b])
                    self.mm(pYo.t[:], ct_.t[:, g, :], Sbf.t[:, g * 512:(g + 1) * 512], True, True, [ct_.b, Sbf.b], [pYo.b])
                    self.dve(lambda e, g=g: e.tensor_tensor(tg.t[:].rearrange("p (h q) -> p h q", h=8), pYo.t[:].rearrange("p (h q) -> p h q", h=8),
                                                            ead.t[:, g * 8:(g + 1) * 8].unsqueeze(2).to_broadcast([128, 8, 64]), ALU.mult),
                             [pYo.b, ead.b], [tg.b])
                    self.dve(lambda e, g=g, y_=y_: e.tensor_tensor(y_.t[:, g * 512:(g + 1) * 512], tg.t[:], pYd.t[:], ALU.add),
                             [tg.b, pYd.b], [y_.b])
                for g in range(4):
                    pcs = pYo if g % 2 == 0 else pYd
                    self.mm(pcs.t[:], bc_.t[:, g * 128:(g + 1) * 128], xte.t[:, g * 512:(g + 1) * 512], True, True, [bc_.b, xte.b], [pcs.b])
                    sl = St.t[:, g * 512:(g + 1) * 512]
                    self.dve(lambda e, g=g, sl=sl: e.tensor_tensor(sl.rearrange("p (h q) -> p h q", h=8), sl.rearrange("p (h q) -> p h q", h=8),
                                                                   ead.t[:, 32 + g * 8:32 + (g + 1) * 8].unsqueeze(2).to_broadcast([128, 8, 64]),
                                                                   ALU.mult), [St.b, ead.b], [St.b])
                    self.dve(lambda e, sl=sl, pcs=pcs: e.tensor_tensor(sl, sl, pcs.t[:], ALU.add), [St.b, pcs.b], [St.b])
                self.act(lambda e: e.activation(out=Sbf.t[:], in_=St.t[:], func=AF.Copy), [St.b], [Sbf.b])
                r0 = c * 128
                if d == 0:
                    S.dma("pool", self.y_f[r0:r0 + 128, :], y_.t[:], reads=[y_.b], writes=[self.B("y_f")], owner=y_.b)
                    continue
                yf_, zs_ = yft[i % 2], zst[i % 2]
                self.dve(lambda e, y_=y_, yf_=yf_: e.tensor_tensor(y_.t[:], y_.t[:], yf_.t[:], ALU.add), [y_.b, yf_.b], [y_.b])
                self.dve(lambda e, xc_=xc_: e.tensor_tensor(tsk.t[:].rearrange("p (h q) -> p h q", h=32), xc_.t[:].rearrange("p (h q) -> p h q", h=32),
                                                            dsk.t[:].unsqueeze(2).to_broadcast([128, 32, 64]), ALU.mult), [xc_.b, dsk.b], [tsk.b])
                self.dve(lambda e, y_=y_: e.tensor_tensor(y_.t[:], y_.t[:], tsk.t[:], ALU.add), [y_.b, tsk.b], [y_.b])
                self.dve(lambda e, y_=y_, zs_=zs_: e.tensor_tensor(y_.t[:], y_.t[:], zs_.t[:], ALU.mult), [y_.b, zs_.b], [y_.b])
                ss, r1, r2 = smx[0], smx[1], smx[2]
                for g in range(4):
                    self.act(lambda e, g=g, y_=y_: e.activation(out=tsk.t[:, g * 512:(g + 1) * 512], in_=y_.t[:, g * 512:(g + 1) * 512],
                                                               func=AF.Square, accum_out=ss.t[:, g:g + 1]), [y_.b], [tsk.b, ss.b])
                self.dve(lambda e: e.tensor_scalar(r1.t[:], ss.t[:], 1.0 / 512, EPS, ALU.mult, ALU.add), [ss.b], [r1.b])
                self.act(lambda e: e.activation(out=r2.t[:], in_=r1.t[:], func=AF.Ln), [r1.b], [r2.b])
                self.act(lambda e: e.activation(out=r2.t[:], in_=r2.t[:], func=AF.Exp, scale=-0.5), [r2.b], [r2.b])
                self.dve(lambda e, y_=y_: e.tensor_tensor(y_.t[:].rearrange("p (g q) -> p g q", g=4), y_.t[:].rearrange("p (g q) -> p g q", g=4),
                                                          r2.t[:].unsqueeze(2).to_broadcast([128, 4, 512]), ALU.mult), [y_.b, r2.b], [y_.b])
                self.dve(lambda e, y_=y_: e.tensor_tensor(gnb.t[:], y_.t[:], nrm.t[:], ALU.mult), [y_.b, nrm.b], [gnb.b])
                gs = gts[i % 2]
                for half in range(2):
                    p = pT[cnt["T"] % 2]
                    cnt["T"] += 1
                    for cc in range(8):
                        k = half * 8 + cc
                        self.tr(p.t[:, cc * 128:(cc + 1) * 128], gnb.t[:, k * 128:(k + 1) * 128], ident_b.t[:], [gnb.b, ident_b.b], [p.b])
                    self.act(lambda e, p=p, half=half, gs=gs: e.activation(out=gs.t[:, half * 8:(half + 1) * 8, :],
                                                                          in_=p.t[:].rearrange("p (c t) -> p c t", c=8), func=AF.Copy),
                             [p.b], [gs.b])
                S.dma("pool", self.oT["ssm"][:, r0:r0 + 128].rearrange("(k p) t -> p k t", p=128), gs.t[:], reads=[gs.b],
                      writes=[self.B("oT_ssm")], owner=gs.b)
        S.end_phase()


    def phaseT(self, l, last):
        cfg = self.cfg
        S = self.S
        T, NT = cfg.T, cfg.NT
        S.begin_phase()
        ident_b, ones_b = self.ident_b, self.ones_b
        KT = [S.sbuf("KT%d" % i, [128, T], BF16) for i in range(2)]
        VT = [S.sbuf("VT%d" % i, [128, NT, 128], BF16) for i in range(2)]
        QT = [S.sbuf("QT%d" % i, [128, T], BF16) for i in range(2)]
        KR = S.sbuf("KR", [64, T], BF16)
        QR = [S.sbuf("QR%d" % i, [64, T], BF16) for i in range(2)]
        nbt = S.sbuf("nbt", [128, self.nbm, 128], BF16)
        PT = [S.sbuf("PT%d" % i, [128, 1024], BF16) for i in range(3)]
        gt = [S.sbuf("gt%d" % i, [128, 512], BF16) for i in range(2)]
        rec = [S.sbuf("rec%d" % i, [128, 512], F32) for i in range(2)]
        of = [S.sbuf("of%d" % i, [128, 512], F32) for i in range(2)]
        ob = [S.sbuf("ob%d" % i, [128, 512], BF16) for i in range(2)]
        pS = [S.psum("pS%d" % i, [128, 1024]) for i in range(2)]
        pO = [S.psum("pO%d" % i, [128, 512]) for i in range(2)]
        pD = [S.psum("pD%d" % i, [128, 512]) for i in range(2)]
        cnt = {"S": 0, "P": 0, "O": 0, "kv": 0, "q": 0}
        qblocks = [(t0 * 128, n * 128) for (t0, n) in cfg.stiles]
        if last:
            qblocks = qblocks[1:]

        def finish(o, dd, gsrc, gB, orow, q0, TSq, dst, dname):
            k = cnt["O"]
            g_, r_, f_, b_ = gt[k % 2], rec[k % 2], of[k % 2], ob[k % 2]
            S.dma("sp", g_.t[:, 0:TSq], gsrc[orow:orow + 128, q0:q0 + TSq], reads=[gB], writes=[g_.b], owner=g_.b)
            self.dve(lambda e: e.reciprocal(r_.t[:, 0:TSq], dd.t[:, 0:TSq]), [dd.b], [r_.b])
            self.dve(lambda e: e.tensor_tensor(f_.t[:, 0:TSq], o.t[:, 0:TSq], r_.t[:, 0:TSq], ALU.mult), [o.b, r_.b], [f_.b])
            self.dve(lambda e: e.tensor_tensor(b_.t[:, 0:TSq], f_.t[:, 0:TSq], g_.t[:, 0:TSq], ALU.mult), [f_.b, g_.b], [b_.b])
            S.dma("pool", dst[orow:orow + 128, q0:q0 + TSq], b_.t[:, 0:TSq], reads=[b_.b], writes=[self.B(dname)], owner=b_.b)

        def dense_block(kparts, v, q0, TSq, ktiles, scale, o, dd, col0=0, first=True, lastg=True):
            nk = len(ktiles)
            for i in range(0, nk, 2):
                pr = ktiles[i:i + 2]
                ps = pS[cnt["S"] % 2]
                cnt["S"] += 1
                pt = PT[cnt["P"] % 3]
                cnt["P"] += 1
                for j, kt in enumerate(pr):
                    for pi, (kT, qT, nr) in enumerate(kparts):
                        self.mm(ps.t[:, j * 512:j * 512 + TSq], kT.t[0:nr, kt * 128:(kt + 1) * 128], qT.t[0:nr, q0:q0 + TSq],
                                pi == 0, pi == len(kparts) - 1, [kT.b, qT.b], [ps.b])
                npr = len(pr)
                if TSq == 512:
                    self.act(lambda e, ps=ps, pt=pt, npr=npr: e.activation(out=pt.t[:, 0:npr * 512], in_=ps.t[:, 0:npr * 512], func=AF.Exp,
                                                                          scale=scale), [ps.b], [pt.b])
                else:
                    self.act(lambda e, ps=ps, pt=pt, npr=npr: e.activation(
                        out=pt.t[:].rearrange("p (a b) -> p a b", a=2)[:, 0:npr, 0:TSq],
                        in_=ps.t[:].rearrange("p (a b) -> p a b", a=2)[:, 0:npr, 0:TSq], func=AF.Exp, scale=scale), [ps.b], [pt.b])
                for j, kt in enumerate(pr):
                    st_ = first and (i + j == 0)
                    sp_ = lastg and (i + j == nk - 1)
                    self.mm(o.t[:, col0:col0 + TSq], v.t[:, kt, :], pt.t[:, j * 512:j * 512 + TSq], st_, sp_, [v.b, pt.b], [o.b])
                    self.mm(dd.t[:, col0:col0 + TSq], ones_b.t[:], pt.t[:, j * 512:j * 512 + TSq], st_, sp_, [ones_b.b, pt.b], [dd.b])

        def load_kv(ksrc, kname, krow, vsrc, vname, vcol):
            i = cnt["kv"] % 2
            cnt["kv"] += 1
            S.dma("sp", KT[i].t[:], ksrc[krow:krow + 128, :], reads=[self.B(kname)], writes=[KT[i].b], owner=KT[i].b)
            S.dma("sp", VT[i].t[:], vsrc[:, vcol:vcol + 128].rearrange("(c p) d -> p c d", p=128), reads=[self.B(vname)],
                  writes=[VT[i].b], owner=VT[i].b)
            return KT[i], VT[i]

        def load_q(qsrc, qname, qrow):
            i = cnt["q"] % 2
            cnt["q"] += 1
            S.dma("sp", QT[i].t[:], qsrc[qrow:qrow + 128, :], reads=[self.B(qname)], writes=[QT[i].b], owner=QT[i].b)
            return QT[i], i

        def full_attention(kparts, v, scale, gsrc, gname, orow, dst, dname):
            for (q0, TSq) in qblocks:
                ktiles = [0, 1] if q0 == 0 else list(range(NT))
                o, dd = pO[cnt["O"] % 2], pD[cnt["O"] % 2]
                dense_block(kparts, v, q0, TSq, ktiles, scale, o, dd)
                finish(o, dd, gsrc, self.B(gname), orow, q0, TSq, dst, dname)
                cnt["O"] += 1

        sc_g = 128.0 ** -0.5
        for g in range(4):
            kT, v = load_kv(self.KT_g, "KT_g", g * 128, self.V_g, "V_g", g * 128)
            for r in range(2):
                h = g * 2 + r
                qT, _ = load_q(self.QT_g, "QT_g", h * 128)
                full_attention([(kT, qT, 128)], v, sc_g, self.gT["gg"], "gT_gg", h * 128, self.oT["gqa"], "oT_gqa")
        sc_m = 192.0 ** -0.5
        S.dma("sp", KR.t[:], self.KT_mr[:, :], reads=[self.B("KT_mr")], writes=[KR.b], owner=KR.b)
        for h in range(8):
            kT, v = load_kv(self.KT_mn, "KT_mn", h * 128, self.V_m, "V_m", h * 128)
            qT, qi = load_q(self.QT_mn, "QT_mn", h * 128)
            S.dma("sp", QR[qi].t[:], self.QT_mr[h * 64:(h + 1) * 64, :], reads=[self.B("QT_mr")], writes=[QR[qi].b], owner=QR[qi].b)
            full_attention([(kT, qT, 128), (KR, QR[qi], 64)], v, sc_m, self.gT["mg"], "gT_mg", h * 128, self.oT["mla"], "oT_mla")
        for h in range(8):
            kT, v = load_kv(self.KT_n, "KT_n", h * 128, self.V_n, "V_n", h * 128)
            qT, _ = load_q(self.QT_n, "QT_n", h * 128)
            S.dma("pool", nbt.t[:], self.na_bias[l, h].rearrange("m k q -> k m q"), reads=[], writes=[nbt.b], owner=nbt.b)
            for (q0, TSq) in qblocks:
                o, dd = pO[cnt["O"] % 2], pD[cnt["O"] % 2]
                if q0 == 0:
                    dense_block([(kT, qT, 128)], v, 0, TSq, [0, 1], 1.0, o, dd)
                else:
                    for qq in range(TSq // 128):
                        qi = (q0 - cfg.C) // 128 + qq
                        keys = self.na_keyset(qi)
                        qc = q0 + qq * 128
                        ps = pS[cnt["S"] % 2]
                        cnt["S"] += 1
                        pt = PT[cnt["P"] % 3]
                        cnt["P"] += 1
                        tiles = [0, 1] + [2 + kj for kj in keys]
                        for j, ktile in enumerate(tiles):
                            isw = j >= 2
                            self.mm(ps.t[:, j * 128:(j + 1) * 128], kT.t[:, ktile * 128:(ktile + 1) * 128], qT.t[:, qc:qc + 128], True, not isw,
                                    [kT.b, qT.b], [ps.b])
                            if isw:
                                bi = self.na_table[(qi, keys[j - 2])]
                                self.mm(ps.t[:, j * 128:(j + 1) * 128], ident_b.t[:], nbt.t[:, bi, :], False, True, [ident_b.b, nbt.b], [ps.b])
                        ncol = len(tiles) * 128
                        self.act(lambda e, ps=ps, pt=pt, ncol=ncol: e.activation(out=pt.t[:, 0:ncol], in_=ps.t[:, 0:ncol], func=AF.Exp), [ps.b], [pt.b])
                        for j, ktile in enumerate(tiles):
                            self.mm(o.t[:, qq * 128:(qq + 1) * 128], v.t[:, ktile, :], pt.t[:, j * 128:(j + 1) * 128], j == 0, j == len(tiles) - 1,
                                    [v.b, pt.b], [o.b])
                            self.mm(dd.t[:, qq * 128:(qq + 1) * 128], ones_b.t[:], pt.t[:, j * 128:(j + 1) * 128], j == 0, j == len(tiles) - 1,
                                    [ones_b.b, pt.b], [dd.b])
                finish(o, dd, self.gT["ng"], self.B("gT_ng"), h * 128, q0, TSq, self.oT["na"], "oT_na")
                cnt["O"] += 1
        S.end_phase()

    def phaseM1(self, l, last):
        cfg = self.cfg
        S = self.S
        S.begin_phase()
        oTs = [S.sbuf("oTs%d" % i, [128, 40, 512], BF16) for i in range(2)]
        wo = [S.sbuf("wo%d" % i, [128, 40, 256], BF16) for i in range(2)]
        mg = [S.sbuf("mg%d" % i, [128, 4, 512], BF16) for i in range(2)]
        tm = [S.sbuf("tm%d" % i, [128, 512], F32) for i in range(4)]
        ys = [S.sbuf("ys%d" % i, [128, 2, 512], BF16) for i in range(2)]
        pY = [S.psum("pY%d" % i, [128, 512]) for i in range(8)]
        cnt = {"p": 0, "w": 0, "m": 0, "y": 0}
        srcs = (("ssm", 0, 16), ("gqa", 16, 8), ("na", 24, 8), ("mla", 32, 8))
        mixv = self.mixT.rearrange("(b f) t -> f b t", b=4)
        stl = cfg.stiles[1:] if last else cfg.stiles
        for si, (t0, n) in enumerate(stl):
            TS = n * 128
            tok0 = t0 * 128
            ot = oTs[si % 2]
            for name, k0, nk in srcs:
                S.dma("sp", ot.t[:, k0:k0 + nk, 0:TS], self.oT[name][:, tok0:tok0 + TS].rearrange("(k p) t -> p k t", p=128),
                      reads=[self.B("oT_" + name)], writes=[ot.b], owner=ot.b)
            for fp in range(8):
                w = wo[cnt["w"] % 2]
                cnt["w"] += 1
                S.dma("sp", w.t[:], self.w_o_bf[l, :, :, fp * 256:(fp + 1) * 256], reads=[self.B_wo[l]], writes=[w.b], owner=w.b)
                yst = ys[cnt["y"] % 2]
                cnt["y"] += 1
                for f2 in range(2):
                    fo = fp * 2 + f2
                    m_ = mg[cnt["m"] % 2]
                    cnt["m"] += 1
                    S.dma("sp", m_.t[:, :, 0:TS], mixv[fo * 128:(fo + 1) * 128, :, tok0:tok0 + TS], reads=[self.B("mixT")], writes=[m_.b],
                          owner=m_.b)
                    for bi, (name, k0, nk) in enumerate(srcs):
                        p = pY[cnt["p"] % 8]
                        cnt["p"] += 1
                        for k in range(nk):
                            self.mm(p.t[:, 0:TS], w.t[:, k0 + k, f2 * 128:(f2 + 1) * 128], ot.t[:, k0 + k, 0:TS], k == 0, k == nk - 1,
                                    [w.b, ot.b], [p.b])
                        self.dve(lambda e, p=p, bi=bi, m_=m_, TS=TS: e.tensor_tensor(tm[bi].t[:, 0:TS], p.t[:, 0:TS], m_.t[:, bi, 0:TS], ALU.mult),
                                 [p.b, m_.b], [tm[bi].b])
                    self.dve(lambda e, TS=TS: e.tensor_tensor(tm[0].t[:, 0:TS], tm[0].t[:, 0:TS], tm[1].t[:, 0:TS], ALU.add), [tm[0].b, tm[1].b], [tm[0].b])
                    self.dve(lambda e, TS=TS: e.tensor_tensor(tm[2].t[:, 0:TS], tm[2].t[:, 0:TS], tm[3].t[:, 0:TS], ALU.add), [tm[2].b, tm[3].b], [tm[2].b])
                    self.dve(lambda e, yst=yst, f2=f2, TS=TS: e.tensor_tensor(yst.t[:, f2, 0:TS], tm[0].t[:, 0:TS], tm[2].t[:, 0:TS], ALU.add),
                             [tm[0].b, tm[2].b], [yst.b])
                S.dma("pool", self.yT[fp * 256:(fp + 1) * 256, tok0:tok0 + TS].rearrange("(a p) t -> p a t", p=128), yst.t[:, :, 0:TS],
                      reads=[yst.b], writes=[self.B("yT")], owner=yst.b)
        S.end_phase()

    def phaseM2(self, l, last):
        cfg = self.cfg
        S = self.S
        S.begin_phase()
        xsrc = self.xin if l == 0 else self.xres[(l - 1) % 2]
        xsrcB = self.B("xin") if l == 0 else self.B("xres%d" % ((l - 1) % 2))
        wout = S.sbuf("wout", [128, 16, D], BF16)
        S.dma("pool", wout.t[:], self.w_out[l], reads=[], writes=[wout.b], owner=wout.b)
        GW = [S.sbuf("GW%d" % i, [128, D], F32) for i in range(2)]
        for w_ in range(2):
            S.dma("sp", GW[w_].t[:], self.modv[l, w_, 2:3, :].partition_broadcast(128), reads=[self.B("modv")], writes=[GW[w_].b],
                  owner=GW[w_].b)
        yt = [S.sbuf("yt%d" % i, [128, 16, 512], BF16) for i in range(2)]
        xt = [S.sbuf("xt%d" % i, [128, D], F32) for i in range(2)]
        ft = [S.sbuf("ft%d" % i, [128, D], F32) for i in range(2)]
        junk = S.sbuf("junk", [128, 512], F32)
        sm = [S.sbuf("sm%d" % i, [128, 4], F32) for i in range(4)]
        pF = [S.psum("pF%d" % i, [128, D]) for i in range(2)]
        stl = cfg.stiles[1:] if last else cfg.stiles
        k_ = 0
        for si, (t0, n) in enumerate(stl):
            TS = n * 128
            tok0 = t0 * 128
            y_ = yt[si % 2]
            S.dma("sp", y_.t[:, :, 0:TS], self.yT[:, tok0:tok0 + TS].rearrange("(k p) t -> p k t", p=128), reads=[self.B("yT")],
                  writes=[y_.b], owner=y_.b)
            for tt in range(n):
                gtile = t0 + tt
                r0 = gtile * 128
                which = 1 if gtile < 2 else 0
                x_, f_, p = xt[k_ % 2], ft[k_ % 2], pF[k_ % 2]
                k_ += 1
                S.dma("sp", x_.t[:], xsrc[r0:r0 + 128, :], reads=[xsrcB], writes=[x_.b], owner=x_.b)
                for q in range(4):
                    for k in range(16):
                        self.mm(p.t[:, q * 512:(q + 1) * 512], y_.t[:, k, tt * 128:(tt + 1) * 128], wout.t[:, k, q * 512:(q + 1) * 512],
                                k == 0, k == 15, [y_.b, wout.b], [p.b])
                ss, r1, r2 = sm[0], sm[1], sm[2]
                for q in range(4):
                    self.act(lambda e, p=p, q=q: e.activation(out=junk.t[:], in_=p.t[:, q * 512:(q + 1) * 512], func=AF.Square,
                                                             accum_out=ss.t[:, q:q + 1]), [p.b], [junk.b, ss.b])
                self.dve(lambda e: e.reduce_sum(r1.t[:, 0:1], ss.t[:, 0:4], AX.X), [ss.b], [r1.b])
                self.dve(lambda e: e.tensor_scalar(r1.t[:, 1:2], r1.t[:, 0:1], 1.0 / D, EPS, ALU.mult, ALU.add), [r1.b], [r1.b])
                self.act(lambda e: e.activation(out=r2.t[:, 0:1], in_=r1.t[:, 1:2], func=AF.Sqrt), [r1.b], [r2.b])
                self.dve(lambda e: e.reciprocal(r2.t[:, 1:2], r2.t[:, 0:1]), [r2.b], [r2.b])
                self.dve(lambda e, p=p, f_=f_, which=which: e.scalar_tensor_tensor(out=f_.t[:], in0=p.t[:], scalar=r2.t[:, 1:2], in1=GW[which].t[:],
                                                                                  op0=ALU.mult, op1=ALU.mult), [p.b, r2.b, GW[which].b], [f_.b])
                self.dve(lambda e, f_=f_, x_=x_: e.tensor_tensor(f_.t[:], f_.t[:], x_.t[:], ALU.add), [f_.b, x_.b], [f_.b])
                if last:
                    S.dma("pool", self.out[r0 - cfg.C:r0 - cfg.C + 128, :], f_.t[:], reads=[f_.b], writes=[self.B("out")], owner=f_.b)
                else:
                    S.dma("pool", self.xres[l % 2][r0:r0 + 128, :], f_.t[:], reads=[f_.b], writes=[self.B("xres%d" % (l % 2))], owner=f_.b)
        S.end_phase()


def _rope_np(n_tok, dim):
    t = np.arange(n_tok, dtype=np.int32)
    row = (t // GRID_W).astype(np.float32)
    col = (t % GRID_W).astype(np.float32)
    quarter = dim // 4
    freqs = (np.float32(ROPE_THETA) ** (-np.arange(quarter, dtype=np.float32) / np.float32(quarter))).astype(np.float32)
    ang = np.concatenate([row[:, None] * freqs, col[:, None] * freqs], axis=-1).astype(np.float32)
    return np.cos(ang).astype(np.float32), np.sin(ang).astype(np.float32)


def _chunked(w, kc):
    return np.ascontiguousarray(w.reshape(kc, 128, w.shape[1]).transpose(1, 0, 2))


def na_bias_tables(cfg, rpb):
    rows = cfg.rows
    nq = rows // 2
    wh = min(NA_WH, rows)
    table = {}
    mats = []

    def build(qi, kj):
        M = np.full((8, 128, 128), -30000.0, np.float32)
        for ql in range(128):
            r = 2 * qi + ql // 64
            c = ql % 64
            r0 = min(max(r - wh // 2, 0), rows - wh)
            c0 = min(max(c - NA_WW // 2, 0), GRID_W - NA_WW)
            for ky in range(2):
                kr_ = 2 * kj + ky
                if not (r0 <= kr_ < r0 + wh):
                    continue
                kx = np.arange(c0, c0 + NA_WW)
                M[:, ky * 64 + kx, ql] = rpb[:, kr_ - r + NA_WH - 1, kx - c + NA_WW - 1]
        return M

    def keyset(qi):
        ks = set()
        for r in (2 * qi, 2 * qi + 1):
            r0 = min(max(r - wh // 2, 0), rows - wh)
            for kr_ in range(r0, r0 + wh):
                ks.add(kr_ // 2)
        return sorted(ks)

    interior = {}
    for qi in range(nq):
        edge = (2 * qi - wh // 2 < 0) or (2 * qi + 1 - wh // 2 > rows - wh)
        for kj in keyset(qi):
            if not edge:
                key = ("i", kj - qi)
                if key not in interior:
                    interior[key] = len(mats)
                    mats.append(build(qi, kj))
                table[(qi, kj)] = interior[key]
            else:
                table[(qi, kj)] = len(mats)
                mats.append(build(qi, kj))
    return mats, table, keyset


def host_prep(cfg, inp, b, shared=None):
    f = np.float32
    m = dict(shared) if shared is not None else host_prep_shared(cfg, inp)
    m["xin"] = np.ascontiguousarray(np.concatenate([inp["ctx"][b], inp["x"][b]], axis=0), dtype=f)
    cm = np.stack([np.asarray(inp["c"][b]), np.asarray(inp["c_ctx"])], 0).reshape(2, 16, 128)
    m["cmod"] = np.ascontiguousarray(cm.transpose(2, 0, 1), dtype=f)
    return m


def host_prep_shared(cfg, inp):
    L = cfg.DEPTH
    f = np.float32
    m = {}
    m["ada_w"] = np.ascontiguousarray(np.asarray(inp["ada_w"][:L]).reshape(L, 16, 128, 12, 512).transpose(0, 3, 2, 1, 4), dtype=f)
    m["ada_b"] = np.ascontiguousarray(np.asarray(inp["ada_b"][:L]).reshape(L, 1, 3 * D), dtype=f)
    m["norm_pre"] = np.ascontiguousarray(np.asarray(inp["norm_pre"][:L]).reshape(L, 1, D), dtype=f)
    m["norm_post"] = np.ascontiguousarray(np.asarray(inp["norm_post"][:L]).reshape(L, 1, D), dtype=f)
    cols = win_device_cols()
    w_in = np.asarray(inp["w_in"][:L])
    wd = np.zeros((L, D, NBLK * 512), f)
    ok = cols >= 0
    wd[:, :, ok] = w_in[:, :, cols[ok]]
    m["w_in"] = np.ascontiguousarray(wd.reshape(L, 16, 128, NBLK, 512).transpose(0, 3, 2, 1, 4))
    del wd
    di64 = _deint(64)
    di128 = _deint(128)
    w_uq = np.asarray(inp["w_uq"][:L])
    cn = np.concatenate([h * 192 + np.arange(128) for h in range(8)])
    cr = np.concatenate([h * 192 + 128 + di64 for h in range(8)])
    m["w_uq_n"] = np.stack([_chunked(w_uq[l][:, cn], 6) for l in range(L)]).astype(f)
    m["w_uq_r"] = np.stack([_chunked(w_uq[l][:, cr], 6) for l in range(L)]).astype(f)
    w_ukv = np.asarray(inp["w_ukv"][:L])
    ck = np.concatenate([h * 256 + np.arange(128) for h in range(8)])
    cv = np.concatenate([h * 256 + 128 + np.arange(128) for h in range(8)])
    m["w_ukv_k"] = np.stack([_chunked(w_ukv[l][:, ck], 4) for l in range(L)]).astype(f)
    m["w_ukv_v"] = np.stack([_chunked(w_ukv[l][:, cv], 4) for l in range(L)]).astype(f)
    wo = [np.concatenate([np.asarray(inp[k][l]) for k in ("w_o_ssm", "w_o_gqa", "w_o_na", "w_o_mla")], 0) for l in range(L)]
    m["w_o"] = np.stack([_chunked(w, 40) for w in wo]).astype(f)
    m["w_out"] = np.stack([_chunked(np.asarray(inp["w_out"][l]), 16) for l in range(L)]).astype(f)
    cw = np.asarray(inp["conv_w"][:L])
    m["conv_w"] = np.ascontiguousarray(cw.transpose(0, 2, 1).reshape(L, 24, 128, 5).transpose(0, 2, 1, 3), dtype=f)
    cb = np.asarray(inp["conv_b"][:L])
    m["conv_b"] = np.ascontiguousarray(cb.reshape(L, 24, 128).transpose(0, 2, 1), dtype=f)
    m["conv_b_row"] = np.ascontiguousarray(cb.reshape(L, 1, 3072), dtype=f)
    m["a_log"] = np.ascontiguousarray(np.asarray(inp["a_log"][:L]).reshape(L, 1, 64), dtype=f)
    m["dt_bias"] = np.ascontiguousarray(np.asarray(inp["dt_bias"][:L]).reshape(L, 1, 64), dtype=f)
    m["d_skip"] = np.ascontiguousarray(np.asarray(inp["d_skip"][:L]).reshape(L, 1, 32), dtype=f)
    m["ssm_norm"] = np.ascontiguousarray(np.asarray(inp["ssm_norm"][:L]).reshape(L, 1, D), dtype=f)
    m["gq_norm"] = np.ascontiguousarray(np.asarray(inp["gqa_q_norm"][:L])[:, di128].reshape(L, 1, 128), dtype=f)
    m["gk_norm"] = np.ascontiguousarray(np.asarray(inp["gqa_k_norm"][:L])[:, di128].reshape(L, 1, 128), dtype=f)
    m["mq_norm"] = np.ascontiguousarray(np.asarray(inp["mla_q_norm"][:L]).reshape(L, 1, 768), dtype=f)
    m["mkv_norm"] = np.ascontiguousarray(np.asarray(inp["mla_kv_norm"][:L]).reshape(L, 1, 512), dtype=f)
    nb = []
    for l in range(L):
        mats, _, _ = na_bias_tables(cfg, np.asarray(inp["na_rpb"][l]))
        nb.append(np.stack(mats, 1))
    m["na_bias"] = np.ascontiguousarray(np.stack(nb), dtype=f)
    cg, sg = _rope_np(cfg.S, 128)
    cm_, sm_ = _rope_np(cfg.S, 64)
    rg = np.zeros((cfg.T, 128), f)
    rg[:CTX, :64] = 1.0
    rg[CTX:, :64] = cg
    rg[CTX:, 64:] = sg
    rm = np.zeros((cfg.T, 64), f)
    rm[:CTX, :32] = 1.0
    rm[CTX:, :32] = cm_
    rm[CTX:, 32:] = sm_
    m["ropeG"] = rg
    m["ropeM"] = rm
    return m


def build(cfg, phases=("A",)):
    mk = MK(cfg)
    mk.declare()
    mk.consts()
    mk.phase0()
    for l in range(cfg.DEPTH):
        mk.phaseA(l)
        if "A" == cfg.stop_after:
            break
        mk.phaseS1(l)
        if "S1" == cfg.stop_after:
            break
        mk.phaseS2(l)
        if "S2" == cfg.stop_after:
            break
        last = l == cfg.DEPTH - 1
        mk.phaseT(l, last)
        if "T" == cfg.stop_after:
            break
        mk.phaseM1(l, last)
        mk.phaseM2(l, last)
    mk.S.close()
    return mk


N_CORES_USED = 4


def kernel(**inputs):
    cfg = Cfg(S=8192, DEPTH=4)
    inp = {k: np.asarray(v) for k, v in inputs.items()}
    mk = build(cfg)
    shared = host_prep_shared(cfg, inp)
    in_maps = [host_prep(cfg, inp, b, shared) for b in range(N_CORES_USED)]
    res = run_bass_kernel_spmd(mk.nc, in_maps, core_ids=list(range(N_CORES_USED)))
    out = np.stack([np.asarray(res.results[b]["out"]) for b in range(N_CORES_USED)], 0)
    return out.astype(np.float32)
```

```python
import contextlib
import math
import numpy as np
import concourse.bass as bass
import concourse.mybir as mybir
from concourse.bass_utils import run_bass_kernel_spmd

F32 = mybir.dt.float32
BF16 = mybir.dt.bfloat16
AF = mybir.ActivationFunctionType
ALU = mybir.AluOpType
AX = mybir.AxisListType

ENGS = ("sp", "act", "dve", "pool", "pe")

D = 2048
CTX = 256
GRID_W = 64
EPS = 1e-6
ROPE_THETA = 10000.0
SSM_HEADS = 32
SSM_P = 64
SSM_G = 4
SSM_N = 128
SSM_CONV = 5
NA_WH = 8
NA_WW = 16
IN_SIZES = (2048, 3072, 64, 1024, 512, 512, 1024, 1024, 1024, 1024, 1024, 768, 512, 64, 1024, 8192)
IN_OFF = np.concatenate([[0], np.cumsum(IN_SIZES)]).astype(np.int64)
(O_Z, O_XBC, O_DT, O_GQ, O_GK, O_GV, O_GG, O_NQ, O_NK, O_NV, O_NG, O_MQA, O_MKVA, O_MKR, O_MG, O_MIX) = [int(v) for v in IN_OFF[:-1]]
NBLK = 45


class Buf:
    __slots__ = ("w", "r", "dsem", "name")

    def __init__(self, name=""):
        self.w = {}
        self.r = {}
        self.dsem = None
        self.name = name


class Tile:
    __slots__ = ("t", "b")

    def __init__(self, t, name=""):
        self.t = t
        self.b = Buf(name)


class Sched:
    def __init__(self, nc, n_dma_sems=84):
        self.nc = nc
        self.es = contextlib.ExitStack()
        self.esem = {}
        self.ecount = {}
        self.seen = {}
        self.q = {}
        for e in ENGS:
            self.esem[e] = self.es.enter_context(nc.semaphore("es_" + e))
            self.ecount[e] = 0
            self.seen[e] = {}
            self.q[e] = []
        self.dma_sems = [self.es.enter_context(nc.semaphore("ds%d" % i)) for i in range(n_dma_sems)]
        self.free_dsems = list(self.dma_sems)
        self.scount = {id(s): 0 for s in self.dma_sems}
        self.semobj = {id(s): s for s in self.dma_sems}
        for e in ENGS:
            self.semobj[id(self.esem[e])] = self.esem[e]
        self.is_dma = set(id(s) for s in self.dma_sems)
        self.phase_bufs = []
        self.phase_es = None
        self.n_instr = 0
        self.uid = 0

    def begin_phase(self):
        self.phase_es = contextlib.ExitStack()
        self.phase_bufs = []

    def sbuf(self, name, shape, dtype):
        self.uid += 1
        nm = "%s_%d" % (name, self.uid)
        t = self.phase_es.enter_context(self.nc.sbuf_tensor(nm, list(shape), dtype))
        return Tile(t, nm)

    def psum(self, name, shape, dtype=F32):
        self.uid += 1
        nm = "%s_%d" % (name, self.uid)
        t = self.phase_es.enter_context(self.nc.psum_tensor(nm, list(shape), dtype))
        return Tile(t, nm)

    def _dsem(self, buf):
        if buf.dsem is None:
            if not self.free_dsems:
                raise RuntimeError("out of DMA semaphores")
            buf.dsem = self.free_dsems.pop()
            self.phase_bufs.append(buf)
        return buf.dsem

    def _events(self, eng, reads, writes, self_sync):
        ev = {}
        own = id(self.esem[eng])
        for b in reads:
            for k, v in b.w.items():
                if ev.get(k, 0) < v:
                    ev[k] = v
        for b in writes:
            for k, v in b.w.items():
                if ev.get(k, 0) < v:
                    ev[k] = v
            for k, v in b.r.items():
                if k == own:
                    continue
                if ev.get(k, 0) < v:
                    ev[k] = v
        if not self_sync and own in ev:
            del ev[own]
        waits = []
        seen = self.seen[eng]
        for k, v in ev.items():
            if k in self.is_dma:
                v = self.scount[k]
            if seen.get(k, 0) < v:
                seen[k] = v
                waits.append((self.semobj[k], v))
        return waits

    def op(self, eng, fn, reads=(), writes=(), self_sync=None):
        if self_sync is None:
            self_sync = eng != "pe"
        waits = self._events(eng, reads, writes, self_sync)
        self.ecount[eng] += 1
        c = self.ecount[eng]
        own = self.esem[eng]
        self.q[eng].append((waits, fn, own, 1))
        k = id(own)
        for b in reads:
            b.r[k] = c
        for b in writes:
            b.w = {k: c}
            b.r = {}
        self.n_instr += 1

    def dma(self, q, out, in_, reads=(), writes=(), owner=None):
        waits = self._events(q, reads, writes, True)
        sem = self._dsem(owner)
        k = id(sem)
        self.scount[k] += 16
        c = self.scount[k]
        self.q[q].append((waits, (lambda e, o=out, i=in_: e.dma_start(out=o, in_=i)), sem, 16))
        for b in reads:
            b.r[k] = c
        for b in writes:
            b.w = {k: c}
            b.r = {}
        self.n_instr += 1

    def barrier(self):
        tot = {}
        for e in ENGS:
            tot[id(self.esem[e])] = self.ecount[e]
        for k, v in self.scount.items():
            tot[k] = v
        for e in ENGS:
            seen = self.seen[e]
            waits = []
            for k, v in tot.items():
                if v > 0 and seen.get(k, 0) < v and k != id(self.esem[e]):
                    seen[k] = v
                    waits.append((self.semobj[k], v))
            if waits:
                self.q[e].append((waits, None, None, 0))

    def end_phase(self):
        self.barrier()
        nc = self.nc
        with nc.Block() as block:
            decos = {"sp": block.sync, "act": block.scalar, "dve": block.vector,
                     "pool": block.gpsimd, "pe": block.tensor}
            for name in ENGS:
                items = self.q[name]

                def body(e, items=items):
                    for waits, fn, sem, inc in items:
                        for ws, wv in waits:
                            e.wait_ge(ws, wv)
                        if fn is not None:
                            fn(e).then_inc(sem, inc)

                if items:
                    decos[name](body)
                self.q[name] = []
        for b in self.phase_bufs:
            self.free_dsems.append(b.dsem)
            b.dsem = None
        self.phase_bufs = []
        self.phase_es.close()
        self.phase_es = None

    def close(self):
        self.es.close()


class Cfg:
    def __init__(self, S=8192, DEPTH=4, debug=(), stop_after=None):
        self.S = S
        self.C = CTX
        self.T = S + CTX
        self.NT = self.T // 128
        self.DEPTH = DEPTH
        self.rows = S // GRID_W
        self.TP = self.T + 8
        self.debug = set(debug)
        self.stop_after = stop_after
        st = [(0, 2)]
        t = 2
        while t < self.NT:
            n = min(4, self.NT - t)
            st.append((t, n))
            t += n
        self.stiles = st

    def padcol(self, tok):
        return tok + 2 if tok < self.C else tok + 6


def _deint(n):
    return np.concatenate([np.arange(0, n, 2), np.arange(1, n, 2)])


def win_device_cols():
    cols = []

    def add(a):
        cols.append(np.asarray(a, dtype=np.int64))

    add(O_Z + np.arange(2048))
    add(O_GG + np.arange(1024))
    add(O_NG + np.arange(1024))
    add(O_MG + np.arange(1024))
    add(O_MIX + np.arange(8192))
    add(O_XBC + np.arange(3072))
    add(O_NQ + np.arange(1024))
    add(O_NK + np.arange(1024))
    add(O_GV + np.arange(512))
    add(O_NV + np.arange(1024))
    di = _deint(128)
    add(np.concatenate([O_GQ + h * 128 + di for h in range(8)]))
    add(np.concatenate([O_GK + h * 128 + di for h in range(4)]))
    add(O_MQA + np.arange(512))
    add(np.concatenate([O_MQA + 512 + np.arange(256), O_MKR + _deint(64), O_DT + np.arange(64),
                        -np.ones(128, np.int64)]))
    add(O_MKVA + np.arange(512))
    c = np.concatenate(cols)
    assert c.shape[0] == NBLK * 512
    return c


BLK_KIND = (["z"] * 4 + ["gg"] * 2 + ["ng"] * 2 + ["mg"] * 2 + ["mix"] * 16 + ["xbc"] * 6 + ["nq"] * 2 + ["nk"] * 2
            + ["gv"] + ["nv"] * 2 + ["gq"] * 2 + ["gk"] + ["mqa0", "mqa1", "mkva"])
BLK_FIRST = {}
for _i, _k in enumerate(BLK_KIND):
    BLK_FIRST.setdefault(_k, _i)
FM_KINDS = ("gg", "ng", "mg", "mix", "xbc", "nq", "nk")


class MK:
    def __init__(self, cfg):
        self.cfg = cfg
        self.nc = bass.Bass("TRN2", target_bir_lowering=False)
        self.S = Sched(self.nc)
        self.dbufs = {}
        self.outputs = []

    def din(self, name, shape, dtype=F32):
        t = self.nc.dram_tensor(name, list(shape), dtype, kind="ExternalInput")
        self.dbufs[name] = Buf(name)
        return t.ap()

    def dscr(self, name, shape, dtype):
        kind = "ExternalOutput" if name in self.cfg.debug else "Internal"
        t = self.nc.dram_tensor(name, list(shape), dtype, kind=kind)
        if kind == "ExternalOutput":
            self.outputs.append(name)
        self.dbufs[name] = Buf(name)
        return t.ap()

    def B(self, name):
        return self.dbufs[name]

    def dump(self, name, ap, shape, dtype, buf):
        if ("dump:" + name) not in self.cfg.debug or name in self.dbufs:
            return
        t = self.nc.dram_tensor(name, list(shape), dtype, kind="ExternalOutput")
        self.outputs.append(name)
        self.dbufs[name] = Buf(name)
        self.S.dma("sp", t.ap(), ap, reads=[buf], writes=[self.dbufs[name]], owner=buf)

    def act(self, fn, reads, writes):
        self.S.op("act", fn, reads, writes)

    def dve(self, fn, reads, writes):
        self.S.op("dve", fn, reads, writes)

    def pool(self, fn, reads, writes):
        self.S.op("pool", fn, reads, writes)

    def pe(self, fn, reads, writes):
        self.S.op("pe", fn, reads, writes)

    def mm(self, out, lhsT, rhs, start, stop, reads, writes):
        self.S.op("pe", lambda e: e.matmul(out, lhsT, rhs, start=start, stop=stop), reads, writes)

    def tr(self, out, in_, ident, reads, writes):
        self.S.op("pe", lambda e: e.transpose(out, in_, ident), reads, writes)

    def declare(self):
        cfg = self.cfg
        L = cfg.DEPTH
        T = cfg.T
        self.xin = self.din("xin", [T, D])
        self.cmod = self.din("cmod", [128, 2, 16])
        self.ada_w = self.din("ada_w", [L, 12, 128, 16, 512])
        self.ada_b = self.din("ada_b", [L, 1, 3 * D])
        self.norm_pre = self.din("norm_pre", [L, 1, D])
        self.norm_post = self.din("norm_post", [L, 1, D])
        self.w_in = self.din("w_in", [L, NBLK, 128, 16, 512])
        self.w_uq_n = self.din("w_uq_n", [L, 128, 6, 1024])
        self.w_uq_r = self.din("w_uq_r", [L, 128, 6, 512])
        self.w_ukv_k = self.din("w_ukv_k", [L, 128, 4, 1024])
        self.w_ukv_v = self.din("w_ukv_v", [L, 128, 4, 1024])
        self.w_o = self.din("w_o", [L, 128, 40, D])
        self.w_out = self.din("w_out", [L, 128, 16, D])
        self.conv_w = self.din("conv_w", [L, 128, 24, 5])
        self.conv_b = self.din("conv_b", [L, 128, 24])
        self.conv_b_row = self.din("conv_b_row", [L, 1, 3072])
        self.a_log = self.din("a_log", [L, 1, 64])
        self.dt_bias = self.din("dt_bias", [L, 1, 64])
        self.d_skip = self.din("d_skip", [L, 1, 32])
        self.ssm_norm = self.din("ssm_norm", [L, 1, D])
        self.gq_norm = self.din("gq_norm", [L, 1, 128])
        self.gk_norm = self.din("gk_norm", [L, 1, 128])
        self.mq_norm = self.din("mq_norm", [L, 1, 768])
        self.mkv_norm = self.din("mkv_norm", [L, 1, 512])
        _mats, self.na_table, self.na_keyset = na_bias_tables(cfg, np.zeros((8, 15, 31), np.float32))
        self.nbm = len(_mats)
        self.na_bias = self.din("na_bias", [L, 8, self.nbm, 128, 128])
        self.ropeG = self.din("ropeG", [T, 128])
        self.ropeM = self.din("ropeM", [T, 64])
        self.out = self.nc.dram_tensor("out", [cfg.S, D], F32, kind="ExternalOutput").ap()
        self.dbufs["out"] = Buf("out")
        self.w_in_bf = [self.dscr("w_in_bf%d" % i, [NBLK, 128, 16, 512], BF16) for i in range(L)]
        self.w_o_bf = self.dscr("w_o_bf", [L, 128, 40, D], BF16)
        self.modv = self.dscr("modv", [L, 2, 3, D], F32)
        self.xres = [self.dscr("xres%d" % i, [T, D], F32) for i in range(2)]
        self.z_s = self.dscr("z_s", [T, D], BF16)
        self.gT = {k: self.dscr("gT_" + k, [1024, T], BF16) for k in ("gg", "ng", "mg")}
        self.mixT = self.dscr("mixT", [8192, T], BF16)
        self.xbcT = self.dscr("xbcT", [3072, cfg.TP], BF16)
        self.QT_n = self.dscr("QT_n", [1024, T], BF16)
        self.KT_n = self.dscr("KT_n", [1024, T], BF16)
        self.V_g = self.dscr("V_g", [T, 512], BF16)
        self.V_n = self.dscr("V_n", [T, 1024], BF16)
        self.QT_g = self.dscr("QT_g", [1024, T], BF16)
        self.KT_g = self.dscr("KT_g", [512, T], BF16)
        self.dtr = self.dscr("dtr", [T, 64], F32)
        self.QT_mn = self.dscr("QT_mn", [1024, T], BF16)
        self.QT_mr = self.dscr("QT_mr", [512, T], BF16)
        self.KT_mn = self.dscr("KT_mn", [1024, T], BF16)
        self.KT_mr = self.dscr("KT_mr", [64, T], BF16)
        self.V_m = self.dscr("V_m", [T, 1024], BF16)
        self.xc = self.dscr("xc", [T, D], BF16)
        self.Bc = self.dscr("Bc", [T, 512], BF16)
        self.BT = self.dscr("BT", [512, T], BF16)
        self.CT = self.dscr("CT", [512, T], BF16)
        self.y_f = self.dscr("y_f", [T, D], F32)
        self.yT = self.dscr("yT", [2048, T], BF16)
        self.oT = {"ssm": self.dscr("oT_ssm", [2048, T], BF16), "gqa": self.dscr("oT_gqa", [1024, T], BF16),
                   "na": self.dscr("oT_na", [1024, T], BF16), "mla": self.dscr("oT_mla", [1024, T], BF16)}

    def consts(self):
        nc = self.nc
        es = self.S.es
        S = self.S

        def g(name, shape, dt):
            return Tile(es.enter_context(nc.sbuf_tensor(name, list(shape), dt)), name)

        self.ident_f = g("ident_f", [128, 128], F32)
        self.ident_b = g("ident_b", [128, 128], BF16)
        self.ones_b = g("ones_b", [128, 128], BF16)
        self.ones_f = g("ones_f", [128, 128], F32)
        S.begin_phase()
        i_f, i_b, o_b, o_f = self.ident_f, self.ident_b, self.ones_b, self.ones_f
        self.pool(lambda e: e.memset(i_f.t[:], 0.0), [], [i_f.b])
        self.pool(lambda e: e.affine_select(out=i_f.t[:], in_=i_f.t[:], pattern=[[-1, 128]], compare_op=ALU.not_equal,
                                            fill=1.0, base=0, channel_multiplier=1), [i_f.b], [i_f.b])
        self.dve(lambda e: e.tensor_copy(i_b.t[:], i_f.t[:]), [i_f.b], [i_b.b])
        self.pool(lambda e: e.memset(o_b.t[:], 1.0), [], [o_b.b])
        self.pool(lambda e: e.memset(o_f.t[:], 1.0), [], [o_f.b])
        S.end_phase()

    def phase0(self):
        cfg = self.cfg
        S = self.S
        L = cfg.DEPTH
        S.begin_phase()
        self.B_win = [Buf("win%d" % l) for l in range(L)]
        self.B_wo = [Buf("wo%d" % l) for l in range(L)]
        p0 = getattr(cfg, "p0", ("cast", "pad", "mod"))
        for l in range(L if "cast" in p0 else 0):
            for j in range(NBLK):
                S.dma("pool", self.w_in_bf[l][j], self.w_in[l, j], reads=[], writes=[self.B_win[l]], owner=self.B_win[l])
            for k in range(0, 40, 8):
                S.dma("pool", self.w_o_bf[l, :, k:k + 8, :], self.w_o[l, :, k:k + 8, :], reads=[], writes=[self.B_wo[l]],
                      owner=self.B_wo[l])
        zt = S.sbuf("zt", [128, 24, 4], BF16)
        self.dve(lambda e: e.memset(zt.t[:], 0.0), [], [zt.b])
        xv = self.xbcT.rearrange("(k p) t -> p k t", p=128)
        C, TP = cfg.C, cfg.TP
        if "pad" in p0:
            S.dma("sp", xv[:, :, 0:2], zt.t[:, :, 0:2], reads=[zt.b], writes=[self.B("xbcT")], owner=zt.b)
            S.dma("sp", xv[:, :, C + 2:C + 6], zt.t[:, :, 0:4], reads=[zt.b], writes=[self.B("xbcT")], owner=zt.b)
            S.dma("sp", xv[:, :, TP - 2:TP], zt.t[:, :, 0:2], reads=[zt.b], writes=[self.B("xbcT")], owner=zt.b)
        if "mod" not in p0:
            S.end_phase()
            return
        cm = S.sbuf("cm", [128, 2, 16], F32)
        S.dma("sp", cm.t[:], self.cmod, reads=[], writes=[cm.b], owner=cm.b)
        cs = S.sbuf("cs", [128, 2, 16], F32)
        self.act(lambda e: e.activation(out=cs.t[:], in_=cm.t[:], func=AF.Silu), [cm.b], [cs.b])
        csr = S.sbuf("csr", [128, 2, 16, 128], BF16)
        for w in range(2):
            self.dve(lambda e, w=w: e.tensor_copy(csr.t[:, w], cs.t[:, w].unsqueeze(2).to_broadcast([128, 16, 128])),
                     [cs.b], [csr.b])
        wts = [S.sbuf("adaw%d" % i, [128, 16, 512], BF16) for i in range(2)]
        ps = [S.psum("ps0_%d" % i, [128, 512]) for i in range(4)]
        modsb = [S.sbuf("modsb%d" % w, [128, 3 * D], F32) for w in range(2)]
        adab = S.sbuf("adab", [128, 3 * D], F32)
        npre = S.sbuf("npre", [128, D], F32)
        npost = S.sbuf("npost", [128, D], F32)
        res = [S.sbuf("modres%d" % i, [1, D], F32) for i in range(3)]
        pi = 0
        for l in range(L):
            S.dma("sp", adab.t[:], self.ada_b[l].partition_broadcast(128), reads=[], writes=[adab.b], owner=adab.b)
            S.dma("sp", npre.t[:], self.norm_pre[l].partition_broadcast(128), reads=[], writes=[npre.b], owner=npre.b)
            S.dma("sp", npost.t[:], self.norm_post[l].partition_broadcast(128), reads=[], writes=[npost.b], owner=npost.b)
            for j in range(12):
                wt = wts[j % 2]
                S.dma("pool", wt.t[:], self.ada_w[l, j], reads=[], writes=[wt.b], owner=wt.b)
                for w in range(2):
                    p = ps[pi % 4]
                    pi += 1
                    for k in range(16):
                        self.mm(p.t[:], csr.t[:, w, k, :], wt.t[:, k, :], k == 0, k == 15, [csr.b, wt.b], [p.b])
                    self.dve(lambda e, p=p, w=w, j=j: e.tensor_tensor(modsb[w].t[:, j * 512:(j + 1) * 512], p.t[:],
                                                                      adab.t[:, j * 512:(j + 1) * 512], ALU.add),
                             [p.b, adab.b], [modsb[w].b])
            for w in range(2):
                m = modsb[w]
                self.dve(lambda e, m=m: e.scalar_tensor_tensor(out=res[0].t[:], in0=m.t[0:1, D:2 * D], scalar=1.0,
                                                               in1=npre.t[0:1, :], op0=ALU.add, op1=ALU.mult),
                         [m.b, npre.b], [res[0].b])
                self.act(lambda e, m=m: e.activation(out=res[1].t[:], in_=m.t[0:1, 0:D], func=AF.Copy), [m.b], [res[1].b])
                self.dve(lambda e, m=m: e.tensor_tensor(res[2].t[:], m.t[0:1, 2 * D:3 * D], npost.t[0:1, :], ALU.mult),
                         [m.b, npost.b], [res[2].b])
                for i in range(3):
                    S.dma("sp", self.modv[l, w, i:i + 1, :], res[i].t[:], reads=[res[i].b], writes=[self.B("modv")],
                          owner=res[i].b)
        S.end_phase()

    def phaseA(self, l):
        cfg = self.cfg
        S = self.S
        T = cfg.T
        S.begin_phase()
        xsrc = self.xin if l == 0 else self.xres[(l - 1) % 2]
        xsrcB = self.B("xin") if l == 0 else self.B("xres%d" % ((l - 1) % 2))
        ident = self.ident_b
        Apre = S.sbuf("Apre", [128, D], F32)
        shf = S.sbuf("shf", [128, D], F32)
        gqn = S.sbuf("gqn", [128, 128], F32)
        gkn = S.sbuf("gkn", [128, 128], F32)
        mqn = S.sbuf("mqn", [128, 768], F32)
        mkvn = S.sbuf("mkvn", [128, 512], F32)
        for tl, src in ((gqn, self.gq_norm), (gkn, self.gk_norm), (mqn, self.mq_norm), (mkvn, self.mkv_norm)):
            S.dma("sp", tl.t[:], src[l].partition_broadcast(128), reads=[], writes=[tl.b], owner=tl.b)
        wuq_n = S.sbuf("wuq_n", [128, 6, 1024], BF16)
        wuq_r = S.sbuf("wuq_r", [128, 6, 512], BF16)
        wukv_k = S.sbuf("wukv_k", [128, 4, 1024], BF16)
        wukv_v = S.sbuf("wukv_v", [128, 4, 1024], BF16)
        for tl, src in ((wuq_n, self.w_uq_n), (wuq_r, self.w_uq_r), (wukv_k, self.w_ukv_k), (wukv_v, self.w_ukv_v)):
            S.dma("pool", tl.t[:], src[l], reads=[], writes=[tl.b], owner=tl.b)
        hT = S.sbuf("hT", [128, 16, 512], BF16)
        Wt = [S.sbuf("Wt%d" % i, [128, 16, 512], BF16) for i in range(2)]
        xt = [S.sbuf("xt%d" % i, [128, D], F32) for i in range(2)]
        tmpf = S.sbuf("tmpf", [128, D], F32)
        hb = S.sbuf("hb", [128, D], BF16)
        stg = [S.sbuf("stg%d" % i, [128, 4, 512], BF16) for i in range(4)]
        rg = S.sbuf("rg", [128, 4, 128], F32)
        rm = S.sbuf("rm", [128, 4, 64], F32)
        qaT = S.sbuf("qaT", [128, 6, 512], BF16)
        kvaT = S.sbuf("kvaT", [128, 4, 512], BF16)
        dts = S.sbuf("dts", [128, 4, 64], F32)
        krT = S.sbuf("krT", [64, 512], BF16)
        sqt = S.sbuf("sqt", [128, 512], F32)
        qn = S.sbuf("qn", [128, 4, 128], F32)
        rt = [S.sbuf("rt%d" % i, [128, 4, 64], F32) for i in range(4)]
        qr = S.sbuf("qr", [128, 4, 128], BF16)
        qan = S.sbuf("qan", [128, 768], BF16)
        kr = S.sbuf("kr", [128, 64], BF16)
        sm = [S.sbuf("sm%d" % i, [128, 8], F32) for i in range(6)]
        pm = [S.psum("pm%d" % i, [128, 512]) for i in range(6)]
        pt = [S.psum("pt%d" % i, [128, 1024], BF16) for i in range(2)]
        st = {"pm": 0, "pt": 0, "stg": 0, "g": 0}

        def nb():
            p = pm[st["pm"] % 6]
            st["pm"] += 1
            return p

        def npt():
            p = pt[st["pt"] % 2]
            st["pt"] += 1
            return p

        def nstg():
            s_ = stg[st["stg"] % 4]
            st["stg"] += 1
            return s_

        slot = {}

        def ensure_loaded(si, j):
            if (si, j) in slot or si >= len(cfg.stiles):
                return
            w = Wt[st["g"] % 2]
            st["g"] += 1
            slot[(si, j)] = w
            S.dma("sp", w.t[:], self.w_in_bf[l][j], reads=[self.B_win[l]], writes=[w.b], owner=w.b)

        def rstd_from(ss_ap, n, rs, r1, r2, reads):
            self.dve(lambda e: e.tensor_scalar(r1[0], ss_ap, 1.0 / n, EPS, ALU.mult, ALU.add), reads, [r1[1]])
            self.act(lambda e: e.activation(out=r2[0], in_=r1[0], func=AF.Sqrt), [r1[1]], [r2[1]])
            self.dve(lambda e: e.reciprocal(rs[0], r2[0]), [r2[1]], [rs[1]])

        def rope(dst, src, reads, cos, sin, tabB, nh, hd, dstB):
            x0 = src[:, :, 0:hd]
            x1 = src[:, :, hd:2 * hd]
            cb = cos.unsqueeze(1).to_broadcast([128, nh, hd])
            sb = sin.unsqueeze(1).to_broadcast([128, nh, hd])
            r = [t_.t[:, 0:nh, 0:hd] if nh <= 4 else None for t_ in rt]
            for i, (a, b_) in enumerate(((x0, cb), (x1, sb), (x0, sb), (x1, cb))):
                self.dve(lambda e, i=i, a=a, b_=b_: e.tensor_tensor(r[i], a, b_, ALU.mult), reads + [tabB], [rt[i].b])
            self.dve(lambda e: e.tensor_tensor(dst[:, :, 0:hd], r[0], r[1], ALU.subtract), [rt[0].b, rt[1].b], [dstB])
            self.dve(lambda e: e.tensor_tensor(dst[:, :, hd:2 * hd], r[2], r[3], ALU.add), [rt[2].b, rt[3].b], [dstB])

        cur_which = [None]
        for si, (t0, n) in enumerate(cfg.stiles):
            TS = n * 128
            tok0 = t0 * 128
            which = 1 if si == 0 else 0
            if cur_which[0] != which:
                cur_which[0] = which
                S.dma("sp", Apre.t[:], self.modv[l, which, 0:1, :].partition_broadcast(128), reads=[self.B("modv")],
                      writes=[Apre.b], owner=Apre.b)
                S.dma("sp", shf.t[:], self.modv[l, which, 1:2, :].partition_broadcast(128), reads=[self.B("modv")],
                      writes=[shf.b], owner=shf.b)
            ensure_loaded(si, 0)
            S.dma("sp", rg.t[:, 0:n, :], self.ropeG[tok0:tok0 + TS, :].rearrange("(a p) c -> p a c", p=128), reads=[],
                  writes=[rg.b], owner=rg.b)
            S.dma("sp", rm.t[:, 0:n, :], self.ropeM[tok0:tok0 + TS, :].rearrange("(a p) c -> p a c", p=128), reads=[],
                  writes=[rm.b], owner=rm.b)
            for tt in range(n):
                gt = t0 + tt
                x_ = xt[gt % 2]
                S.dma("sp", x_.t[:], xsrc[gt * 128:(gt + 1) * 128, :], reads=[xsrcB], writes=[x_.b], owner=x_.b)
                ss, r1, r2, rs = sm[0], sm[1], sm[2], sm[3]
                self.act(lambda e, x_=x_: e.activation(out=tmpf.t[:], in_=x_.t[:], func=AF.Square, accum_out=ss.t[:, 0:1]),
                         [x_.b], [tmpf.b, ss.b])
                rstd_from(ss.t[:, 0:1], D, (rs.t[:, 0:1], rs.b), (r1.t[:, 0:1], r1.b), (r2.t[:, 0:1], r2.b), [ss.b])
                self.dve(lambda e, x_=x_: e.scalar_tensor_tensor(out=tmpf.t[:], in0=x_.t[:], scalar=rs.t[:, 0:1], in1=Apre.t[:],
                                                                 op0=ALU.mult, op1=ALU.mult), [x_.b, rs.b, Apre.b], [tmpf.b])
                self.dve(lambda e: e.tensor_tensor(hb.t[:], tmpf.t[:], shf.t[:], ALU.add), [tmpf.b, shf.b], [hb.b])
                for half in range(2):
                    p = npt()
                    for c in range(8):
                        cc = half * 8 + c
                        self.tr(p.t[:, c * 128:(c + 1) * 128], hb.t[:, cc * 128:(cc + 1) * 128], ident.t[:], [hb.b, ident.b], [p.b])
                    self.act(lambda e, p=p, half=half, tt=tt: e.activation(
                        out=hT.t[:, half * 8:(half + 1) * 8, tt * 128:(tt + 1) * 128],
                        in_=p.t[:].rearrange("p (c t) -> p c t", c=8), func=AF.Copy), [p.b], [hT.b])

            def tm_dst(ap2d, c0, ncols):
                return ap2d[tok0:tok0 + TS, c0:c0 + ncols].rearrange("(a p) c -> p a c", p=128)

            def fm_dst(ap2d, r0, nrows, col0):
                return ap2d[r0:r0 + nrows, col0:col0 + TS].rearrange("(a p) t -> p a t", p=128)

            def tm_matmul(w, tt):
                p = nb()
                for k in range(16):
                    self.mm(p.t[:], hT.t[:, k, tt * 128:(tt + 1) * 128], w.t[:, k, :], k == 0, k == 15, [hT.b, w.b], [p.b])
                return p

            def fm_matmul(w, cb):
                p = nb()
                for k in range(16):
                    self.mm(p.t[:, 0:TS], w.t[:, k, cb * 128:(cb + 1) * 128], hT.t[:, k, 0:TS], k == 0, k == 15,
                            [hT.b, w.b], [p.b])
                return p

            def simple_tm(j, w, func, dst2d, dstB, c0):
                sg = nstg()
                for tt in range(n):
                    p = tm_matmul(w, tt)
                    self.act(lambda e, p=p, tt=tt: e.activation(out=sg.t[:, tt, :], in_=p.t[:], func=func), [p.b], [sg.b])
                S.dma("pool", tm_dst(dst2d, c0, 512), sg.t[:, 0:n, :], reads=[sg.b], writes=[dstB], owner=sg.b)

            def simple_fm(j, w, func, dst2d, dstB, r0, col0, scale=1.0):
                sg = nstg()
                for cb in range(4):
                    p = fm_matmul(w, cb)
                    self.act(lambda e, p=p, cb=cb, TS=TS: e.activation(out=sg.t[:, cb, 0:TS], in_=p.t[:, 0:TS], func=func, scale=scale),
                             [p.b], [sg.b])
                S.dma("pool", fm_dst(dst2d, r0, 512, col0), sg.t[:, :, 0:TS], reads=[sg.b], writes=[dstB], owner=sg.b)

            def qk_block(j, w, gain, dst2d, dstB, h0):
                sg = nstg()
                for tt in range(n):
                    p = tm_matmul(w, tt)
                    ss4, r1, r2, rs4 = sm[0], sm[1], sm[2], sm[3]
                    self.act(lambda e, p=p: e.activation(out=sqt.t[:], in_=p.t[:], func=AF.Square), [p.b], [sqt.b])
                    self.dve(lambda e: e.reduce_sum(ss4.t[:, 0:4], sqt.t[:].rearrange("p (h d) -> p h d", h=4), AX.X),
                             [sqt.b], [ss4.b])
                    rstd_from(ss4.t[:, 0:4], 128, (rs4.t[:, 0:4], rs4.b), (r1.t[:, 0:4], r1.b), (r2.t[:, 0:4], r2.b), [ss4.b])
                    self.dve(lambda e, p=p: e.tensor_tensor(qn.t[:], p.t[:].rearrange("p (h d) -> p h d", h=4),
                                                            rs4.t[:, 0:4].unsqueeze(2).to_broadcast([128, 4, 128]), ALU.mult),
                             [p.b, rs4.b], [qn.b])
                    self.dve(lambda e: e.tensor_tensor(qn.t[:], qn.t[:], gain.t[:].unsqueeze(1).to_broadcast([128, 4, 128]),
                                                       ALU.mult), [qn.b, gain.b], [qn.b])
                    rope(qr.t, qn.t, [qn.b], rg.t[:, tt, 0:64], rg.t[:, tt, 64:128], rg.b, 4, 64, qr.b)
                    pp = npt()
                    for h in range(4):
                        self.tr(pp.t[:, h * 128:(h + 1) * 128], qr.t[:, h, :], ident.t[:], [qr.b, ident.b], [pp.b])
                    self.act(lambda e, pp=pp, tt=tt: e.activation(out=sg.t[:, :, tt * 128:(tt + 1) * 128],
                                                                 in_=pp.t[:, 0:512].rearrange("p (h t) -> p h t", h=4),
                                                                 func=AF.Copy), [pp.b], [sg.b])
                S.dma("pool", fm_dst(dst2d, h0 * 128, 512, tok0), sg.t[:, :, 0:TS], reads=[sg.b], writes=[dstB], owner=sg.b)

            def mqa_pair(w0, w1):
                for tt in range(n):
                    p0 = tm_matmul(w0, tt)
                    p1 = tm_matmul(w1, tt)
                    ssa, ssb, r1, r2, rs = sm[0], sm[4], sm[1], sm[2], sm[3]
                    self.act(lambda e, p0=p0: e.activation(out=sqt.t[:], in_=p0.t[:], func=AF.Square, accum_out=ssa.t[:, 0:1]),
                             [p0.b], [sqt.b, ssa.b])
                    self.act(lambda e, p1=p1: e.activation(out=sqt.t[:, 0:256], in_=p1.t[:, 0:256], func=AF.Square,
                                                           accum_out=ssb.t[:, 0:1]), [p1.b], [sqt.b, ssb.b])
                    self.dve(lambda e: e.tensor_tensor(ssa.t[:, 0:1], ssa.t[:, 0:1], ssb.t[:, 0:1], ALU.add), [ssa.b, ssb.b], [ssa.b])
                    rstd_from(ssa.t[:, 0:1], 768, (rs.t[:, 0:1], rs.b), (r1.t[:, 0:1], r1.b), (r2.t[:, 0:1], r2.b), [ssa.b])
                    self.dve(lambda e, p0=p0: e.scalar_tensor_tensor(out=qan.t[:, 0:512], in0=p0.t[:], scalar=rs.t[:, 0:1],
                                                                     in1=mqn.t[:, 0:512], op0=ALU.mult, op1=ALU.mult),
                             [p0.b, rs.b, mqn.b], [qan.b])
                    self.dve(lambda e, p1=p1: e.scalar_tensor_tensor(out=qan.t[:, 512:768], in0=p1.t[:, 0:256], scalar=rs.t[:, 0:1],
                                                                     in1=mqn.t[:, 512:768], op0=ALU.mult, op1=ALU.mult),
                             [p1.b, rs.b, mqn.b], [qan.b])
                    pp = npt()
                    for c in range(6):
                        self.tr(pp.t[:, c * 128:(c + 1) * 128], qan.t[:, c * 128:(c + 1) * 128], ident.t[:], [qan.b, ident.b], [pp.b])
                    self.act(lambda e, pp=pp, tt=tt: e.activation(out=qaT.t[:, :, tt * 128:(tt + 1) * 128],
                                                                 in_=pp.t[:, 0:768].rearrange("p (c t) -> p c t", c=6),
                                                                 func=AF.Copy), [pp.b], [qaT.b])
                    rope(kr.t[:].rearrange("p (h d) -> p h d", h=1), p1.t[:, 256:320].rearrange("p (h d) -> p h d", h=1), [p1.b],
                         rm.t[:, tt, 0:32], rm.t[:, tt, 32:64], rm.b, 1, 32, kr.b)
                    pp2 = npt()
                    self.tr(pp2.t[0:64, 0:128], kr.t[:, :], ident.t[:], [kr.b, ident.b], [pp2.b])
                    self.act(lambda e, pp2=pp2, tt=tt: e.activation(out=krT.t[:, tt * 128:(tt + 1) * 128], in_=pp2.t[0:64, 0:128],
                                                                   func=AF.Copy), [pp2.b], [krT.b])
                    self.act(lambda e, p1=p1, tt=tt: e.activation(out=dts.t[:, tt, :], in_=p1.t[:, 320:384], func=AF.Copy),
                             [p1.b], [dts.b])
                S.dma("pool", self.KT_mr[:, tok0:tok0 + TS], krT.t[:, 0:TS], reads=[krT.b], writes=[self.B("KT_mr")], owner=krT.b)
                S.dma("pool", self.dtr[tok0:tok0 + TS, :].rearrange("(a p) c -> p a c", p=128), dts.t[:, 0:n, :], reads=[dts.b],
                      writes=[self.B("dtr")], owner=dts.b)

            def mkva_block(w):
                for tt in range(n):
                    p = tm_matmul(w, tt)
                    ss, r1, r2, rs = sm[0], sm[1], sm[2], sm[3]
                    self.act(lambda e, p=p: e.activation(out=sqt.t[:], in_=p.t[:], func=AF.Square, accum_out=ss.t[:, 0:1]),
                             [p.b], [sqt.b, ss.b])
                    rstd_from(ss.t[:, 0:1], 512, (rs.t[:, 0:1], rs.b), (r1.t[:, 0:1], r1.b), (r2.t[:, 0:1], r2.b), [ss.b])
                    self.dve(lambda e, p=p: e.scalar_tensor_tensor(out=qan.t[:, 0:512], in0=p.t[:], scalar=rs.t[:, 0:1],
                                                                   in1=mkvn.t[:], op0=ALU.mult, op1=ALU.mult),
                             [p.b, rs.b, mkvn.b], [qan.b])
                    pp = npt()
                    for c in range(4):
                        self.tr(pp.t[:, c * 128:(c + 1) * 128], qan.t[:, c * 128:(c + 1) * 128], ident.t[:], [qan.b, ident.b], [pp.b])
                    self.act(lambda e, pp=pp, tt=tt: e.activation(out=kvaT.t[:, :, tt * 128:(tt + 1) * 128],
                                                                 in_=pp.t[:, 0:512].rearrange("p (c t) -> p c t", c=4),
                                                                 func=AF.Copy), [pp.b], [kvaT.b])

            def mla_stage2():
                for (wt_, nk, srcT, dst2d, dname) in ((wuq_n, 6, qaT, self.QT_mn, "QT_mn"), (wukv_k, 4, kvaT, self.KT_mn, "KT_mn")):
                    for hg in range(2):
                        sg = nstg()
                        for hh in range(4):
                            h = hg * 4 + hh
                            p = nb()
                            for k in range(nk):
                                self.mm(p.t[:, 0:TS], wt_.t[:, k, h * 128:(h + 1) * 128], srcT.t[:, k, 0:TS], k == 0, k == nk - 1,
                                        [wt_.b, srcT.b], [p.b])
                            self.act(lambda e, p=p, hh=hh, sg=sg, TS=TS: e.activation(out=sg.t[:, hh, 0:TS], in_=p.t[:, 0:TS], func=AF.Copy),
                                     [p.b], [sg.b])
                        S.dma("pool", fm_dst(dst2d, hg * 512, 512, tok0), sg.t[:, :, 0:TS], reads=[sg.b], writes=[self.B(dname)],
                              owner=sg.b)
                sg = nstg()
                for tt in range(n):
                    p = nb()
                    for k in range(6):
                        self.mm(p.t[:], qaT.t[:, k, tt * 128:(tt + 1) * 128], wuq_r.t[:, k, :], k == 0, k == 5, [qaT.b, wuq_r.b], [p.b])
                    for hg in range(2):
                        rope(qr.t[:].rearrange("p h d -> p (h d)")[:, hg * 256:(hg + 1) * 256].rearrange("p (h d) -> p h d", h=4),
                             p.t[:, hg * 256:(hg + 1) * 256].rearrange("p (h d) -> p h d", h=4), [p.b],
                             rm.t[:, tt, 0:32], rm.t[:, tt, 32:64], rm.b, 4, 32, qr.b)
                    pp = npt()
                    qrf = qr.t[:].rearrange("p h d -> p (h d)")
                    for c in range(4):
                        self.tr(pp.t[:, c * 128:(c + 1) * 128], qrf[:, c * 128:(c + 1) * 128], ident.t[:], [qr.b, ident.b], [pp.b])
                    self.act(lambda e, pp=pp, tt=tt, sg=sg: e.activation(out=sg.t[:, :, tt * 128:(tt + 1) * 128],
                                                                        in_=pp.t[:, 0:512].rearrange("p (c t) -> p c t", c=4),
                                                                        func=AF.Copy), [pp.b], [sg.b])
                S.dma("pool", fm_dst(self.QT_mr, 0, 512, tok0), sg.t[:, :, 0:TS], reads=[sg.b], writes=[self.B("QT_mr")], owner=sg.b)
                for half in range(2):
                    sg = nstg()
                    for tt in range(n):
                        p = nb()
                        for k in range(4):
                            self.mm(p.t[:], kvaT.t[:, k, tt * 128:(tt + 1) * 128], wukv_v.t[:, k, half * 512:(half + 1) * 512],
                                    k == 0, k == 3, [kvaT.b, wukv_v.b], [p.b])
                        self.act(lambda e, p=p, tt=tt, sg=sg: e.activation(out=sg.t[:, tt, :], in_=p.t[:], func=AF.Copy), [p.b], [sg.b])
                    S.dma("pool", tm_dst(self.V_m, half * 512, 512), sg.t[:, 0:n, :], reads=[sg.b], writes=[self.B("V_m")], owner=sg.b)

            for j in range(NBLK):
                kind = BLK_KIND[j]
                ensure_loaded(si, j)
                nxt = (si, j + 1) if j + 1 < NBLK else (si + 1, 0)
                if kind == "mqa0":
                    ensure_loaded(si, j + 1)
                    continue
                if kind != "mqa1":
                    ensure_loaded(*nxt)
                w = slot[(si, j)]
                jj = j - BLK_FIRST[kind]
                if kind == "z":
                    simple_tm(j, w, AF.Silu, self.z_s, self.B("z_s"), jj * 512)
                elif kind in ("gg", "ng", "mg"):
                    simple_fm(j, w, AF.Silu, self.gT[kind], self.B("gT_" + kind), jj * 512, tok0)
                elif kind == "mix":
                    simple_fm(j, w, AF.Sigmoid, self.mixT, self.B("mixT"), jj * 512, tok0)
                elif kind == "xbc":
                    simple_fm(j, w, AF.Copy, self.xbcT, self.B("xbcT"), jj * 512, cfg.padcol(tok0))
                elif kind == "nq":
                    simple_fm(j, w, AF.Copy, self.QT_n, self.B("QT_n"), jj * 512, tok0, scale=128.0 ** -0.5)
                elif kind == "nk":
                    simple_fm(j, w, AF.Copy, self.KT_n, self.B("KT_n"), jj * 512, tok0)
                elif kind == "gv":
                    simple_tm(j, w, AF.Copy, self.V_g, self.B("V_g"), 0)
                elif kind == "nv":
                    simple_tm(j, w, AF.Copy, self.V_n, self.B("V_n"), jj * 512)
                elif kind == "gq":
                    qk_block(j, w, gqn, self.QT_g, self.B("QT_g"), jj * 4)
                elif kind == "gk":
                    qk_block(j, w, gkn, self.KT_g, self.B("KT_g"), 0)
                elif kind == "mqa1":
                    mqa_pair(slot[(si, j - 1)], w)
                    ensure_loaded(*nxt)
                elif kind == "mkva":
                    mkva_block(w)
                    mla_stage2()
        S.end_phase()


    def phaseS1(self, l):
        cfg = self.cfg
        S = self.S
        S.begin_phase()
        ident = self.ident_b
        cw = S.sbuf("cw", [128, 24, 5], F32)
        cb = S.sbuf("cb", [128, 24], F32)
        cbrow = S.sbuf("cbrow", [128, 2560], F32)
        S.dma("sp", cw.t[:], self.conv_w[l], reads=[], writes=[cw.b], owner=cw.b)
        S.dma("sp", cb.t[:], self.conv_b[l], reads=[], writes=[cb.b], owner=cb.b)
        S.dma("sp", cbrow.t[:], self.conv_b_row[l][:, 0:2560].partition_broadcast(128), reads=[], writes=[cbrow.b], owner=cbrow.b)
        dg = S.sbuf("dg", [128, 120, 128], BF16)
        self.dve(lambda e: e.tensor_tensor(dg.t[:], ident.t[:].unsqueeze(1).to_broadcast([128, 120, 128]),
                                           cw.t[:].rearrange("p a b -> p (a b)").unsqueeze(2).to_broadcast([128, 120, 128]), ALU.mult),
                 [ident.b, cw.b], [dg.b])
        uw = [S.sbuf("uw%d" % i, [128, 24, 516], BF16) for i in range(2)]
        tf = [S.sbuf("tf%d" % i, [128, 512], F32) for i in range(2)]
        sx = [S.sbuf("sx%d" % i, [128, 2560], BF16) for i in range(2)]
        sf = [S.sbuf("sf%d" % i, [128, 4, 512], BF16) for i in range(2)]
        pm = [S.psum("pS1_%d" % i, [128, 512]) for i in range(8)]
        cnt = {"pm": 0, "tf": 0, "sx": 0, "sf": 0}

        def nb():
            p = pm[cnt["pm"] % 8]
            cnt["pm"] += 1
            return p

        xv = self.xbcT.rearrange("(k p) t -> p k t", p=128)
        for si, (t0, n) in enumerate(cfg.stiles):
            TS = n * 128
            tok0 = t0 * 128
            col0 = cfg.padcol(tok0)
            u = uw[si % 2]
            S.dma("sp", u.t[:, :, 0:TS + 4], xv[:, :, col0 - 2:col0 + TS + 2], reads=[self.B("xbcT")], writes=[u.b], owner=u.b)
            for tt in range(n):
                sxt = sx[cnt["sx"] % 2]
                cnt["sx"] += 1
                for bk in range(5):
                    p = nb()
                    for q4 in range(4):
                        blk = bk * 4 + q4
                        for tap in range(5):
                            self.mm(p.t[:, q4 * 128:(q4 + 1) * 128], u.t[:, blk, tt * 128 + tap:tt * 128 + tap + 128],
                                    dg.t[:, blk * 5 + tap, :], tap == 0, tap == 4, [u.b, dg.b], [p.b])
                    t_ = tf[cnt["tf"] % 2]
                    cnt["tf"] += 1
                    self.dve(lambda e, p=p, t_=t_, bk=bk: e.tensor_tensor(t_.t[:], p.t[:], cbrow.t[:, bk * 512:(bk + 1) * 512], ALU.add),
                             [p.b, cbrow.b], [t_.b])
                    self.act(lambda e, t_=t_, sxt=sxt, bk=bk: e.activation(out=sxt.t[:, bk * 512:(bk + 1) * 512], in_=t_.t[:], func=AF.Silu),
                             [t_.b], [sxt.b])
                r0 = tok0 + tt * 128
                S.dma("pool", self.xc[r0:r0 + 128, :], sxt.t[:, 0:2048], reads=[sxt.b], writes=[self.B("xc")], owner=sxt.b)
                S.dma("pool", self.Bc[r0:r0 + 128, :], sxt.t[:, 2048:2560], reads=[sxt.b], writes=[self.B("Bc")], owner=sxt.b)
            for which, dst, dname in ((0, self.BT, "BT"), (1, self.CT, "CT")):
                sft = sf[cnt["sf"] % 2]
                cnt["sf"] += 1
                for q4 in range(4):
                    blk = 16 + which * 4 + q4
                    p = nb()
                    for tap in range(5):
                        self.mm(p.t[:, 0:TS], dg.t[:, blk * 5 + tap, :], u.t[:, blk, tap:tap + TS], tap == 0, tap == 4, [u.b, dg.b], [p.b])
                    self.act(lambda e, p=p, sft=sft, q4=q4, blk=blk, TS=TS: e.activation(out=sft.t[:, q4, 0:TS], in_=p.t[:, 0:TS], func=AF.Silu,
                                                                                bias=cb.t[:, blk:blk + 1]), [p.b, cb.b], [sft.b])
                S.dma("pool", dst[:, tok0:tok0 + TS].rearrange("(a p) t -> p a t", p=128), sft.t[:, :, 0:TS], reads=[sft.b],
                      writes=[self.B(dname)], owner=sft.b)
        S.end_phase()

    def _s2_state(self, sl, ead, g, pcs, St):
        self.dve(lambda e: e.tensor_tensor(sl.rearrange("p (h q) -> p h q", h=8), sl.rearrange("p (h q) -> p h q", h=8),
                                           ead.t[:, 32 + g * 8:32 + (g + 1) * 8].unsqueeze(2).to_broadcast([128, 8, 64]), ALU.mult),
                 [St.b, ead.b], [St.b])
        self.dve(lambda e: e.tensor_tensor(sl, sl, pcs.t[:], ALU.add), [St.b, pcs.b], [St.b])

    def phaseS2(self, l):
        cfg = self.cfg
        S = self.S
        NT = cfg.NT
        S.begin_phase()
        ident_f, ident_b, ones_f = self.ident_f, self.ident_b, self.ones_f
        tri = [S.sbuf("tri%d" % d, [128, 128], F32) for d in range(2)]
        for d in range(2):
            self.pool(lambda e, d=d: e.memset(tri[d].t[:], 1.0), [], [tri[d].b])
            self.pool(lambda e, d=d: e.affine_select(out=tri[d].t[:], in_=tri[d].t[:], pattern=[[1 if d == 0 else -1, 128]],
                                                     compare_op=ALU.is_ge, fill=0.0, base=0,
                                                     channel_multiplier=-1 if d == 0 else 1), [tri[d].b], [tri[d].b])
        E = S.sbuf("Esel", [32, 32, 128], F32)
        self.pool(lambda e: e.memset(E.t[:], 0.0), [], [E.b])
        self.pool(lambda e: e.affine_select(out=E.t[:], in_=E.t[:], pattern=[[1, 32], [0, 128]], compare_op=ALU.not_equal, fill=1.0,
                                            base=0, channel_multiplier=-1), [E.b], [E.b])
        dt_all = S.sbuf("dt_all", [128, NT, 64], F32)
        da_all = S.sbuf("da_all", [128, NT, 64], F32)
        avec = S.sbuf("avec", [128, 64], F32)
        dtb = S.sbuf("dtb", [128, 64], F32)
        dsk = S.sbuf("dsk", [128, 32], F32)
        nrm = S.sbuf("nrm", [128, D], F32)
        S.dma("sp", dt_all.t[:], self.dtr.rearrange("(c p) h -> p c h", p=128), reads=[self.B("dtr")], writes=[dt_all.b], owner=dt_all.b)
        S.dma("sp", avec.t[:], self.a_log[l].partition_broadcast(128), reads=[], writes=[avec.b], owner=avec.b)
        S.dma("sp", dtb.t[:], self.dt_bias[l].partition_broadcast(128), reads=[], writes=[dtb.b], owner=dtb.b)
        S.dma("sp", dsk.t[:], self.d_skip[l].partition_broadcast(128), reads=[], writes=[dsk.b], owner=dsk.b)
        S.dma("sp", nrm.t[:], self.ssm_norm[l].partition_broadcast(128), reads=[], writes=[nrm.b], owner=nrm.b)
        self.dve(lambda e: e.tensor_tensor(dt_all.t[:], dt_all.t[:], dtb.t[:].unsqueeze(1).to_broadcast([128, NT, 64]), ALU.add),
                 [dt_all.b, dtb.b], [dt_all.b])
        self.act(lambda e: e.activation(out=dt_all.t[:], in_=dt_all.t[:], func=AF.Exp), [dt_all.b], [dt_all.b])
        self.act(lambda e: e.activation(out=dt_all.t[:], in_=dt_all.t[:], func=AF.Ln, bias=1.0), [dt_all.b], [dt_all.b])
        self.act(lambda e: e.activation(out=avec.t[:], in_=avec.t[:], func=AF.Exp), [avec.b], [avec.b])
        self.dve(lambda e: e.scalar_tensor_tensor(out=da_all.t[:], in0=dt_all.t[:], scalar=-1.0,
                                                  in1=avec.t[:].unsqueeze(1).to_broadcast([128, NT, 64]), op0=ALU.mult, op1=ALU.mult),
                 [dt_all.b, avec.b], [da_all.b])
        St = S.sbuf("St", [128, D], F32)
        Sbf = S.sbuf("Sbf", [128, D], BF16)
        xct = [S.sbuf("xct%d" % i, [128, D], BF16) for i in range(2)]
        bct = [S.sbuf("bct%d" % i, [128, 512], BF16) for i in range(2)]
        btt = [S.sbuf("btt%d" % i, [128, 4, 128], BF16) for i in range(2)]
        ctt = [S.sbuf("ctt%d" % i, [128, 4, 128], BF16) for i in range(2)]
        yft = [S.sbuf("yft%d" % i, [128, D], F32) for i in range(2)]
        zst = [S.sbuf("zst%d" % i, [128, D], BF16) for i in range(2)]
        acs2 = [S.sbuf("acs%d" % i, [128, 64], F32) for i in range(2)]
        acsT2 = [S.sbuf("acsT%d" % i, [32, 128], F32) for i in range(2)]
        ead2 = [S.sbuf("ead%d" % i, [128, 64], F32) for i in range(2)]
        te2 = [S.sbuf("te%d" % i, [128, 32], F32) for i in range(2)]
        dtte2 = [S.sbuf("dtte%d" % i, [128, 32], F32) for i in range(2)]
        xs2 = [S.sbuf("xs%d" % i, [128, D], BF16) for i in range(2)]
        xte2 = [S.sbuf("xte%d" % i, [128, D], BF16) for i in range(2)]
        CBm2 = [S.sbuf("CBm%d" % i, [128, 4, 128], F32) for i in range(2)]
        seg = [S.sbuf("seg%d" % i, [128, 4, 128], F32) for i in range(2)]
        e4 = [S.sbuf("e4_%d" % i, [128, 4, 128], F32) for i in range(2)]
        G4 = [S.sbuf("G4_%d" % i, [128, 4, 128], BF16) for i in range(3)]
        tg = S.sbuf("tg", [128, 512], F32)
        ysb = [S.sbuf("ysb%d" % i, [128, D], F32) for i in range(2)]
        tsk = S.sbuf("tsk", [128, D], F32)
        gnb = S.sbuf("gnb", [128, D], BF16)
        gts = [S.sbuf("gts%d" % i, [128, 16, 128], BF16) for i in range(2)]
        smx = [S.sbuf("smx%d" % i, [128, 4], F32) for i in range(4)]
        pR = [S.psum("pR%d" % i, [128, 512]) for i in range(2)]
        pYd = [S.psum("pYd%d" % i, [128, 512]) for i in range(2)]
        pYo = S.psum("pYo", [128, 512])
        pC = S.psum("pC", [128, 512])
        pM = S.psum("pM", [128, 512])
        pT = S.psum("pT", [128, 1024], BF16)
        cnt = {"R": 0, "seg": 0, "G": 0}

        for d in range(2):
            order = [0, 1] + list(range(2, NT)) if d == 0 else [1, 0] + list(range(NT - 1, 1, -1))
            self.dve(lambda e: e.memset(St.t[:], 0.0), [], [St.b])
            self.dve(lambda e: e.memset(Sbf.t[:], 0.0), [], [Sbf.b])
            trd = tri[d]

            def load(i, d=d, order=order):
                c = order[i]
                r0 = c * 128
                S.dma("sp", xct[i % 2].t[:], self.xc[r0:r0 + 128, :], reads=[self.B("xc")], writes=[xct[i % 2].b], owner=xct[i % 2].b)
                S.dma("sp", bct[i % 2].t[:], self.Bc[r0:r0 + 128, :], reads=[self.B("Bc")], writes=[bct[i % 2].b], owner=bct[i % 2].b)
                S.dma("sp", btt[i % 2].t[:], self.BT[:, r0:r0 + 128].rearrange("(g n) t -> n g t", n=128), reads=[self.B("BT")],
                      writes=[btt[i % 2].b], owner=btt[i % 2].b)
                S.dma("sp", ctt[i % 2].t[:], self.CT[:, r0:r0 + 128].rearrange("(g n) t -> n g t", n=128), reads=[self.B("CT")],
                      writes=[ctt[i % 2].b], owner=ctt[i % 2].b)
                if d == 1:
                    S.dma("sp", yft[i % 2].t[:], self.y_f[r0:r0 + 128, :], reads=[self.B("y_f")], writes=[yft[i % 2].b], owner=yft[i % 2].b)
                    S.dma("sp", zst[i % 2].t[:], self.z_s[r0:r0 + 128, :], reads=[self.B("z_s")], writes=[zst[i % 2].b], owner=zst[i % 2].b)

            def prologue(i, d=d, order=order, trd=trd):
                c = order[i]
                par = i % 2
                xc_, bt_, ct_ = xct[par], btt[par], ctt[par]
                acs, acsT, ead, te, dtte, xs, xte, CBm = acs2[par], acsT2[par], ead2[par], te2[par], dtte2[par], xs2[par], xte2[par], CBm2[par]
                da_c = da_all.t[:, c, d * 32:(d + 1) * 32]
                dt_c = dt_all.t[:, c, d * 32:(d + 1) * 32]

                def s0():
                    self.mm(pM.t[:, 0:32], trd.t[:], da_c, True, True, [trd.b, da_all.b], [pM.b])
                    self.mm(pM.t[:, 32:64], ones_f.t[:], da_c, True, True, [ones_f.b, da_all.b], [pM.b])

                def s1():
                    self.dve(lambda e: e.tensor_copy(acs.t[:], pM.t[:, 0:64]), [pM.b], [acs.b])

                def s2():
                    self.S.op("pe", lambda e: e.transpose(pM.t[0:32, 128:256], acs.t[:, 0:32], ident_f.t[:]), [acs.b, ident_f.b], [pM.b])
                    self.act(lambda e: e.activation(out=ead.t[:], in_=acs.t[:], func=AF.Exp), [acs.b], [ead.b])

                def s3():
                    self.act(lambda e: e.activation(out=acsT.t[:], in_=pM.t[0:32, 128:256], func=AF.Copy), [pM.b], [acsT.b])
                    self.dve(lambda e: e.tensor_tensor(te.t[:], acs.t[:, 32:64], acs.t[:, 0:32], ALU.subtract), [acs.b], [te.b])

                def s4():
                    self.act(lambda e: e.activation(out=te.t[:], in_=te.t[:], func=AF.Exp), [te.b], [te.b])
                    self.dve(lambda e: e.tensor_tensor(xs.t[:].rearrange("p (h q) -> p h q", h=32), xc_.t[:].rearrange("p (h q) -> p h q", h=32),
                                                       dt_c.unsqueeze(2).to_broadcast([128, 32, 64]), ALU.mult), [xc_.b, dt_all.b], [xs.b])

                def s5():
                    self.dve(lambda e: e.tensor_tensor(dtte.t[:], te.t[:], dt_c, ALU.mult), [te.b, dt_all.b], [dtte.b])
                    self.dve(lambda e: e.tensor_tensor(xte.t[:].rearrange("p (h q) -> p h q", h=32), xc_.t[:].rearrange("p (h q) -> p h q", h=32),
                                                       dtte.t[:].unsqueeze(2).to_broadcast([128, 32, 64]), ALU.mult), [xc_.b, dtte.b], [xte.b])

                def s6():
                    for g in range(4):
                        self.mm(pC.t[:, g * 128:(g + 1) * 128], bt_.t[:, g, :], ct_.t[:, g, :], True, True, [bt_.b, ct_.b], [pC.b])

                def s7():
                    self.dve(lambda e: e.tensor_tensor(CBm.t[:], pC.t[:].rearrange("p (g t) -> p g t", g=4),
                                                       trd.t[:].unsqueeze(1).to_broadcast([128, 4, 128]), ALU.mult), [pC.b, trd.b], [CBm.b])

                return [s0, s1, s2, s3, s4, s5, s6, s7]

            def body(i, nxt_steps, d=d, order=order):
                c = order[i]
                par = i % 2
                xc_, bc_, ct_ = xct[par], bct[par], ctt[par]
                acs, acsT, ead, xs, xte, CBm = acs2[par], acsT2[par], ead2[par], xs2[par], xte2[par], CBm2[par]
                y_ = ysb[par]
                batches = {}

                def R_(b):
                    h0 = b * 4
                    R = pR[cnt["R"] % 2]
                    cnt["R"] += 1
                    for hh in range(4):
                        self.mm(R.t[:, hh * 128:(hh + 1) * 128], E.t[:, h0 + hh, :], acsT.t[:], True, True, [E.b, acsT.b], [R.b])
                    batches[b] = [R, None, None, None]

                def seg_exp(b):
                    h0 = b * 4
                    R = batches[b][0]
                    sg_ = seg[cnt["seg"] % 2]
                    e_ = e4[cnt["seg"] % 2]
                    cnt["seg"] += 1
                    self.dve(lambda e: e.tensor_tensor(sg_.t[:], R.t[:].rearrange("p (h t) -> p h t", h=4),
                                                       acs.t[:, h0:h0 + 4].unsqueeze(2).to_broadcast([128, 4, 128]), ALU.subtract),
                             [R.b, acs.b], [sg_.b])
                    self.act(lambda e: e.activation(out=e_.t[:], in_=sg_.t[:], func=AF.Exp), [sg_.b], [e_.b])
                    batches[b][1] = e_

                def G_Yd(b):
                    g = b // 2
                    e_ = batches[b][1]
                    G_ = G4[cnt["G"] % 3]
                    cnt["G"] += 1
                    self.dve(lambda e: e.scalar_tensor_tensor(out=G_.t[:], in0=e_.t[:], scalar=1.0, in1=CBm.t[:, g:g + 1, :].to_broadcast([128, 4, 128]),
                                                              op0=ALU.min, op1=ALU.mult), [e_.b, CBm.b], [G_.b])
                    pyd = pYd[g % 2]
                    for hh in range(4):
                        h = b * 4 + hh
                        self.mm(pyd.t[:, (h % 8) * 64:(h % 8 + 1) * 64], G_.t[:, hh, :], xs.t[:, h * 64:(h + 1) * 64], True, True,
                                [G_.b, xs.b], [pyd.b])
                    if b % 2 == 1:
                        self.mm(pYo.t[:], ct_.t[:, g, :], Sbf.t[:, g * 512:(g + 1) * 512], True, True, [ct_.b, Sbf.b], [pYo.b])
                        self.dve(lambda e: e.tensor_tensor(tg.t[:].rearrange("p (h q) -> p h q", h=8), pYo.t[:].rearrange("p (h q) -> p h q", h=8),
                                                           ead.t[:, g * 8:(g + 1) * 8].unsqueeze(2).to_broadcast([128, 8, 64]), ALU.mult),
                                 [pYo.b, ead.b], [tg.b])
                        self.dve(lambda e: e.tensor_tensor(y_.t[:, g * 512:(g + 1) * 512], tg.t[:], pyd.t[:], ALU.add), [tg.b, pyd.b], [y_.b])

                R_(0)
                for b in range(8):
                    if b + 1 < 8:
                        R_(b + 1)
                    seg_exp(b)
                    if b >= 1:
                        G_Yd(b - 1)
                    if nxt_steps is not None:
                        nxt_steps[b]()
                G_Yd(7)
                for g in range(4):
                    pcs = pYd[g % 2]
                    self.mm(pcs.t[:], bc_.t[:, g * 128:(g + 1) * 128], xte.t[:, g * 512:(g + 1) * 512], True, True, [bc_.b, xte.b], [pcs.b])
                    sl = St.t[:, g * 512:(g + 1) * 512]
                    self._s2_state(sl, ead, g, pcs, St)
                self.act(lambda e: e.activation(out=Sbf.t[:], in_=St.t[:], func=AF.Copy), [St.b], [Sbf.b])
                r0 = c * 128
                if d == 0:
                    S.dma("pool", self.y_f[r0:r0 + 128, :], y_.t[:], reads=[y_.b], writes=[self.B("y_f")], owner=y_.b)
                    return
                yf_, zs_ = yft[par], zst[par]
                self.dve(lambda e: e.tensor_tensor(y_.t[:], y_.t[:], yf_.t[:], ALU.add), [y_.b, yf_.b], [y_.b])
                self.dve(lambda e: e.tensor_tensor(tsk.t[:].rearrange("p (h q) -> p h q", h=32), xc_.t[:].rearrange("p (h q) -> p h q", h=32),
                                                   dsk.t[:].unsqueeze(2).to_broadcast([128, 32, 64]), ALU.mult), [xc_.b, dsk.b], [tsk.b])
                self.dve(lambda e: e.tensor_tensor(y_.t[:], y_.t[:], tsk.t[:], ALU.add), [y_.b, tsk.b], [y_.b])
                self.dve(lambda e: e.tensor_tensor(y_.t[:], y_.t[:], zs_.t[:], ALU.mult), [y_.b, zs_.b], [y_.b])
                ss, r1, r2 = smx[0], smx[1], smx[2]
                for g in range(4):
                    self.act(lambda e, g=g: e.activation(out=tsk.t[:, g * 512:(g + 1) * 512], in_=y_.t[:, g * 512:(g + 1) * 512],
                                                        func=AF.Square, accum_out=ss.t[:, g:g + 1]), [y_.b], [tsk.b, ss.b])
                self.dve(lambda e: e.tensor_scalar(r1.t[:], ss.t[:], 1.0 / 512, EPS, ALU.mult, ALU.add), [ss.b], [r1.b])
                self.act(lambda e: e.activation(out=r2.t[:], in_=r1.t[:], func=AF.Ln), [r1.b], [r2.b])
                self.act(lambda e: e.activation(out=r2.t[:], in_=r2.t[:], func=AF.Exp, scale=-0.5), [r2.b], [r2.b])
                self.dve(lambda e: e.tensor_tensor(y_.t[:].rearrange("p (g q) -> p g q", g=4), y_.t[:].rearrange("p (g q) -> p g q", g=4),
                                                   r2.t[:].unsqueeze(2).to_broadcast([128, 4, 512]), ALU.mult), [y_.b, r2.b], [y_.b])
                self.dve(lambda e: e.tensor_tensor(gnb.t[:], y_.t[:], nrm.t[:], ALU.mult), [y_.b, nrm.b], [gnb.b])
                gs = gts[par]
                for half in range(2):
                    for cc in range(8):
                        k = half * 8 + cc
                        self.tr(pT.t[:, cc * 128:(cc + 1) * 128], gnb.t[:, k * 128:(k + 1) * 128], ident_b.t[:], [gnb.b, ident_b.b], [pT.b])
                    self.act(lambda e, half=half: e.activation(out=gs.t[:, half * 8:(half + 1) * 8, :],
                                                              in_=pT.t[:].rearrange("p (c t) -> p c t", c=8), func=AF.Copy), [pT.b], [gs.b])
                S.dma("pool", self.oT["ssm"][:, r0:r0 + 128].rearrange("(k p) t -> p k t", p=128), gs.t[:], reads=[gs.b],
                      writes=[self.B("oT_ssm")], owner=gs.b)

            load(0)
            for st_ in prologue(0):
                st_()
            for i in range(NT):
                nxt = None
                if i + 1 < NT:
                    load(i + 1)
                    nxt = prologue(i + 1)
                body(i, nxt)
        S.end_phase()


    def phaseT(self, l, last):
        cfg = self.cfg
        S = self.S
        T, NT = cfg.T, cfg.NT
        S.begin_phase()
        ident_b, ones_b, ones_f = self.ident_b, self.ones_b, self.ones_f
        acc = [S.sbuf("acc%d" % i, [128, 1024], F32) for i in range(2)]
        KT = [S.sbuf("KT%d" % i, [128, T], BF16) for i in range(2)]
        VT = [S.sbuf("VT%d" % i, [128, NT, 128], BF16) for i in range(2)]
        QT = [S.sbuf("QT%d" % i, [128, T], BF16) for i in range(2)]
        KR = S.sbuf("KR", [64, T], BF16)
        QR = [S.sbuf("QR%d" % i, [64, T], BF16) for i in range(2)]
        nbt = S.sbuf("nbt", [128, self.nbm, 128], BF16)
        PT = [S.sbuf("PT%d" % i, [128, 1024], BF16) for i in range(3)]
        gt = [S.sbuf("gt%d" % i, [128, 512], BF16) for i in range(2)]
        rec = [S.sbuf("rec%d" % i, [128, 512], F32) for i in range(2)]
        of = [S.sbuf("of%d" % i, [128, 512], F32) for i in range(2)]
        ob = [S.sbuf("ob%d" % i, [128, 512], BF16) for i in range(2)]
        pS = [S.psum("pS%d" % i, [128, 1024]) for i in range(2)]
        pO = [S.psum("pO%d" % i, [128, 512]) for i in range(2)]
        pD = [S.psum("pD%d" % i, [128, 512]) for i in range(2)]
        cnt = {"S": 0, "P": 0, "O": 0, "kv": 0, "q": 0}
        qblocks = [(t0 * 128, n * 128) for (t0, n) in cfg.stiles]
        if last:
            qblocks = qblocks[1:]

        def finish(o, dd, gsrc, gB, orow, q0, TSq, dst, dname):
            k = cnt["O"]
            g_, r_, f_, b_ = gt[k % 2], rec[k % 2], of[k % 2], ob[k % 2]
            S.dma("sp", g_.t[:, 0:TSq], gsrc[orow:orow + 128, q0:q0 + TSq], reads=[gB], writes=[g_.b], owner=g_.b)
            self.dve(lambda e: e.reciprocal(r_.t[:, 0:TSq], dd.t[:, 0:TSq]), [dd.b], [r_.b])
            self.dve(lambda e: e.tensor_tensor(f_.t[:, 0:TSq], o.t[:, 0:TSq], r_.t[:, 0:TSq], ALU.mult), [o.b, r_.b], [f_.b])
            self.dve(lambda e: e.tensor_tensor(b_.t[:, 0:TSq], f_.t[:, 0:TSq], g_.t[:, 0:TSq], ALU.mult), [f_.b, g_.b], [b_.b])
            S.dma("pool", dst[orow:orow + 128, q0:q0 + TSq], b_.t[:, 0:TSq], reads=[b_.b], writes=[self.B(dname)], owner=b_.b)

        def dense_block(kparts, v, q0, TSq, ktiles, scale, o, dd):
            nk = len(ktiles)
            pairs = [ktiles[i:i + 2] for i in range(0, nk, 2)]
            ac = acc[cnt["O"] % 2]

            def view(t_, npr):
                return t_.t[:].rearrange("p (a b) -> p a b", a=2)[:, 0:npr, 0:TSq]

            def qk(i):
                ps = pS[cnt["S"] % 2]
                cnt["S"] += 1
                pt = PT[cnt["P"] % 3]
                cnt["P"] += 1
                for j, kt in enumerate(pairs[i]):
                    for pi, (kT, qT, nr) in enumerate(kparts):
                        self.mm(ps.t[:, j * 512:j * 512 + TSq], kT.t[0:nr, kt * 128:(kt + 1) * 128], qT.t[0:nr, q0:q0 + TSq],
                                pi == 0, pi == len(kparts) - 1, [kT.b, qT.b], [ps.b])
                return ps, pt

            def ex_pv(i, ps, pt):
                npr = len(pairs[i])
                self.act(lambda e: e.activation(out=view(pt, npr), in_=view(ps, npr), func=AF.Exp, scale=scale), [ps.b], [pt.b])
                if i == 0:
                    self.dve(lambda e: e.tensor_copy(view(ac, npr), view(pt, npr)), [pt.b], [ac.b])
                else:
                    self.dve(lambda e: e.tensor_tensor(view(ac, npr), view(ac, npr), view(pt, npr), ALU.add), [pt.b, ac.b], [ac.b])
                for j, kt in enumerate(pairs[i]):
                    self.mm(o.t[:, 0:TSq], v.t[:, kt, :], pt.t[:, j * 512:j * 512 + TSq], i == 0 and j == 0,
                            i == len(pairs) - 1 and j == npr - 1, [v.b, pt.b], [o.b])

            cur = qk(0)
            for i in range(len(pairs)):
                nxt = qk(i + 1) if i + 1 < len(pairs) else None
                ex_pv(i, *cur)
                cur = nxt
            nacc = min(2, nk)
            for j in range(nacc):
                self.mm(dd.t[:, 0:TSq], ones_f.t[:], ac.t[:, j * 512:j * 512 + TSq], j == 0, j == nacc - 1, [ones_f.b, ac.b], [dd.b])

        def load_kv(ksrc, kname, krow, vsrc, vname, vcol):
            i = cnt["kv"] % 2
            cnt["kv"] += 1
            S.dma("sp", KT[i].t[:], ksrc[krow:krow + 128, :], reads=[self.B(kname)], writes=[KT[i].b], owner=KT[i].b)
            S.dma("sp", VT[i].t[:], vsrc[:, vcol:vcol + 128].rearrange("(c p) d -> p c d", p=128), reads=[self.B(vname)],
                  writes=[VT[i].b], owner=VT[i].b)
            return KT[i], VT[i]

        def load_q(qsrc, qname, qrow):
            i = cnt["q"] % 2
            cnt["q"] += 1
            S.dma("sp", QT[i].t[:], qsrc[qrow:qrow + 128, :], reads=[self.B(qname)], writes=[QT[i].b], owner=QT[i].b)
            return QT[i], i

        def full_attention(kparts, v, scale, gsrc, gname, orow, dst, dname):
            for (q0, TSq) in qblocks:
                ktiles = [0, 1] if q0 == 0 else list(range(NT))
                o, dd = pO[cnt["O"] % 2], pD[cnt["O"] % 2]
                dense_block(kparts, v, q0, TSq, ktiles, scale, o, dd)
                finish(o, dd, gsrc, self.B(gname), orow, q0, TSq, dst, dname)
                cnt["O"] += 1

        sc_g = 128.0 ** -0.5
        for g in range(4):
            kT, v = load_kv(self.KT_g, "KT_g", g * 128, self.V_g, "V_g", g * 128)
            for r in range(2):
                h = g * 2 + r
                qT, _ = load_q(self.QT_g, "QT_g", h * 128)
                full_attention([(kT, qT, 128)], v, sc_g, self.gT["gg"], "gT_gg", h * 128, self.oT["gqa"], "oT_gqa")
        sc_m = 192.0 ** -0.5
        S.dma("sp", KR.t[:], self.KT_mr[:, :], reads=[self.B("KT_mr")], writes=[KR.b], owner=KR.b)
        for h in range(8):
            kT, v = load_kv(self.KT_mn, "KT_mn", h * 128, self.V_m, "V_m", h * 128)
            qT, qi = load_q(self.QT_mn, "QT_mn", h * 128)
            S.dma("sp", QR[qi].t[:], self.QT_mr[h * 64:(h + 1) * 64, :], reads=[self.B("QT_mr")], writes=[QR[qi].b], owner=QR[qi].b)
            full_attention([(kT, qT, 128), (KR, QR[qi], 64)], v, sc_m, self.gT["mg"], "gT_mg", h * 128, self.oT["mla"], "oT_mla")
        for h in range(8):
            kT, v = load_kv(self.KT_n, "KT_n", h * 128, self.V_n, "V_n", h * 128)
            qT, _ = load_q(self.QT_n, "QT_n", h * 128)
            S.dma("pool", nbt.t[:], self.na_bias[l, h].rearrange("m k q -> k m q"), reads=[], writes=[nbt.b], owner=nbt.b)
            for (q0, TSq) in qblocks:
                o, dd = pO[cnt["O"] % 2], pD[cnt["O"] % 2]
                if q0 == 0:
                    dense_block([(kT, qT, 128)], v, 0, TSq, [0, 1], 1.0, o, dd)
                else:
                    def na_qk(qq):
                        qi = (q0 - cfg.C) // 128 + qq
                        keys = self.na_keyset(qi)
                        qc = q0 + qq * 128
                        ps = pS[cnt["S"] % 2]
                        cnt["S"] += 1
                        pt = PT[cnt["P"] % 3]
                        cnt["P"] += 1
                        tiles = [0, 1] + [2 + kj for kj in keys]
                        for j, ktile in enumerate(tiles):
                            isw = j >= 2
                            self.mm(ps.t[:, j * 128:(j + 1) * 128], kT.t[:, ktile * 128:(ktile + 1) * 128], qT.t[:, qc:qc + 128], True, not isw,
                                    [kT.b, qT.b], [ps.b])
                            if isw:
                                bi = self.na_table[(qi, keys[j - 2])]
                                self.mm(ps.t[:, j * 128:(j + 1) * 128], ident_b.t[:], nbt.t[:, bi, :], False, True, [ident_b.b, nbt.b], [ps.b])
                        return ps, pt, tiles

                    def na_pv(qq, ps, pt, tiles):
                        ncol = len(tiles) * 128
                        self.act(lambda e: e.activation(out=pt.t[:, 0:ncol], in_=ps.t[:, 0:ncol], func=AF.Exp), [ps.b], [pt.b])
                        for j, ktile in enumerate(tiles):
                            self.mm(o.t[:, qq * 128:(qq + 1) * 128], v.t[:, ktile, :], pt.t[:, j * 128:(j + 1) * 128], j == 0, j == len(tiles) - 1,
                                    [v.b, pt.b], [o.b])
                            self.mm(dd.t[:, qq * 128:(qq + 1) * 128], ones_b.t[:], pt.t[:, j * 128:(j + 1) * 128], j == 0, j == len(tiles) - 1,
                                    [ones_b.b, pt.b], [dd.b])

                    nqq = TSq // 128
                    cur = na_qk(0)
                    for qq in range(nqq):
                        nxt = na_qk(qq + 1) if qq + 1 < nqq else None
                        na_pv(qq, *cur)
                        cur = nxt
                finish(o, dd, self.gT["ng"], self.B("gT_ng"), h * 128, q0, TSq, self.oT["na"], "oT_na")
                cnt["O"] += 1
        S.end_phase()

    def phaseM1(self, l, last):
        cfg = self.cfg
        S = self.S
        S.begin_phase()
        oTs = [S.sbuf("oTs%d" % i, [128, 40, 512], BF16) for i in range(2)]
        wo = [S.sbuf("wo%d" % i, [128, 40, 256], BF16) for i in range(2)]
        mg = [S.sbuf("mg%d" % i, [128, 4, 512], BF16) for i in range(2)]
        tm = [S.sbuf("tm%d" % i, [128, 512], F32) for i in range(4)]
        ys = [S.sbuf("ys%d" % i, [128, 2, 512], BF16) for i in range(2)]
        pY = [S.psum("pY%d" % i, [128, 512]) for i in range(8)]
        cnt = {"p": 0, "w": 0, "m": 0, "y": 0}
        srcs = (("ssm", 0, 16), ("gqa", 16, 8), ("na", 24, 8), ("mla", 32, 8))
        mixv = self.mixT.rearrange("(b f) t -> f b t", b=4)
        stl = cfg.stiles[1:] if last else cfg.stiles
        for si, (t0, n) in enumerate(stl):
            TS = n * 128
            tok0 = t0 * 128
            ot = oTs[si % 2]
            for name, k0, nk in srcs:
                S.dma("sp", ot.t[:, k0:k0 + nk, 0:TS], self.oT[name][:, tok0:tok0 + TS].rearrange("(k p) t -> p k t", p=128),
                      reads=[self.B("oT_" + name)], writes=[ot.b], owner=ot.b)
            for fp in range(8):
                w = wo[cnt["w"] % 2]
                cnt["w"] += 1
                S.dma("sp", w.t[:], self.w_o_bf[l, :, :, fp * 256:(fp + 1) * 256], reads=[self.B_wo[l]], writes=[w.b], owner=w.b)
                yst = ys[cnt["y"] % 2]
                cnt["y"] += 1
                for f2 in range(2):
                    fo = fp * 2 + f2
                    m_ = mg[cnt["m"] % 2]
                    cnt["m"] += 1
                    S.dma("sp", m_.t[:, :, 0:TS], mixv[fo * 128:(fo + 1) * 128, :, tok0:tok0 + TS], reads=[self.B("mixT")], writes=[m_.b],
                          owner=m_.b)
                    for bi, (name, k0, nk) in enumerate(srcs):
                        p = pY[cnt["p"] % 8]
                        cnt["p"] += 1
                        for k in range(nk):
                            self.mm(p.t[:, 0:TS], w.t[:, k0 + k, f2 * 128:(f2 + 1) * 128], ot.t[:, k0 + k, 0:TS], k == 0, k == nk - 1,
                                    [w.b, ot.b], [p.b])
                        self.dve(lambda e, p=p, bi=bi, m_=m_, TS=TS: e.tensor_tensor(tm[bi].t[:, 0:TS], p.t[:, 0:TS], m_.t[:, bi, 0:TS], ALU.mult),
                                 [p.b, m_.b], [tm[bi].b])
                    self.dve(lambda e, TS=TS: e.tensor_tensor(tm[0].t[:, 0:TS], tm[0].t[:, 0:TS], tm[1].t[:, 0:TS], ALU.add), [tm[0].b, tm[1].b], [tm[0].b])
                    self.dve(lambda e, TS=TS: e.tensor_tensor(tm[2].t[:, 0:TS], tm[2].t[:, 0:TS], tm[3].t[:, 0:TS], ALU.add), [tm[2].b, tm[3].b], [tm[2].b])
                    self.dve(lambda e, yst=yst, f2=f2, TS=TS: e.tensor_tensor(yst.t[:, f2, 0:TS], tm[0].t[:, 0:TS], tm[2].t[:, 0:TS], ALU.add),
                             [tm[0].b, tm[2].b], [yst.b])
                S.dma("pool", self.yT[fp * 256:(fp + 1) * 256, tok0:tok0 + TS].rearrange("(a p) t -> p a t", p=128), yst.t[:, :, 0:TS],
                      reads=[yst.b], writes=[self.B("yT")], owner=yst.b)
        S.end_phase()

    def phaseM2(self, l, last):
        cfg = self.cfg
        S = self.S
        S.begin_phase()
        xsrc = self.xin if l == 0 else self.xres[(l - 1) % 2]
        xsrcB = self.B("xin") if l == 0 else self.B("xres%d" % ((l - 1) % 2))
        wout = S.sbuf("wout", [128, 16, D], BF16)
        S.dma("pool", wout.t[:], self.w_out[l], reads=[], writes=[wout.b], owner=wout.b)
        GW = [S.sbuf("GW%d" % i, [128, D], F32) for i in range(2)]
        for w_ in range(2):
            S.dma("sp", GW[w_].t[:], self.modv[l, w_, 2:3, :].partition_broadcast(128), reads=[self.B("modv")], writes=[GW[w_].b],
                  owner=GW[w_].b)
        yt = [S.sbuf("yt%d" % i, [128, 16, 512], BF16) for i in range(2)]
        xt = [S.sbuf("xt%d" % i, [128, D], F32) for i in range(2)]
        ft = [S.sbuf("ft%d" % i, [128, D], F32) for i in range(2)]
        junk = S.sbuf("junk", [128, 512], F32)
        sm = [S.sbuf("sm%d" % i, [128, 4], F32) for i in range(4)]
        pF = [S.psum("pF%d" % i, [128, D]) for i in range(2)]
        stl = cfg.stiles[1:] if last else cfg.stiles
        k_ = 0
        for si, (t0, n) in enumerate(stl):
            TS = n * 128
            tok0 = t0 * 128
            y_ = yt[si % 2]
            S.dma("sp", y_.t[:, :, 0:TS], self.yT[:, tok0:tok0 + TS].rearrange("(k p) t -> p k t", p=128), reads=[self.B("yT")],
                  writes=[y_.b], owner=y_.b)
            for tt in range(n):
                gtile = t0 + tt
                r0 = gtile * 128
                which = 1 if gtile < 2 else 0
                x_, f_, p = xt[k_ % 2], ft[k_ % 2], pF[k_ % 2]
                k_ += 1
                S.dma("sp", x_.t[:], xsrc[r0:r0 + 128, :], reads=[xsrcB], writes=[x_.b], owner=x_.b)
                for q in range(4):
                    for k in range(16):
                        self.mm(p.t[:, q * 512:(q + 1) * 512], y_.t[:, k, tt * 128:(tt + 1) * 128], wout.t[:, k, q * 512:(q + 1) * 512],
                                k == 0, k == 15, [y_.b, wout.b], [p.b])
                ss, r1, r2 = sm[0], sm[1], sm[2]
                for q in range(4):
                    self.act(lambda e, p=p, q=q: e.activation(out=junk.t[:], in_=p.t[:, q * 512:(q + 1) * 512], func=AF.Square,
                                                             accum_out=ss.t[:, q:q + 1]), [p.b], [junk.b, ss.b])
                self.dve(lambda e: e.reduce_sum(r1.t[:, 0:1], ss.t[:, 0:4], AX.X), [ss.b], [r1.b])
                self.dve(lambda e: e.tensor_scalar(r1.t[:, 1:2], r1.t[:, 0:1], 1.0 / D, EPS, ALU.mult, ALU.add), [r1.b], [r1.b])
                self.act(lambda e: e.activation(out=r2.t[:, 0:1], in_=r1.t[:, 1:2], func=AF.Sqrt), [r1.b], [r2.b])
                self.dve(lambda e: e.reciprocal(r2.t[:, 1:2], r2.t[:, 0:1]), [r2.b], [r2.b])
                self.dve(lambda e, p=p, f_=f_, which=which: e.scalar_tensor_tensor(out=f_.t[:], in0=p.t[:], scalar=r2.t[:, 1:2], in1=GW[which].t[:],
                                                                                  op0=ALU.mult, op1=ALU.mult), [p.b, r2.b, GW[which].b], [f_.b])
                self.dve(lambda e, f_=f_, x_=x_: e.tensor_tensor(f_.t[:], f_.t[:], x_.t[:], ALU.add), [f_.b, x_.b], [f_.b])
                if last:
                    S.dma("pool", self.out[r0 - cfg.C:r0 - cfg.C + 128, :], f_.t[:], reads=[f_.b], writes=[self.B("out")], owner=f_.b)
                else:
                    S.dma("pool", self.xres[l % 2][r0:r0 + 128, :], f_.t[:], reads=[f_.b], writes=[self.B("xres%d" % (l % 2))], owner=f_.b)
        S.end_phase()


def _rope_np(n_tok, dim):
    t = np.arange(n_tok, dtype=np.int32)
    row = (t // GRID_W).astype(np.float32)
    col = (t % GRID_W).astype(np.float32)
    quarter = dim // 4
    freqs = (np.float32(ROPE_THETA) ** (-np.arange(quarter, dtype=np.float32) / np.float32(quarter))).astype(np.float32)
    ang = np.concatenate([row[:, None] * freqs, col[:, None] * freqs], axis=-1).astype(np.float32)
    return np.cos(ang).astype(np.float32), np.sin(ang).astype(np.float32)


def _chunked(w, kc):
    return np.ascontiguousarray(w.reshape(kc, 128, w.shape[1]).transpose(1, 0, 2))


def na_bias_tables(cfg, rpb):
    rows = cfg.rows
    nq = rows // 2
    wh = min(NA_WH, rows)
    table = {}
    mats = []

    def build(qi, kj):
        M = np.full((8, 128, 128), -30000.0, np.float32)
        for ql in range(128):
            r = 2 * qi + ql // 64
            c = ql % 64
            r0 = min(max(r - wh // 2, 0), rows - wh)
            c0 = min(max(c - NA_WW // 2, 0), GRID_W - NA_WW)
            for ky in range(2):
                kr_ = 2 * kj + ky
                if not (r0 <= kr_ < r0 + wh):
                    continue
                kx = np.arange(c0, c0 + NA_WW)
                M[:, ky * 64 + kx, ql] = rpb[:, kr_ - r + NA_WH - 1, kx - c + NA_WW - 1]
        return M

    def keyset(qi):
        ks = set()
        for r in (2 * qi, 2 * qi + 1):
            r0 = min(max(r - wh // 2, 0), rows - wh)
            for kr_ in range(r0, r0 + wh):
                ks.add(kr_ // 2)
        return sorted(ks)

    interior = {}
    for qi in range(nq):
        edge = (2 * qi - wh // 2 < 0) or (2 * qi + 1 - wh // 2 > rows - wh)
        for kj in keyset(qi):
            if not edge:
                key = ("i", kj - qi)
                if key not in interior:
                    interior[key] = len(mats)
                    mats.append(build(qi, kj))
                table[(qi, kj)] = interior[key]
            else:
                table[(qi, kj)] = len(mats)
                mats.append(build(qi, kj))
    return mats, table, keyset


def host_prep(cfg, inp, b, shared=None):
    f = np.float32
    m = dict(shared) if shared is not None else host_prep_shared(cfg, inp)
    m["xin"] = np.ascontiguousarray(np.concatenate([inp["ctx"][b], inp["x"][b]], axis=0), dtype=f)
    cm = np.stack([np.asarray(inp["c"][b]), np.asarray(inp["c_ctx"])], 0).reshape(2, 16, 128)
    m["cmod"] = np.ascontiguousarray(cm.transpose(2, 0, 1), dtype=f)
    return m


def host_prep_shared(cfg, inp):
    L = cfg.DEPTH
    f = np.float32
    m = {}
    m["ada_w"] = np.ascontiguousarray(np.asarray(inp["ada_w"][:L]).reshape(L, 16, 128, 12, 512).transpose(0, 3, 2, 1, 4), dtype=f)
    m["ada_b"] = np.ascontiguousarray(np.asarray(inp["ada_b"][:L]).reshape(L, 1, 3 * D), dtype=f)
    m["norm_pre"] = np.ascontiguousarray(np.asarray(inp["norm_pre"][:L]).reshape(L, 1, D), dtype=f)
    m["norm_post"] = np.ascontiguousarray(np.asarray(inp["norm_post"][:L]).reshape(L, 1, D), dtype=f)
    cols = win_device_cols()
    w_in = np.asarray(inp["w_in"][:L])
    wd = np.zeros((L, D, NBLK * 512), f)
    ok = cols >= 0
    wd[:, :, ok] = w_in[:, :, cols[ok]]
    m["w_in"] = np.ascontiguousarray(wd.reshape(L, 16, 128, NBLK, 512).transpose(0, 3, 2, 1, 4))
    del wd
    di64 = _deint(64)
    di128 = _deint(128)
    w_uq = np.asarray(inp["w_uq"][:L])
    cn = np.concatenate([h * 192 + np.arange(128) for h in range(8)])
    cr = np.concatenate([h * 192 + 128 + di64 for h in range(8)])
    m["w_uq_n"] = np.stack([_chunked(w_uq[l][:, cn], 6) for l in range(L)]).astype(f)
    m["w_uq_r"] = np.stack([_chunked(w_uq[l][:, cr], 6) for l in range(L)]).astype(f)
    w_ukv = np.asarray(inp["w_ukv"][:L])
    ck = np.concatenate([h * 256 + np.arange(128) for h in range(8)])
    cv = np.concatenate([h * 256 + 128 + np.arange(128) for h in range(8)])
    m["w_ukv_k"] = np.stack([_chunked(w_ukv[l][:, ck], 4) for l in range(L)]).astype(f)
    m["w_ukv_v"] = np.stack([_chunked(w_ukv[l][:, cv], 4) for l in range(L)]).astype(f)
    wo = [np.concatenate([np.asarray(inp[k][l]) for k in ("w_o_ssm", "w_o_gqa", "w_o_na", "w_o_mla")], 0) for l in range(L)]
    m["w_o"] = np.stack([_chunked(w, 40) for w in wo]).astype(f)
    m["w_out"] = np.stack([_chunked(np.asarray(inp["w_out"][l]), 16) for l in range(L)]).astype(f)
    cw = np.asarray(inp["conv_w"][:L])
    m["conv_w"] = np.ascontiguousarray(cw.transpose(0, 2, 1).reshape(L, 24, 128, 5).transpose(0, 2, 1, 3), dtype=f)
    cb = np.asarray(inp["conv_b"][:L])
    m["conv_b"] = np.ascontiguousarray(cb.reshape(L, 24, 128).transpose(0, 2, 1), dtype=f)
    m["conv_b_row"] = np.ascontiguousarray(cb.reshape(L, 1, 3072), dtype=f)
    m["a_log"] = np.ascontiguousarray(np.asarray(inp["a_log"][:L]).reshape(L, 1, 64), dtype=f)
    m["dt_bias"] = np.ascontiguousarray(np.asarray(inp["dt_bias"][:L]).reshape(L, 1, 64), dtype=f)
    m["d_skip"] = np.ascontiguousarray(np.asarray(inp["d_skip"][:L]).reshape(L, 1, 32), dtype=f)
    m["ssm_norm"] = np.ascontiguousarray(np.asarray(inp["ssm_norm"][:L]).reshape(L, 1, D), dtype=f)
    m["gq_norm"] = np.ascontiguousarray(np.asarray(inp["gqa_q_norm"][:L])[:, di128].reshape(L, 1, 128), dtype=f)
    m["gk_norm"] = np.ascontiguousarray(np.asarray(inp["gqa_k_norm"][:L])[:, di128].reshape(L, 1, 128), dtype=f)
    m["mq_norm"] = np.ascontiguousarray(np.asarray(inp["mla_q_norm"][:L]).reshape(L, 1, 768), dtype=f)
    m["mkv_norm"] = np.ascontiguousarray(np.asarray(inp["mla_kv_norm"][:L]).reshape(L, 1, 512), dtype=f)
    nb = []
    for l in range(L):
        mats, _, _ = na_bias_tables(cfg, np.asarray(inp["na_rpb"][l]))
        nb.append(np.stack(mats, 1))
    m["na_bias"] = np.ascontiguousarray(np.stack(nb), dtype=f)
    cg, sg = _rope_np(cfg.S, 128)
    cm_, sm_ = _rope_np(cfg.S, 64)
    rg = np.zeros((cfg.T, 128), f)
    rg[:CTX, :64] = 1.0
    rg[CTX:, :64] = cg
    rg[CTX:, 64:] = sg
    rm = np.zeros((cfg.T, 64), f)
    rm[:CTX, :32] = 1.0
    rm[CTX:, :32] = cm_
    rm[CTX:, 32:] = sm_
    m["ropeG"] = rg
    m["ropeM"] = rm
    return m


def build(cfg, phases=("A",)):
    mk = MK(cfg)
    mk.declare()
    mk.consts()
    mk.phase0()
    for l in range(cfg.DEPTH):
        if "P0" == cfg.stop_after:
            break
        mk.phaseA(l)
        if "A" == cfg.stop_after:
            break
        mk.phaseS1(l)
        if "S1" == cfg.stop_after:
            break
        mk.phaseS2(l)
        if "S2" == cfg.stop_after:
            break
        last = l == cfg.DEPTH - 1
        mk.phaseT(l, last)
        if "T" == cfg.stop_after:
            break
        mk.phaseM1(l, last)
        mk.phaseM2(l, last)
    mk.S.close()
    return mk


N_CORES_USED = 4


def kernel(**inputs):
    cfg = Cfg(S=8192, DEPTH=4)
    inp = {k: np.asarray(v) for k, v in inputs.items()}
    mk = build(cfg)
    shared = host_prep_shared(cfg, inp)
    in_maps = [host_prep(cfg, inp, b, shared) for b in range(N_CORES_USED)]
    res = run_bass_kernel_spmd(mk.nc, in_maps, core_ids=list(range(N_CORES_USED)))
    out = np.stack([np.asarray(res.results[b]["out"]) for b in range(N_CORES_USED)], 0)
    return out.astype(np.float32)
```
